# Optimizing a Trainium2 kernel written in Bass

```python
import jax, jax.numpy as jnp
from jax import lax
import numpy as np

D_MODEL = 2048
BATCH = 1
SEQ = 16384
DEPTH = 1

CTX_LEN = 256
GRID_W = 64
N_HEADS = 8
N_KV_HEADS = 2
HEAD_DIM = 128
GROUP = N_HEADS // N_KV_HEADS
ATTN_WIDTH = N_HEADS * HEAD_DIM
KV_WIDTH = N_KV_HEADS * HEAD_DIM
CONV_WIDTH = D_MODEL - ATTN_WIDTH
CONV_GROUPS = 16
CONV_K = 3
IN_WIDTH = ATTN_WIDTH + 2 * KV_WIDTH + 3 * CONV_WIDTH
D_FF = 5632
ROPE_THETA = 10000.0
ROPE_AXIS_DIM = HEAD_DIM // 2
Q_BLOCK = 128
EPS = 1e-6
N_MOD = 6

kernel_name = "hybrid_gqa_shortconv_convffn_dit"


def rmsnorm(x, w):
    xf = x.astype(jnp.float32)
    y = xf * lax.rsqrt(jnp.mean(xf * xf, axis=-1, keepdims=True) + EPS)
    return (y * w.astype(jnp.float32)).astype(x.dtype)


def modulate(h, shift, scale):
    return h * (1.0 + scale[:, None, :]) + shift[:, None, :]


def dwconv3(x, w):
    xp = jnp.pad(x, ((0, 0), (1, 1), (0, 0)))
    return xp[:, :-2] * w[0] + xp[:, 1:-1] * w[1] + xp[:, 2:] * w[2]


def rope_axis(x, pos):
    half = ROPE_AXIS_DIM // 2
    freqs = ROPE_THETA ** (-jnp.arange(half, dtype=jnp.float32) / half)
    ang = pos.astype(jnp.float32)[:, None] * freqs[None, :]
    cos = jnp.cos(ang)[None, :, None, :]
    sin = jnp.sin(ang)[None, :, None, :]
    xf = x.astype(jnp.float32)
    x1, x2 = xf[..., :half], xf[..., half:]
    return jnp.concatenate([x1 * cos - x2 * sin, x2 * cos + x1 * sin], axis=-1)


def rope_2d(x, row, col):
    out = jnp.concatenate([rope_axis(x[..., :ROPE_AXIS_DIM], row),
                           rope_axis(x[..., ROPE_AXIS_DIM:], col)], axis=-1)
    return out.astype(x.dtype)


def split_proj(p):
    o = 0
    q = p[..., o:o + ATTN_WIDTH]; o += ATTN_WIDTH
    k = p[..., o:o + KV_WIDTH]; o += KV_WIDTH
    v = p[..., o:o + KV_WIDTH]; o += KV_WIDTH
    cb = p[..., o:o + CONV_WIDTH]; o += CONV_WIDTH
    cc = p[..., o:o + CONV_WIDTH]; o += CONV_WIDTH
    ch = p[..., o:o + CONV_WIDTH]
    return q, k, v, cb, cc, ch


def heads(t, n_heads):
    b, n = t.shape[:2]
    return t.reshape(b, n, n_heads, HEAD_DIM)


def attend(qb, k, v):
    b, nq = qb.shape[:2]
    qg = qb.reshape(b, nq, N_KV_HEADS, GROUP, HEAD_DIM)
    s = jnp.einsum('bqkgd,bskd->bkgqs', qg, k,
                   preferred_element_type=jnp.float32) * (HEAD_DIM ** -0.5)
    p = jax.nn.softmax(s, axis=-1).astype(v.dtype)
    o = jnp.einsum('bkgqs,bskd->bqkgd', p, v)
    return o.reshape(b, nq, ATTN_WIDTH)


def short_conv(cb, cc, ch, w):
    return cb * dwconv3(cc * ch, w)


def merge_groups(attn, conv, aon, con, w_o):
    return jnp.concatenate([rmsnorm(attn, aon), rmsnorm(conv, con)], axis=-1) @ w_o


def conv_ffn(h, w_up, w_conv, w_down):
    u = dwconv3(h @ w_up, w_conv)
    a, g = u[..., :D_FF], u[..., D_FF:]
    return (jax.nn.silu(g) * a) @ w_down


def setup_inputs(seed: int = 0) -> dict:
    key = jax.random.key(seed)
    ks = jax.random.split(key, 20)
    f32 = jnp.float32
    nrm = lambda k, s, sc: jax.random.normal(k, s, f32) * sc
    return {
        "x": nrm(ks[0], (BATCH, SEQ, D_MODEL), 1.0),
        "c": nrm(ks[1], (BATCH, D_MODEL), 1.0),
        "ctx": nrm(ks[2], (BATCH, CTX_LEN, D_MODEL), 1.0),
        "c_ctx": nrm(ks[3], (D_MODEL,), 1.0),
        "w_ada": nrm(ks[4], (DEPTH, D_MODEL, N_MOD * D_MODEL), 0.5 * D_MODEL ** -0.5),
        "b_ada": nrm(ks[5], (DEPTH, N_MOD * D_MODEL), 0.01),
        "norm1_w": 1.0 + nrm(ks[6], (DEPTH, D_MODEL), 0.02),
        "w_in": nrm(ks[7], (DEPTH, D_MODEL, IN_WIDTH), D_MODEL ** -0.5),
        "q_norm_w": 1.0 + nrm(ks[8], (DEPTH, HEAD_DIM), 0.02),
        "k_norm_w": 1.0 + nrm(ks[9], (DEPTH, HEAD_DIM), 0.02),
        "conv_w": nrm(ks[10], (DEPTH, CONV_K, CONV_WIDTH), CONV_K ** -0.5),
        "attn_out_norm_w": 1.0 + nrm(ks[11], (DEPTH, ATTN_WIDTH), 0.02),
        "conv_out_norm_w": 1.0 + nrm(ks[12], (DEPTH, CONV_WIDTH), 0.02),
        "w_o": nrm(ks[13], (DEPTH, D_MODEL, D_MODEL), D_MODEL ** -0.5),
        "norm2_w": 1.0 + nrm(ks[14], (DEPTH, D_MODEL), 0.02),
        "w_ffn_up": nrm(ks[15], (DEPTH, D_MODEL, 2 * D_FF), D_MODEL ** -0.5),
        "ffn_conv_w": nrm(ks[16], (DEPTH, CONV_K, 2 * D_FF), CONV_K ** -0.5),
        "w_ffn_down": nrm(ks[17], (DEPTH, D_FF, D_MODEL), D_FF ** -0.5),
        "final_norm_w": 1.0 + nrm(ks[18], (D_MODEL,), 0.02),
    }


def reference(x, c, ctx, c_ctx, w_ada, b_ada, norm1_w, w_in, q_norm_w, k_norm_w,
              conv_w, attn_out_norm_w, conv_out_norm_w, w_o, norm2_w,
              w_ffn_up, ffn_conv_w, w_ffn_down, final_norm_w):
    b, n, _ = x.shape
    rows = n // GRID_W
    row = jnp.repeat(jnp.arange(rows, dtype=jnp.int32), GRID_W)
    col = jnp.tile(jnp.arange(GRID_W, dtype=jnp.int32), rows)
    nb = n // Q_BLOCK

    xs, cs = x, ctx
    for i in range(DEPTH):
        last = i == DEPTH - 1
        mod = jax.nn.silu(c) @ w_ada[i] + b_ada[i]
        mod_c = jax.nn.silu(c_ctx)[None, :] @ w_ada[i] + b_ada[i]
        sh1, sc1, g1, sh2, sc2, g2 = jnp.split(mod, N_MOD, axis=-1)
        csh1, csc1, cg1, csh2, csc2, cg2 = jnp.split(mod_c, N_MOD, axis=-1)

        h = modulate(rmsnorm(xs, norm1_w[i]), sh1, sc1)
        hc = modulate(rmsnorm(cs, norm1_w[i]), csh1, csc1)
        q, k, v, cb, cc, ch = split_proj(h @ w_in[i])
        qc, kc, vc, cbc, ccc, chc = split_proj(hc @ w_in[i])

        q = rope_2d(rmsnorm(heads(q, N_HEADS), q_norm_w[i]), row, col)
        k = rope_2d(rmsnorm(heads(k, N_KV_HEADS), k_norm_w[i]), row, col)
        v = heads(v, N_KV_HEADS)
        kc = rmsnorm(heads(kc, N_KV_HEADS), k_norm_w[i])
        vc = heads(vc, N_KV_HEADS)
        k_all = jnp.concatenate([kc, k], axis=1)
        v_all = jnp.concatenate([vc, v], axis=1)

        q_blocks = q.reshape(b, nb, Q_BLOCK, N_HEADS, HEAD_DIM).swapaxes(0, 1)
        attn = lax.map(lambda qb: attend(qb, k_all, v_all), q_blocks)
        attn = attn.swapaxes(0, 1).reshape(b, n, ATTN_WIDTH)
        conv = short_conv(cb, cc, ch, conv_w[i])
        xs = xs + g1[:, None, :] * merge_groups(attn, conv, attn_out_norm_w[i],
                                                  conv_out_norm_w[i], w_o[i])

        if not last:
            qc = rmsnorm(heads(qc, N_HEADS), q_norm_w[i])
            attn_c = attend(qc, kc, vc)
            conv_c = short_conv(cbc, ccc, chc, conv_w[i])
            cs = cs + cg1[:, None, :] * merge_groups(attn_c, conv_c, attn_out_norm_w[i],
                                                       conv_out_norm_w[i], w_o[i])

        h2 = modulate(rmsnorm(xs, norm2_w[i]), sh2, sc2)
        xs = xs + g2[:, None, :] * conv_ffn(h2, w_ffn_up[i], ffn_conv_w[i], w_ffn_down[i])
        if not last:
            hc2 = modulate(rmsnorm(cs, norm2_w[i]), csh2, csc2)
            cs = cs + cg2[:, None, :] * conv_ffn(hc2, w_ffn_up[i], ffn_conv_w[i], w_ffn_down[i])

    return rmsnorm(xs, final_norm_w)
```

```python
import os
import numpy as np
from contextlib import ExitStack
import concourse.bass as bass
import concourse.mybir as mybir
from concourse.bass_utils import run_bass_kernel_spmd

F32 = mybir.dt.float32
BF16 = mybir.dt.bfloat16
AF = mybir.ActivationFunctionType
ALU = mybir.AluOpType
AX = mybir.AxisListType

NCORES = 8
D = 2048
KC = 16
SEQ = 16384
TOWN = 2048
E = 2052
NT = 17
CTXC = 32
KOWN = TOWN + CTXC
NKEYS = KOWN * NCORES
NKC = NKEYS // 128
DFF = 5632
NFB = DFF // 128
EPS = 1e-6
TWO_PI = 6.283185307179586
PI = 3.141592653589793


def tile_rows(t):
    return 128 if t < 16 else 4


class Sem:
    def __init__(self, h, name):
        self.h = h
        self.name = name
        self.cnt = 0


class Buf:
    def __init__(self, name, excl=False):
        self.name = name
        self.w = {}
        self.r = {}
        self.dsem = None
        self.excl = excl


class Eng:
    def __init__(self, e, sem, name, selfwait=True):
        self.e = e
        self.sem = sem
        self.name = name
        self.waited = {}
        self.selfwait = selfwait


class Ctx:
    def __init__(self, nc, es):
        self.nc = nc
        self.es = es
        self.nsem = 0
        self.pe = Eng(nc.tensor, self.new_sem("pe"), "pe", selfwait=False)
        self.act = Eng(nc.scalar, self.new_sem("act"), "act")
        self.dve = Eng(nc.vector, self.new_sem("dve"), "dve")
        self.pool = Eng(nc.gpsimd, self.new_sem("pool"), "pool")
        self.sp = Eng(nc.sync, None, "sp")
        self.engs = [self.pe, self.act, self.dve, self.pool, self.sp]

    def new_sem(self, name):
        self.nsem += 1
        h = self.es.enter_context(self.nc.semaphore(f"s{self.nsem}_{name}"))
        s = Sem(h, name)
        if not hasattr(self, "sems"):
            self.sems = []
        self.sems.append(s)
        return s

    def _wait(self, eng, need):
        for sem, val in need.items():
            if sem is eng.sem and not eng.selfwait:
                continue
            if eng.waited.get(sem, 0) < val:
                eng.e.wait_ge(sem.h, val)
                eng.waited[sem] = val

    @staticmethod
    def _merge(dst, src):
        for s, v in src.items():
            if dst.get(s, 0) < v:
                dst[s] = v

    def op(self, eng, fn, reads=(), writes=()):
        need = {}
        for b in reads:
            self._merge(need, b.w)
            if b.excl:
                for s_, v_ in b.r.items():
                    if s_ is not eng.sem and need.get(s_, 0) < v_:
                        need[s_] = v_
        for b in writes:
            self._merge(need, b.w)
            self._merge(need, b.r)
        self._wait(eng, need)
        ins = fn()
        eng.sem.cnt += 1
        ins.then_inc(eng.sem.h, 1)
        v = eng.sem.cnt
        for b in reads:
            if b.r.get(eng.sem, 0) < v:
                b.r[eng.sem] = v
        for b in writes:
            b.w = {eng.sem: v}
            b.r = {}
        return ins

    def dma(self, q, out, in_, src, dst, owner, n=1, fn=None):
        need = {}
        self._merge(need, src.w)
        self._merge(need, dst.w)
        self._merge(need, dst.r)
        self._wait(q, need)
        if owner.dsem is None:
            owner.dsem = self.new_sem("d_" + owner.name)
        sem = owner.dsem
        if fn is None:
            q.e.dma_start(out=out, in_=in_).then_inc(sem.h, 16)
            sem.cnt += 16
        else:
            for ins in fn():
                ins.then_inc(sem.h, 16)
                sem.cnt += 16
        v = sem.cnt
        if src.r.get(sem, 0) < v:
            src.r[sem] = v
        dst.w = {sem: v}
        dst.r = {}

    def wait_all(self, eng, bufs):
        need = {}
        for b in bufs:
            self._merge(need, b.w)
            self._merge(need, b.r)
        self._wait(eng, need)


def build(stop=None):
    lvl = 5 if stop is None else stop
    nc = bass.Bass("TRN2", target_bir_lowering=False)
    es = ExitStack()
    K = Ctx(nc, es)
    pe, act, dve, pool, sp = K.pe, K.act, K.dve, K.pool, K.sp
    V_, S_, T_, G_ = nc.vector, nc.scalar, nc.tensor, nc.gpsimd

    def din(name, shape, dt=F32, need=0):
        if lvl < need:
            return None
        return nc.dram_tensor(name, shape, dt, kind="ExternalInput").ap()

    def dint(name, shape, dt, dbg_lvl=None):
        kind = "ExternalOutput" if (stop is not None and dbg_lvl is not None and stop + 0.5 >= dbg_lvl) else "Internal"
        return nc.dram_tensor(name, shape, dt, kind=kind).ap()

    x_all = din("x_all", [SEQ, D], need=2)
    x_ext = din("x_ext", [E, D], need=0.5)
    ctx_a = din("ctx_a", [256, D], need=2)
    cT_d = din("cT", [128, KC * 2])
    wada = din("w_ada", [D, 12288])
    bada = din("b_ada2", [2, 12288])
    n1T_d = din("n1T", [128, KC])
    n2T_d = din("n2T", [128, KC])
    w_in = din("w_in", [D, 4608], need=0.5)
    w_o = din("w_o", [D, D], need=4)
    w_up = din("w_up", [D, 2 * DFF], need=5)
    w_dn = din("w_dn", [DFF, D], need=5)
    qw_d = din("qw_bc", [128, 128])
    kw_d = din("kw_bc", [128, 128])
    convT_d = din("convT", [128, 24])
    fconvT_d = din("fconvT", [128, 264])
    aon_d = din("aon_bc", [128, 1024])
    conT_d = din("conT", [128, 8])
    fnw_d = din("fnw_bc", [128, D], need=5)
    posr_d = din("posr", [128, NT])
    posc_d = din("posc", [128, NT])
    posra_d = din("posr_all", [128, 128])
    posca_d = din("posc_all", [128, 1])
    jf_d = din("jfreq", [128, 32])
    hmask_d = din("hmask", [128, 2])
    ident_d = din("ident", [128, 128])
    out_d = nc.dram_tensor("out", [TOWN, D], F32, kind="ExternalOutput").ap() if stop is None else None

    tab_d = dint("tab_d", [128, 128 * 96], F32, 0)
    q_d = dint("q_d", [128, 8 * E], BF16, 1)
    zT_d = dint("zT_d", [128, 8 * E], BF16, 1)
    x1_d = dint("x1_d", [E, D], F32, 4)
    ao_d = dint("ao_d", [128, 8 * E], BF16, 3)
    h2_d = dint("h2_d", [128, KC * E], BF16, 4)
    modrow_d = dint("modrow_d", [2, 12288], F32, 0)
    wup_b = dint("wup_b", [22 * 128, 16 * 512], BF16)
    wdn_b = dint("wdn_b", [DFF, D], BF16)
    q_dv = q_d.rearrange("p (h e) -> p h e", h=8)
    zT_dv = zT_d.rearrange("p (h e) -> p h e", h=8)
    ao_dv = ao_d.rearrange("p (h e) -> p h e", h=8)
    h2_dv = h2_d.rearrange("p (c e) -> p c e", c=KC)

    uid = [0]

    def sb(st, name, shape, dt=F32):
        uid[0] += 1
        return st.enter_context(nc.sbuf_tensor(f"sb{uid[0]}_{name}", shape, dt))

    def psum(st, name, shape, dt=F32):
        uid[0] += 1
        return st.enter_context(nc.psum_tensor(f"pp{uid[0]}_{name}", shape, dt))

    class Ring:
        def __init__(self, st, name, shape, dt, n, ps=False):
            mk = psum if ps else sb
            self.t = [mk(st, f"{name}{i}", shape, dt) for i in range(n)]
            self.b = [Buf(f"{name}{i}", excl=ps) for i in range(n)]
            if ps:
                for b_ in self.b:
                    b_.hi = Buf(b_.name + "hi", excl=True)
            self.i = 0

        def next(self):
            k = self.i % len(self.t)
            self.i += 1
            return self.t[k], self.b[k]

    XB = Buf("ext")
    DBG = os.environ.get("KDUMP") is not None and stop is not None

    def ddump(name, ap, buf, shape, dt=F32):
        if not DBG:
            return
        t_ = nc.dram_tensor("dd_" + name, shape, dt, kind="ExternalOutput").ap()
        K.dma(sp, t_, ap, buf, Buf("dd" + name), buf)

    ident_f = sb(es, "ident_f", [128, 128]); B_identf = Buf("identf")
    ident_b = sb(es, "ident_b", [128, 128], BF16); B_identb = Buf("identb")
    n1T = sb(es, "n1T", [128, KC]); n2T = sb(es, "n2T", [128, KC]); B_nT = Buf("nT")
    a1 = sb(es, "a1", [128, KC]); s1 = sb(es, "s1", [128, KC])
    a1c = sb(es, "a1c", [128, KC]); s1c = sb(es, "s1c", [128, KC])
    a2 = sb(es, "a2", [128, KC]); s2 = sb(es, "s2", [128, KC]); B_mods = Buf("mods")
    eps_t = sb(es, "eps_t", [128, 1]); B_eps = Buf("eps")
    ssa = sb(es, "ssa", [128, NT]); B_ssa = Buf("ssa")
    ssc = sb(es, "ssc", [128, NT]); B_ssc = Buf("ssc")
    hmask = sb(es, "hmask", [128, 2]); B_hmask = Buf("hmask")
    convT = sb(es, "convT", [128, 8, 3]); fconvT = sb(es, "fconvT", [128, 88, 3]); conT = sb(es, "conT", [128, 8])
    B_cw = Buf("cw")
    ones_c = sb(es, "ones_c", [128, 1]); B_onesc = Buf("onesc")
    junk = sb(es, "junk", [128, D], BF16); B_junk = Buf("junk")
    stR = Ring(es, "st", [128, 16], F32, 6)

    def ld(dst, src, B):
        K.dma(sp, dst, src, XB, B, B)
    ld(ident_f[:], ident_d[:, :], B_identf)
    K.op(dve, lambda: V_.tensor_copy(ident_b[:], ident_f[:]), [B_identf], [B_identb])
    K.dma(sp, None, None, XB, B_nT, B_nT,
          fn=lambda: [nc.sync.dma_start(out=n1T[:], in_=n1T_d[:, :]), nc.sync.dma_start(out=n2T[:], in_=n2T_d[:, :])])
    ld(hmask[:], hmask_d[:, :], B_hmask)
    K.dma(sp, None, None, XB, B_cw, B_cw, fn=lambda: [
        nc.sync.dma_start(out=convT[:], in_=convT_d.rearrange("p (j k) -> p j k", k=3)),
        nc.sync.dma_start(out=fconvT[:], in_=fconvT_d.rearrange("p (j k) -> p j k", k=3)),
        nc.sync.dma_start(out=conT[:], in_=conT_d[:, :])])
    K.op(dve, lambda: V_.memset(ones_c[:], 1.0), [], [B_onesc])
    K.op(dve, lambda: V_.memset(eps_t[:], EPS), [], [B_eps])

    B_wupb = Buf("wupb"); B_wdnb = Buf("wdnb")
    wup_bv = wup_b.rearrange("(j p) (c a n) -> j p c a n", p=128, c=16, a=2)
    w_up_v = w_up.rearrange("(c p) (a j n) -> j p c a n", p=128, a=2, j=22) if w_up is not None else None

    def cvt_up():
        return [G_.dma_start(out=wup_bv[j][:, :, a, :], in_=w_up_v[j][:, :, a, :]) for j in range(22) for a in range(2)]

    def cvt_dn():
        return [G_.dma_start(out=wdn_b[i * 704:(i + 1) * 704, :], in_=w_dn[i * 704:(i + 1) * 704, :]) for i in range(8)]

    def norm_T(xs, Bx, nq, av, sv, dst, Bdst2, xhR, tpR, defer=False):
        st, Bst = stR.next()
        K.op(dve, lambda: V_.memset(st[:], 0.0), [], [Bst])
        K.op(act, lambda: S_.activation(out=junk[:nq], in_=xs[:nq], func=AF.Square, accum_out=st[:nq, 0:1]),
             [Bx], [B_junk, Bst])
        K.op(act, lambda: S_.activation(out=st[:nq, 1:2], in_=st[:nq, 0:1], func=AF.Sqrt, scale=1.0 / D, bias=eps_t[:nq, 0:1]),
             [Bst, B_eps], [Bst])
        K.op(dve, lambda: V_.reciprocal(st[:nq, 2:3], st[:nq, 1:2]), [Bst], [Bst])
        xh, Bxh = xhR.next()
        K.op(dve, lambda: V_.tensor_scalar(xh[:nq], xs[:nq], st[:nq, 2:3], None, op0=ALU.mult), [Bx, Bst], [Bxh])
        tp, Btp = tpR.next()

        def trs():
            ins = None
            for c in range(KC):
                ins = T_.transpose(tp[:, c * 128: c * 128 + nq], xh[:nq, c * 128:(c + 1) * 128], ident_b[:nq, :nq])
            return ins
        K.op(pe, trs, [Bxh, B_identb], [Btp, Btp.hi])

        def ev_act():
            ins = None
            for c in range(0, KC // 2):
                ins = S_.activation(out=dst(c), in_=tp[:, c * 128: c * 128 + nq], func=AF.Identity,
                                    scale=av[:, c:c + 1], bias=sv[:, c:c + 1])
            return ins

        def ev_dve():
            ins = None
            for c in range(KC // 2, KC):
                ins = V_.tensor_scalar(dst(c), tp[:, c * 128: c * 128 + nq], av[:, c:c + 1], sv[:, c:c + 1],
                                       op0=ALU.mult, op1=ALU.add)
            return ins
        def evac():
            K.op(act, ev_act, [Btp, B_mods], [Bdst2[0]])
            K.op(dve, ev_dve, [Btp.hi, B_mods], [Bdst2[1]])
        if defer:
            return evac
        evac()

    def rope_add(tmpA, tmpB, nq):
        return V_.tensor_tensor(out=tmpA[:nq], in0=tmpA[:nq], in1=tmpB[:nq], op=ALU.add)

    def rope(xq, tmpA, tmpB, nq, nh, cr, sr, nsr, cc_, sc_, nsc_):
        ins = None
        xv = xq[:nq].rearrange("p h (a f j) -> p h a f j", a=2, f=2)
        av = tmpA[:nq].rearrange("p h (a f j) -> p h a f j", a=2, f=2)
        bv = tmpB[:nq].rearrange("p h (a f j) -> p h a f j", a=2, f=2)
        for ax, (c_, s_, ns_) in enumerate(((cr, sr, nsr), (cc_, sc_, nsc_))):
            cb = c_.unsqueeze(1).unsqueeze(1).to_broadcast([nq, nh, 2, 32])
            V_.tensor_tensor(out=av[:, :, ax, :, :], in0=xv[:, :, ax, :, :], in1=cb, op=ALU.mult)
            sb_ = s_.unsqueeze(1).to_broadcast([nq, nh, 32])
            nsb = ns_.unsqueeze(1).to_broadcast([nq, nh, 32])
            V_.tensor_tensor(out=bv[:, :, ax, 0, :], in0=xv[:, :, ax, 1, :], in1=nsb, op=ALU.mult)
            ins = V_.tensor_tensor(out=bv[:, :, ax, 1, :], in0=xv[:, :, ax, 0, :], in1=sb_, op=ALU.mult)
        return ins

    def make_tables(st, pos, n, tab, Btab, Bpos, name):
        fr = sb(st, name + "fr", [128, 32]); jf = sb(st, name + "jf", [128, 32])
        ang = sb(st, name + "ang", [128, n, 32]); tmp = sb(st, name + "tmp", [128, n, 32]); ang2 = sb(st, name + "ang2", [128, n, 32])
        negpi = sb(st, name + "npi", [128, 1])
        Bl = Buf(name + "tl")
        K.dma(sp, jf[:], jf_d[:, :], XB, Bl, Bl)
        K.op(dve, lambda: V_.tensor_copy(fr[:], jf[:]), [Bl], [Bl])

        D1 = lambda f_: K.op(dve, f_, [Bl, Bpos], [Bl])
        D1(lambda: V_.memset(negpi[:], -PI))
        D1(lambda: V_.tensor_tensor(out=ang[:], in0=pos.unsqueeze(2).to_broadcast([128, n, 32]),
                                    in1=fr[:].unsqueeze(1).to_broadcast([128, n, 32]), op=ALU.mult))
        D1(lambda: V_.tensor_scalar(ang[:], ang[:], PI, None, op0=ALU.add))
        for m in (32, 16, 8, 4, 2, 1):
            cst = float(m * TWO_PI)
            D1(lambda cst=cst: V_.tensor_scalar(tmp[:], ang[:], cst, cst, op0=ALU.is_ge, op1=ALU.mult))
            D1(lambda: V_.tensor_tensor(out=ang[:], in0=ang[:], in1=tmp[:], op=ALU.subtract))
        D1(lambda: V_.tensor_scalar(ang2[:], ang[:], PI / 2, None, op0=ALU.add))
        D1(lambda: V_.tensor_scalar(tmp[:], ang2[:], TWO_PI, TWO_PI, op0=ALU.is_ge, op1=ALU.mult))
        D1(lambda: V_.tensor_tensor(out=ang2[:], in0=ang2[:], in1=tmp[:], op=ALU.subtract))

        def sins():
            S_.activation(out=tab[:, :, 32:64], in_=ang[:], func=AF.Sin, bias=negpi[:, 0:1])
            return S_.activation(out=tab[:, :, 0:32], in_=ang2[:], func=AF.Sin, bias=negpi[:, 0:1])
        K.op(act, sins, [Bl], [Btab])
        K.op(dve, lambda: V_.tensor_scalar(tab[:, :, 64:96], tab[:, :, 32:64], -1.0, None, op0=ALU.mult), [Btab], [Btab])

    tabca = sb(es, "tabca", [128, 1, 96]); B_tabca = Buf("tabca")
    tab_st = ExitStack()
    tabo = sb(tab_st, "tabo", [128, NT, 96]); B_tabo = Buf("tabo")
    tabc = sb(tab_st, "tabc", [128, NT, 96]); B_tabc = Buf("tabc")
    B_tabd = Buf("tabd"); B_modd = Buf("modd")
    with ExitStack() as ph:
        cT = sb(ph, "cT", [128, KC, 2]); B_cT = Buf("cT")
        sT = sb(ph, "sT", [128, KC, 2]); B_sT = Buf("sT")
        wadR = Ring(ph, "wad", [128, KC, 512], F32, 2)
        bsR = Ring(ph, "bsb", [2, 512], F32, 2)
        modrow = sb(ph, "modrow", [2, 12288]); B_modrow = Buf("modrow")
        fm = sb(ph, "fm", [128, 64, 2]); B_fm = Buf("fm")
        ones_r = sb(ph, "ones_r", [1, 128]); B_ones = Buf("ones")
        psmR = Ring(ph, "ps_mod", [2, 512], F32, 2, ps=True)
        ps_fm = psum(ph, "ps_fm", [128, 64, 2]); B_psfm = Buf("psfm", excl=True)

        ld(cT[:], cT_d.rearrange("p (c i) -> p c i", i=2), B_cT)
        K.op(act, lambda: S_.activation(out=sT[:], in_=cT[:], func=AF.Silu), [B_cT], [B_sT])
        K.op(dve, lambda: V_.memset(ones_r[:], 1.0), [], [B_ones])
        wr = wada.rearrange("(c p) n -> p c n", p=128)
        for i in range(24):
            wad, Bw = wadR.next()
            ld(wad[:], wr[:, :, i * 512:(i + 1) * 512], Bw)
            bsb, B_bsb = bsR.next()
            ld(bsb[:], bada[:, i * 512:(i + 1) * 512], B_bsb)
            psm, Bpsm = psmR.next()

            def mm_mod(wad=wad, psm=psm):
                ins = None
                for c in range(KC):
                    ins = T_.matmul(psm[0:2, :], sT[:, c, :], wad[:, c, :], start=(c == 0), stop=(c == KC - 1))
                return ins
            K.op(pe, mm_mod, [B_sT, Bw], [Bpsm])
            K.op(dve, lambda psm=psm, i=i, bsb=bsb: V_.tensor_tensor(out=modrow[0:2, i * 512:(i + 1) * 512], in0=psm[0:2, :],
                                                             in1=bsb[0:2, :], op=ALU.add),
                 [Bpsm, B_bsb], [B_modrow])
        bases = [0, 2048, 6144, 8192]

        def mm_fm():
            ins = None
            for q in range(4):
                for c in range(KC):
                    ins = T_.matmul(ps_fm[:, q * KC + c, :], modrow[0:2, bases[q] + c * 128: bases[q] + (c + 1) * 128],
                                    ident_f[0:2, 0:2], start=True, stop=True)
            return ins
        K.op(pe, mm_fm, [B_modrow, B_identf], [B_psfm])
        K.op(dve, lambda: V_.tensor_copy(fm[:], ps_fm[:]), [B_psfm], [B_fm])

        for (av, sv, nT, qsc, qsh, i) in ((a1, s1, n1T, 1, 0, 0), (a1c, s1c, n1T, 1, 0, 1), (a2, s2, n2T, 3, 2, 0)):
            K.op(dve, lambda av=av, qsc=qsc, i=i: V_.tensor_scalar(av[:], fm[:, qsc * KC:(qsc + 1) * KC, i], 1.0, None, op0=ALU.add),
                 [B_fm, B_nT], [B_mods])
            K.op(dve, lambda av=av, nT=nT: V_.tensor_tensor(out=av[:], in0=av[:], in1=nT[:], op=ALU.mult), [B_fm, B_nT], [B_mods])
            K.op(dve, lambda sv=sv, qsh=qsh, i=i: V_.tensor_copy(sv[:], fm[:, qsh * KC:(qsh + 1) * KC, i]), [B_fm, B_nT], [B_mods])
        K.dma(sp, modrow_d[:, :], modrow[:], B_modrow, B_modd, B_modrow)
        allb = [B_cT, B_sT, B_modrow, B_fm, B_ones, B_psfm, B_modd, B_mods] + wadR.b + psmR.b + bsR.b
        for e in K.engs:
            K.wait_all(e, allb)
    with ExitStack() as ph:
        posr = sb(ph, "posr", [128, NT]); posc = sb(ph, "posc", [128, NT])
        posra = sb(ph, "posra", [128, 128]); posca = sb(ph, "posca", [128, 1]); B_pos = Buf("pos")
        K.dma(sp, None, None, XB, B_pos, B_pos, fn=lambda: [
            nc.sync.dma_start(out=posr[:], in_=posr_d[:, :]), nc.sync.dma_start(out=posc[:], in_=posc_d[:, :]),
            nc.sync.dma_start(out=posra[:], in_=posra_d[:, :]), nc.sync.dma_start(out=posca[:], in_=posca_d[:, :])])
        make_tables(ph, posr[:], NT, tabo, B_tabo, B_pos, "to")
        make_tables(ph, posc[:], NT, tabc, B_tabc, B_pos, "tc")
        make_tables(ph, posca[:], 1, tabca, B_tabca, B_pos, "tca")
        taba = sb(ph, "taba", [128, 128, 96]); B_taba = Buf("taba")
        make_tables(ph, posra[:], 128, taba, B_taba, B_pos, "ta")
        K.dma(sp, tab_d[:, :], taba[:].rearrange("p n k -> p (n k)"), B_taba, B_tabd, B_taba)
        allb = [B_pos, B_taba, B_tabd, B_tabo, B_tabc, B_tabca]
        for e in K.engs:
            K.wait_all(e, allb)


    if lvl == 0:
        tab_st.close()
        for s_ in K.sems:
            if s_.cnt > 0:
                nc.sync.wait_ge(s_.h, s_.cnt)
        return nc, es
    with ExitStack() as ph:
        hT = sb(ph, "hT", [128, KC, E], BF16)
        B_hT = [[Buf(f"hTe{t}"), Buf(f"hTo{t}")] for t in range(NT)]
        zacc = sb(ph, "zacc", [128, E]); B_zacc = Buf("zacc")
        w_in_v = w_in.rearrange("(c p) n -> p c n", p=128)
        K.op(dve, lambda: V_.memset(zacc[:], 0.0), [], [B_zacc])
        B_qd = [Buf(f"qd{t}") for t in range(NT)]
        with ExitStack() as ph2:
            wq = sb(ph2, "wq", [128, KC, 1024], BF16); B_wq = Buf("wq")
            qw_bc = sb(ph2, "qw_bc", [128, 128]); B_qw = Buf("qw")
            xsR = Ring(ph2, "xs", [128, D], F32, 2)
            xhR = Ring(ph2, "xh", [128, D], BF16, 2)
            ld(qw_bc[:], qw_d[:, :], B_qw)
            K.dma(pool, None, None, XB, B_wq, B_wq, fn=lambda: [
                G_.dma_start(out=wq[:, :, i * 512:(i + 1) * 512], in_=w_in_v[:, :, i * 512:(i + 1) * 512]) for i in range(2)])
            tpR = Ring(ph2, "tp", [128, D], BF16, 2, ps=True)
            psqR = Ring(ph2, "psq", [128, 1024], F32, 1, ps=True)
            pstR = Ring(ph2, "pst", [128, 1024], BF16, 1, ps=True)
            sqR = Ring(ph2, "sq", [128, 1024], F32, 1)
            qnR = Ring(ph2, "qn", [128, 8, 128], F32, 1)
            qaR = Ring(ph2, "qa", [128, 8, 128], F32, 1)
            qbR = Ring(ph2, "qb", [128, 8, 128], F32, 1)
            qhR = Ring(ph2, "qh", [128, 1024], BF16, 1)
            qTR = Ring(ph2, "qT", [128, 8, 128], BF16, 2)
            KTILES = [int(v) for v in os.environ.get("KTILES", ",".join(str(i) for i in range(NT))).split(",")]
            KSTAGE = int(os.environ.get("KSTAGE", "9"))
            for t in range(NT):
                if t not in KTILES:
                    continue
                nq = tile_rows(t)
                xs, Bxs = xsR.next()
                ld(xs[:nq], x_ext[128 * t: 128 * t + nq, :], Bxs)
                norm_T(xs, Bxs, nq, a1, s1, lambda c, t=t, nq=nq: hT[:, c, 128 * t: 128 * t + nq], B_hT[t], xhR, tpR)
                if t == 0:
                    K.wait_all(sp, B_hT[0])
                    ddump("hT0", hT[:, :, 0:128], B_hT[0][0], [128, KC, 128], BF16)
                    ddump("a1", a1[:], B_mods, [128, KC]); ddump("s1", s1[:], B_mods, [128, KC])
                if KSTAGE <= 1:
                    continue
                psq, Bpsq = psqR.next()

                def mm_q(t=t, nq=nq, psq=psq):
                    ins = None
                    for nb in range(2):
                        for c in range(KC):
                            ins = T_.matmul(psq[:nq, nb * 512:(nb + 1) * 512], hT[:, c, 128 * t: 128 * t + nq],
                                            wq[:, c, nb * 512:(nb + 1) * 512], start=(c == 0), stop=(c == KC - 1))
                    return ins
                K.op(pe, mm_q, B_hT[t] + [B_wq], [Bpsq])
                sq, Bsq = sqR.next(); st, Bst = stR.next()
                K.op(act, lambda sq=sq, psq=psq, nq=nq: S_.activation(out=sq[:nq], in_=psq[:nq], func=AF.Square), [Bpsq], [Bsq])
                K.op(dve, lambda sq=sq, st=st, nq=nq: V_.reduce_sum(out=st[:nq, 0:8], in_=sq[:nq].rearrange("p (h d) -> p h d", h=8), axis=AX.X),
                     [Bsq], [Bst])
                K.op(act, lambda st=st, nq=nq: S_.activation(out=st[:nq, 8:16], in_=st[:nq, 0:8], func=AF.Sqrt, scale=1.0 / 128, bias=eps_t[:nq, 0:1]),
                     [Bst], [Bst])
                if t == 0:
                    ddump("sq0", sq[:, :], Bsq, [128, 1024]); ddump("st0", st[:, :], Bst, [128, 16])
                if KSTAGE <= 2:
                    continue
                qn, Bqn = qnR.next(); qa, Bqa = qaR.next(); qb, Bqb = qbR.next()

                K.op(dve, lambda st=st, nq=nq: V_.reciprocal(st[:nq, 0:8], st[:nq, 8:16]), [Bst], [Bst])
                K.op(dve, lambda st=st, nq=nq, psq=psq, qn=qn: V_.tensor_tensor(
                    out=qn[:nq], in0=psq[:nq].rearrange("p (h d) -> p h d", h=8),
                    in1=st[:nq, 0:8].unsqueeze(2).to_broadcast([nq, 8, 128]), op=ALU.mult), [Bst, Bpsq], [Bqn])
                K.op(dve, lambda nq=nq, qn=qn: V_.tensor_tensor(out=qn[:nq], in0=qn[:nq], in1=qw_bc[:nq].unsqueeze(1).to_broadcast([nq, 8, 128]),
                                                               op=ALU.mult), [B_qw], [Bqn])
                K.op(dve, lambda nq=nq, qn=qn, qa=qa, qb=qb, t=t: rope(
                    qn, qa, qb, nq, 8, tabo[:nq, t, 0:32], tabo[:nq, t, 32:64], tabo[:nq, t, 64:96],
                    tabc[:nq, t, 0:32], tabc[:nq, t, 32:64], tabc[:nq, t, 64:96]), [Bqn, B_tabo, B_tabc], [Bqa, Bqb])
                K.op(dve, lambda nq=nq, qa=qa, qb=qb: rope_add(qa, qb, nq), [Bqb], [Bqa])
                if t == 0:
                    ddump("qn0", qn[:].rearrange("p h d -> p (h d)"), Bqn, [128, 1024]); ddump("qa0", qa[:].rearrange("p h d -> p (h d)"), Bqa, [128, 1024])
                    ddump("st0b", st[:, :], Bst, [128, 16])
                if KSTAGE <= 3:
                    continue
                qh, Bqh = qhR.next()
                K.op(act, lambda qh=qh, qa=qa, nq=nq: S_.copy(qh[:nq], qa[:nq].rearrange("p h d -> p (h d)")), [Bqa], [Bqh])
                pst, Bpst = pstR.next()

                def trq(pst=pst, qh=qh, nq=nq):
                    ins = None
                    for h in range(8):
                        ins = T_.transpose(pst[:, h * 128: h * 128 + nq], qh[:nq, h * 128:(h + 1) * 128], ident_b[:nq, :nq])
                    return ins
                K.op(pe, trq, [Bqh, B_identb], [Bpst])
                qT, BqT = qTR.next()
                K.op(dve, lambda qT=qT, pst=pst, nq=nq: V_.tensor_copy(qT[:, :, :nq], pst[:].rearrange("p (h d) -> p h d", h=8)[:, :, :nq]),
                     [Bpst], [BqT])
                K.dma(sp, q_dv[:, :, 128 * t: 128 * t + nq], qT[:, :, :nq], BqT, B_qd[t], BqT)
            for e in K.engs:
                K.wait_all(e, tpR.b + [b_.hi for b_ in tpR.b] + psqR.b + pstR.b + [B_wq, B_qw] + xsR.b + xhR.b + sqR.b + qnR.b + qaR.b + qbR.b + qhR.b + qTR.b + B_qd)
        if lvl >= 1:
            with ExitStack() as ph2:
                pcR = Ring(ph2, "pc", [128, 3, 512], F32, 2, ps=True)
                uR = Ring(ph2, "u", [128, 512], F32, 2)
                ucR = Ring(ph2, "uc", [128, 512], F32, 2)
                yR = Ring(ph2, "y", [128, 512], F32, 2)
                zR = Ring(ph2, "z", [128, 512], F32, 2)
                zsR = Ring(ph2, "zs", [128, 512], F32, 2)
                zbR = Ring(ph2, "zb", [128, 512], BF16, 2)
                wcR = Ring(ph2, "wc", [128, KC, 3, 128], BF16, 2)
                zz = sb(ph2, "zz", [128, 8, 1], BF16); B_zz = Buf("zz")
                B_zTd = Buf("zTd")
                K.op(dve, lambda: V_.memset(zz[:], 0.0), [], [B_zz])
                K.dma(sp, None, None, B_zz, B_zTd, B_zz, fn=lambda: [
                    nc.sync.dma_start(out=zT_dv[:, :, 0:1], in_=zz[:], allow_slow_non_contiguous=True),
                    nc.sync.dma_start(out=zT_dv[:, :, 2051:2052], in_=zz[:], allow_slow_non_contiguous=True)])
                ps_ss = psum(ph2, "ps_ss", [128, NT]); B_psss = Buf("psss", excl=True)
                allhT = [b for pr in B_hT for b in pr]
                for j in range(8):
                    wc, Bwc = wcR.next()
                    K.dma(pool, None, None, XB, Bwc, Bwc, fn=lambda wc=wc, j=j: [
                        G_.dma_start(out=wc[:, :, k, :], in_=w_in_v[:, :, 1536 + 1024 * k + 128 * j: 1536 + 1024 * k + 128 * (j + 1)])
                        for k in range(3)])
                    for s in range(5):
                        c0 = 510 * s
                        N = 512 if s < 4 else 12
                        pc, Bpc = pcR.next()

                        def mm_c(pc=pc, wc=wc, c0=c0, N=N):
                            ins = None
                            for k in range(3):
                                for c in range(KC):
                                    ins = T_.matmul(pc[:, k, :N], wc[:, c, k, :], hT[:, c, c0:c0 + N], start=(c == 0), stop=(c == KC - 1))
                            return ins
                        K.op(pe, mm_c, allhT + [Bwc], [Bpc])
                        uc, Buc = ucR.next(); u, Bu = uR.next(); y, By = yR.next(); z, Bz = zR.next(); zs, Bzs = zsR.next()
                        zb, Bzb = zbR.next()
                        K.op(act, lambda uc=uc, pc=pc, N=N: S_.copy(uc[:, :N], pc[:, 1, :N]), [Bpc], [Buc])

                        M = N - 2
                        K.op(dve, lambda u=u, uc=uc, pc=pc, N=N: V_.tensor_tensor(out=u[:, :N], in0=uc[:, :N], in1=pc[:, 2, :N], op=ALU.mult),
                             [Buc, Bpc], [Bu])
                        if c0 <= 1 < c0 + N:
                            K.op(dve, lambda u=u, c0=c0: V_.tensor_scalar(u[:, 1 - c0:2 - c0], u[:, 1 - c0:2 - c0], hmask[:, 0:1], None, op0=ALU.mult),
                                 [B_hmask], [Bu])
                        if c0 <= 2050 < c0 + N:
                            K.op(dve, lambda u=u, c0=c0: V_.tensor_scalar(u[:, 2050 - c0:2051 - c0], u[:, 2050 - c0:2051 - c0], hmask[:, 1:2], None, op0=ALU.mult),
                                 [B_hmask], [Bu])
                        K.op(dve, lambda u=u, y=y, M=M, j=j: V_.tensor_scalar(y[:, :M], u[:, 0:M], convT[:, j, 0:1], None, op0=ALU.mult), [Bu, B_cw], [By])
                        K.op(dve, lambda u=u, y=y, M=M, j=j: V_.scalar_tensor_tensor(out=y[:, :M], in0=u[:, 1:M + 1], scalar=convT[:, j, 1:2], in1=y[:, :M],
                                                                                     op0=ALU.mult, op1=ALU.add), [Bu, B_cw], [By])
                        K.op(dve, lambda u=u, y=y, M=M, j=j: V_.scalar_tensor_tensor(out=y[:, :M], in0=u[:, 2:M + 2], scalar=convT[:, j, 2:3], in1=y[:, :M],
                                                                                     op0=ALU.mult, op1=ALU.add), [Bu, B_cw], [By])
                        K.op(dve, lambda y=y, z=z, pc=pc, M=M: V_.tensor_tensor(out=z[:, :M], in0=y[:, :M], in1=pc[:, 0, 1:M + 1], op=ALU.mult), [By, Bpc], [Bz])
                        K.op(dve, lambda z=z, zs=zs, M=M: V_.tensor_tensor(out=zs[:, :M], in0=z[:, :M], in1=z[:, :M], op=ALU.mult), [Bz], [Bzs])
                        K.op(dve, lambda zs=zs, M=M, c0=c0: V_.tensor_tensor(out=zacc[:, c0 + 1:c0 + 1 + M], in0=zacc[:, c0 + 1:c0 + 1 + M], in1=zs[:, :M], op=ALU.add),
                             [Bzs], [B_zacc])
                        K.op(dve, lambda z=z, zb=zb, M=M, j=j: V_.tensor_scalar(zb[:, :M], z[:, :M], conT[:, j:j + 1], None, op0=ALU.mult), [Bz, B_cw], [Bzb])
                        K.dma(sp, zT_dv[:, j, c0 + 1:c0 + N - 1], zb[:, :N - 2], Bzb, B_zTd, Bzb)

                def mm_ss():
                    ins = None
                    for t in range(NT):
                        nq = tile_rows(t)
                        ins = T_.matmul(ps_ss[:nq, t:t + 1], zacc[:, 128 * t:128 * t + nq], ones_c[:, 0:1], start=True, stop=True)
                    return ins
                K.op(dve, lambda: V_.memset(ssc[:], 1.0), [], [B_ssc])
                K.op(pe, mm_ss, [B_zacc, B_onesc], [B_psss])

                def cp_ss():
                    V_.tensor_copy(ssc[:, 0:16], ps_ss[:, 0:16])
                    return V_.tensor_copy(ssc[0:4, 16:17], ps_ss[0:4, 16:17])
                K.op(dve, cp_ss, [B_psss], [B_ssc])
                allb = pcR.b + uR.b + ucR.b + yR.b + zR.b + zsR.b + zbR.b + [B_psss, B_zacc, B_zTd, B_zz] + allhT + wcR.b
                for e in K.engs:
                    K.wait_all(e, allb)

    tab_st.close()
    if lvl <= 1:
        ssc_dbg = nc.dram_tensor("ssc_dbg", [128, NT], F32, kind="ExternalOutput").ap()
        K.dma(sp, ssc_dbg[:, :], ssc[:], B_ssc, Buf("sscd"), B_ssc)
        for s_ in K.sems:
            if s_.cnt > 0:
                nc.sync.wait_ge(s_.h, s_.cnt)
        return nc, es
    B_AO = [Buf(f"AO{t}") for t in range(NT)]
    with ExitStack() as ph:
        KT = sb(ph, "KT", [128, 2, NKEYS], BF16); B_KT = Buf("KT")
        VV = sb(ph, "VV", [128, NKC, 2, 130], BF16); B_VV = Buf("VV")
        K.op(dve, lambda: V_.memset(VV[:, :, :, 128:130], 1.0), [], [B_VV])
        with ExitStack() as ph2:
            wkv = sb(ph2, "wkv", [128, KC, 512], BF16); B_wkv = Buf("wkv")
            kw_bc = sb(ph2, "kw_bc", [128, 128]); B_kw = Buf("kw")
            xsR = Ring(ph2, "xs", [128, D], F32, 2)
            xhR = Ring(ph2, "xh", [128, D], BF16, 2)
            hTR = Ring(ph2, "hTt", [128, KC, 128], BF16, 2)
            hTB = [[Buf("hTa0"), Buf("hTb0")], [Buf("hTa1"), Buf("hTb1")]]
            tbR = Ring(ph2, "tb", [128, 96], F32, 3)
            tpR = Ring(ph2, "tp", [128, D], BF16, 2, ps=True)
            pkvR = Ring(ph2, "pkv", [128, 512], F32, 3, ps=True)
            pktR = Ring(ph2, "pkt", [128, 256], BF16, 1, ps=True)
            sqR = Ring(ph2, "sq", [128, 256], F32, 1)
            knR = Ring(ph2, "kn", [128, 2, 128], F32, 1)
            kaR = Ring(ph2, "ka", [128, 2, 128], F32, 1)
            kbR = Ring(ph2, "kb", [128, 2, 128], F32, 1)
            khR = Ring(ph2, "kh", [128, 256], BF16, 1)
            ld(kw_bc[:], kw_d[:, :], B_kw)
            w_in_v = w_in.rearrange("(c p) n -> p c n", p=128)
            K.dma(pool, wkv[:], w_in_v[:, :, 1024:1536], XB, B_wkv, B_wkv, fn=lambda: [G_.dma_start(out=wkv[:], in_=w_in_v[:, :, 1024:1536])])
            tab_dv = tab_d.rearrange("p (n k) -> p n k", k=96)
            def partA1(kc):
                lat = kc < 128
                xs, Bxs = xsR.next()
                if lat:
                    ld(xs[:], x_all[128 * kc:128 * (kc + 1), :], Bxs)
                    tb, Btb = tbR.next()
                    K.dma(sp, tb[:], tab_dv[:, kc, :], B_tabd, Btb, Btb)
                else:
                    ld(xs[:], ctx_a[128 * (kc - 128):128 * (kc - 127), :], Bxs)
                hTt, BhTt = hTR.next()
                hb = hTB[kc % 2]
                ev_ = norm_T(xs, Bxs, 128, a1 if lat else a1c, s1 if lat else s1c, lambda c, hTt=hTt: hTt[:, c, :], hb, xhR, tpR, defer=True)
                return (hTt, hb, (tb if lat else None), (Btb if lat else None), lat), ev_

            def partA2(kc, a1_):
                hTt, hb, tb, Btb, lat = a1_
                pkv, Bpkv = pkvR.next()

                def mm_kv(pkv=pkv, hTt=hTt):
                    ins = None
                    for c in range(KC):
                        ins = T_.matmul(pkv[:, :], hTt[:, c, :], wkv[:, c, :], start=(c == 0), stop=(c == KC - 1))
                    return ins
                K.op(pe, mm_kv, hb + [B_wkv], [Bpkv])
                return (pkv, Bpkv, tb, Btb, lat)

            def partB(kc, stt_):
                pkv, Bpkv, tb, Btb, lat = stt_
                sq, Bsq = sqR.next(); st, Bst = stR.next()

                K.op(dve, lambda st=st: V_.memset(st[:], 0.0), [], [Bst])

                def a_post(pkv=pkv, sq=sq, kc=kc, st=st):
                    S_.copy(VV[:, kc, :, 0:128], pkv[:, 256:512].rearrange("p (g d) -> p g d", g=2))
                    S_.activation(out=sq[:, 0:128], in_=pkv[:, 0:128], func=AF.Square, accum_out=st[:, 0:1])
                    return S_.activation(out=sq[:, 128:256], in_=pkv[:, 128:256], func=AF.Square, accum_out=st[:, 1:2])
                K.op(act, a_post, [Bpkv], [Bsq, B_VV, Bst])
                K.op(act, lambda st=st: S_.activation(out=st[:, 2:4], in_=st[:, 0:2], func=AF.Sqrt, scale=1.0 / 128, bias=eps_t[:, 0:1]), [Bst], [Bst])
                kn, Bkn = knR.next(); ka, Bka = kaR.next(); kb, Bkb = kbR.next(); kh, Bkh = khR.next()

                K.op(dve, lambda st=st: V_.reciprocal(st[:, 0:2], st[:, 2:4]), [Bst], [Bst])
                def knorm(st=st, pkv=pkv, dst=(kn if lat else ka)):
                    ins = None
                    for h in range(2):
                        ins = V_.scalar_tensor_tensor(out=dst[:, h, :], in0=pkv[:, h * 128:(h + 1) * 128], scalar=st[:, h:h + 1], in1=kw_bc[:],
                                                      op0=ALU.mult, op1=ALU.mult)
                    return ins
                K.op(dve, knorm, [Bst, Bpkv, B_kw], [Bkn if lat else Bka])
                if lat:
                    K.op(dve, lambda kn=kn, ka=ka, kb=kb, tb=tb: rope(kn, ka, kb, 128, 2, tb[:, 0:32], tb[:, 32:64], tb[:, 64:96],
                                                                       tabca[:, 0, 0:32], tabca[:, 0, 32:64], tabca[:, 0, 64:96]),
                         [Bkn, Btb, B_tabca], [Bka, Bkb])
                    K.op(dve, lambda ka=ka, kb=kb: rope_add(ka, kb, 128), [Bkb], [Bka])
                K.op(dve, lambda kh=kh, ka=ka: V_.tensor_copy(kh[:], ka[:].rearrange("p h d -> p (h d)")), [Bka], [Bkh])
                pkt, Bpkt = pktR.next()

                def trk(pkt=pkt, kh=kh):
                    T_.transpose(pkt[:, 0:128], kh[:, 0:128], ident_b[:, :])
                    return T_.transpose(pkt[:, 128:256], kh[:, 128:256], ident_b[:, :])
                K.op(pe, trk, [Bkh, B_identb], [Bpkt])
                K.op(act, lambda pkt=pkt, kc=kc: S_.copy(KT[:, :, 128 * kc:128 * (kc + 1)], pkt[:].rearrange("p (g d) -> p g d", g=2)),
                     [Bpkt], [B_KT])
            a1s = {}
            for k0 in (0, 1):
                a1s[k0], ev0 = partA1(k0)
                ev0()
            sts = {0: partA2(0, a1s.pop(0))}
            for kc in range(NKC):
                ev_n = None
                if kc + 2 < NKC:
                    a1s[kc + 2], ev_n = partA1(kc + 2)
                if kc + 1 < NKC:
                    sts[kc + 1] = partA2(kc + 1, a1s.pop(kc + 1))
                partB(kc, sts.pop(kc))
                if ev_n is not None:
                    ev_n()
            allb = [B_wkv, B_kw] + xsR.b + xhR.b + hTB[0] + hTB[1] + tbR.b + tpR.b + [b_.hi for b_ in tpR.b] + pkvR.b + pktR.b + sqR.b + knR.b + kaR.b + kbR.b + khR.b
            for e in K.engs:
                K.wait_all(e, allb)

        if lvl == 2:
            kt_dbg = nc.dram_tensor("kt_dbg", [128, 2 * NKEYS], BF16, kind="ExternalOutput").ap()
            vv_dbg = nc.dram_tensor("vv_dbg", [128, NKC * 260], BF16, kind="ExternalOutput").ap()
            K.dma(sp, kt_dbg[:, :], KT[:].rearrange("p g n -> p (g n)"), B_KT, Buf("ktd"), B_KT)
            K.dma(sp, vv_dbg[:, :], VV[:].rearrange("p c g n -> p (c g n)"), B_VV, Buf("vvd"), B_VV)
            for s_ in K.sems:
                if s_.cnt > 0:
                    nc.sync.wait_ge(s_.h, s_.cnt)
            return nc, es
        with ExitStack() as ph2:
            if lvl >= 5:
                K.dma(pool, None, None, XB, B_wupb, B_wupb, fn=cvt_up)
                K.dma(pool, None, None, XB, B_wdnb, B_wdnb, fn=cvt_dn)
            aon_bc = sb(ph2, "aon_bc", [128, 8, 128]); B_aon = Buf("aon")
            aoR = Ring(ph2, "aot", [128, 8, 128], BF16, 2)
            ld(aon_bc[:], aon_d.rearrange("p (h d) -> p h d", h=8), B_aon)
            qTR = Ring(ph2, "qTa", [128, 8, 128], BF16, 2)
            stpR = Ring(ph2, "stp", [128, 1024], F32, 2, ps=True)
            ptR = Ring(ph2, "pt", [128, 1024], BF16, 3)
            o_ps = [psum(ph2, f"ops{i}", [128, 512], F32) for i in range(3)]; B_ops = Buf("ops", excl=True)
            patR = Ring(ph2, "pat", [128, 1024], BF16, 1, ps=True)
            atR = Ring(ph2, "at", [128, 8, 128], F32, 1)
            asqR = Ring(ph2, "asq", [128, 1024], F32, 1)
            ahR = Ring(ph2, "ah", [128, 1024], BF16, 2)
            K.op(dve, lambda: V_.memset(ssa[:], 1.0), [], [B_ssa])
            SC = float(128 ** -0.5)
            pending = None

            def attn_post_pe(pp):
                (t, nq, ah, Bah) = pp
                pat, Bpat = patR.next()

                def tra():
                    ins = None
                    for h in range(8):
                        ins = T_.transpose(pat[:, h * 128:h * 128 + nq], ah[:nq, h * 128:(h + 1) * 128], ident_b[:nq, :nq])
                    return ins
                K.op(pe, tra, [Bah, B_identb], [Bpat])
                aot, Baot = aoR.next()
                K.op(dve, lambda: V_.tensor_copy(aot[:, :, :nq], pat[:].rearrange("p (h d) -> p h d", h=8)[:, :, :nq]),
                     [Bpat], [Baot])
                K.dma(sp, ao_dv[:, :, 128 * t:128 * t + nq], aot[:, :, :nq], Baot, B_AO[t], Baot)

            for t in range(NT):
                nq = tile_rows(t)
                qT, BqT = qTR.next()
                K.dma(sp, qT[:, :, :nq], q_dv[:, :, 128 * t:128 * t + nq], B_qd[t], BqT, BqT)
                def issue_qk(kc, qT=qT, BqT=BqT, nq=nq):
                    stp, Bstp = stpR.next()

                    def mm_qk():
                        ins = None
                        for g in range(2):
                            ins = T_.matmul(stp[:, g * 512:g * 512 + 4 * nq].rearrange("p (h q) -> p h q", h=4),
                                            KT[:, g, 128 * kc:128 * (kc + 1)], qT[:, 4 * g:4 * g + 4, :nq], start=True, stop=True)
                        return ins
                    K.op(pe, mm_qk, [BqT, B_KT], [Bstp])
                    pt, Bpt = ptR.next()
                    K.op(act, lambda: S_.activation(
                        out=pt[:].rearrange("p (g x) -> p g x", g=2)[:, :, 0:4 * nq],
                        in_=stp[:].rearrange("p (g x) -> p g x", g=2)[:, :, 0:4 * nq], func=AF.Exp, scale=SC), [Bstp], [Bpt])
                    return pt, Bpt
                nxt = issue_qk(0)
                for kc in range(NKC):
                    pt, Bpt = nxt
                    if kc + 1 < NKC:
                        nxt = issue_qk(kc + 1)

                    def mm_pv(pt=pt, kc=kc, nq=nq):
                        ins = None
                        for h in range(8):
                            g, hh = divmod(h, 4)
                            ins = T_.matmul(o_ps[h // 3][:nq, (h % 3) * 129:(h % 3) * 129 + 129],
                                            pt[:, g * 512 + hh * nq: g * 512 + (hh + 1) * nq], VV[:, kc, g, 0:129],
                                            start=(kc == 0), stop=(kc == NKC - 1))
                        return ins
                    K.op(pe, mm_pv, [Bpt, B_VV], [B_ops])
                    if kc == 6 and pending is not None:
                        attn_post_pe(pending)
                        pending = None
                at, Bat = atR.next(); st, Bst = stR.next(); asq, Basq = asqR.next(); ah, Bah = ahR.next()

                def ap1(st=st, nq=nq):
                    ins = None
                    for h in range(8):
                        ins = V_.reciprocal(st[:nq, h:h + 1], o_ps[h // 3][:nq, (h % 3) * 129 + 128:(h % 3) * 129 + 129])
                    return ins

                def ap2(at=at, st=st, nq=nq):
                    ins = None
                    for h in range(8):
                        ins = V_.tensor_scalar(at[:nq, h, :], o_ps[h // 3][:nq, (h % 3) * 129:(h % 3) * 129 + 128], st[:nq, h:h + 1], None, op0=ALU.mult)
                    return ins
                K.op(dve, ap1, [B_ops], [Bst])
                K.op(dve, ap2, [B_ops, Bst], [Bat])
                K.op(dve, lambda at=at, asq=asq, nq=nq: V_.tensor_tensor(out=asq[:nq], in0=at[:nq].rearrange("p h d -> p (h d)"),
                                                                         in1=at[:nq].rearrange("p h d -> p (h d)"), op=ALU.mult), [Bat], [Basq])
                K.op(dve, lambda asq=asq, nq=nq, t=t: V_.reduce_sum(out=ssa[:nq, t:t + 1], in_=asq[:nq], axis=AX.X), [Basq], [B_ssa])
                K.op(dve, lambda at=at, ah=ah, nq=nq: V_.tensor_tensor(out=ah[:nq].rearrange("p (h d) -> p h d", h=8), in0=at[:nq], in1=aon_bc[:nq], op=ALU.mult),
                     [Bat, B_aon], [Bah])
                pending = (t, nq, ah, Bah)
            attn_post_pe(pending)
            allb = [B_KT, B_VV, B_aon, B_ops] + qTR.b + stpR.b + [b_.hi for b_ in tpR.b] + ptR.b + patR.b + atR.b + asqR.b + ahR.b + B_AO + aoR.b
            for e in K.engs:
                K.wait_all(e, allb)

    if lvl == 3:
        for s_ in K.sems:
            if s_.cnt > 0:
                nc.sync.wait_ge(s_.h, s_.cnt)
        return nc, es
    B_h2d = [Buf(f"h2d{t}") for t in range(NT)]
    B_x1d = [Buf(f"x1d{t}") for t in range(NT)]
    with ExitStack() as ph:
        wo = sb(ph, "wo", [128, KC, D], BF16); B_wo = Buf("wo")
        g1_bc = sb(ph, "g1_bc", [128, D]); B_g1 = Buf("g1")
        rsa = sb(ph, "rsa", [128, NT]); rsc = sb(ph, "rsc", [128, NT]); B_rs = Buf("rs")
        xsR = Ring(ph, "xs", [128, D], F32, 2)
        x1R = Ring(ph, "x1", [128, D], F32, 2)
        t1R = Ring(ph, "t1", [128, 512], F32, 2)
        xhR = Ring(ph, "xh", [128, D], BF16, 2)
        aoR = Ring(ph, "aot4", [128, 8, 128], BF16, 2)
        zTR = Ring(ph, "zT4", [128, 8, 128], BF16, 2)
        h2R = Ring(ph, "h2t", [128, KC, 128], BF16, 2)
        h2B = [[Buf("h2a0"), Buf("h2b0")], [Buf("h2a1"), Buf("h2b1")]]
        tpR = Ring(ph, "tp", [128, D], BF16, 2, ps=True)
        pacR = Ring(ph, "pac", [128, 2, 512], F32, 2, ps=True)
        w_o_v = w_o.rearrange("(c p) n -> p c n", p=128)
        K.dma(pool, None, None, XB, B_wo, B_wo, fn=lambda: [
            G_.dma_start(out=wo[:, :, i * 512:(i + 1) * 512], in_=w_o_v[:, :, i * 512:(i + 1) * 512]) for i in range(4)])
        K.dma(sp, g1_bc[:], modrow_d[0:1, 4096:6144].to_broadcast([128, D]), B_modd, B_g1, B_g1)

        def rstd2():
            S_.activation(out=rsa[:], in_=ssa[:], func=AF.Sqrt, scale=1.0 / 1024, bias=eps_t[:, 0:1])
            return S_.activation(out=rsc[:], in_=ssc[:], func=AF.Sqrt, scale=1.0 / 1024, bias=eps_t[:, 0:1])
        K.op(act, rstd2, [B_ssa, B_ssc, B_eps], [B_rs])

        def rstd3():
            V_.reciprocal(rsa[:], rsa[:])
            return V_.reciprocal(rsc[:], rsc[:])
        K.op(dve, rstd3, [B_rs], [B_rs])
        def p4A(t):
            nq = tile_rows(t)
            xs, Bxs = xsR.next()
            ld(xs[:nq], x_ext[128 * t:128 * t + nq, :], Bxs)
            aot, Baot = aoR.next()
            K.dma(sp, aot[:, :, :nq], ao_dv[:, :, 128 * t:128 * t + nq], B_AO[t], Baot, Baot)
            zTt, BzTt = zTR.next()
            K.dma(sp, zTt[:, :, :nq], zT_dv[:, :, 128 * t:128 * t + nq], B_zTd, BzTt, BzTt)
            x1, Bx1 = x1R.next()
            for nb in range(4):
                pac, Bpac = pacR.next()

                def mm_o(pac=pac, nq=nq, nb=nb, aot=aot, zTt=zTt):
                    ins = None
                    for h in range(8):
                        ins = T_.matmul(pac[:nq, 0, :], aot[:, h, :nq], wo[:, h, nb * 512:(nb + 1) * 512],
                                        start=(h == 0), stop=(h == 7))
                    for j in range(8):
                        ins = T_.matmul(pac[:nq, 1, :], zTt[:, j, :nq], wo[:, 8 + j, nb * 512:(nb + 1) * 512],
                                        start=(j == 0), stop=(j == 7))
                    return ins
                K.op(pe, mm_o, [Baot, BzTt, B_wo], [Bpac])
                t1, Bt1 = t1R.next()
                K.op(act, lambda t1=t1, pac=pac, nq=nq, t=t: S_.activation(out=t1[:nq, :], in_=pac[:nq, 0, :], func=AF.Identity,
                                                                             scale=rsa[:nq, t:t + 1]), [Bpac, B_rs], [Bt1])

                K.op(dve, lambda t1=t1, pac=pac, nq=nq, t=t: V_.scalar_tensor_tensor(
                    out=t1[:nq, :], in0=pac[:nq, 1, :], scalar=rsc[:nq, t:t + 1], in1=t1[:nq, :], op0=ALU.mult, op1=ALU.add), [Bpac, B_rs], [Bt1])
                K.op(dve, lambda t1=t1, nq=nq, nb=nb: V_.tensor_tensor(out=t1[:nq, :], in0=t1[:nq, :], in1=g1_bc[:nq, nb * 512:(nb + 1) * 512], op=ALU.mult),
                     [B_g1], [Bt1])
                K.op(dve, lambda t1=t1, nq=nq, nb=nb, x1=x1, xs=xs: V_.tensor_tensor(
                    out=x1[:nq, nb * 512:(nb + 1) * 512], in0=t1[:nq, :], in1=xs[:nq, nb * 512:(nb + 1) * 512], op=ALU.add), [Bt1, Bxs], [Bx1])
            return (x1, Bx1, nq)

        def p4B(t, st_):
            x1, Bx1, nq = st_
            K.dma(pool, x1_d[128 * t:128 * t + nq, :], x1[:nq], Bx1, B_x1d[t], Bx1)
            h2t, Bh2t = h2R.next()
            hb = h2B[t % 2]
            norm_T(x1, Bx1, nq, a2, s2, lambda c, h2t=h2t, nq=nq: h2t[:, c, :nq], hb, xhR, tpR)
            K.wait_all(pool, hb)
            K.dma(pool, h2_dv[:, :, 128 * t:128 * t + nq], h2t[:, :, :nq], hb[0], B_h2d[t], hb[0])
            hb[1].r[hb[0].dsem] = hb[0].dsem.cnt
        st4 = p4A(0)
        for t in range(NT):
            nx4 = p4A(t + 1) if t + 1 < NT else None
            p4B(t, st4)
            st4 = nx4
        allb = [B_wo, B_g1, B_rs] + xsR.b + x1R.b + t1R.b + xhR.b + tpR.b + [b_.hi for b_ in tpR.b] + pacR.b + B_x1d + B_AO + B_h2d + aoR.b + zTR.b + h2B[0] + h2B[1]
        for e in K.engs:
            K.wait_all(e, allb)

    if lvl == 4:
        for s_ in K.sems:
            if s_.cnt > 0:
                nc.sync.wait_ge(s_.h, s_.cnt)
        return nc, es
    with ExitStack() as ph:
        actT = sb(ph, "actT", [128, NFB, 512], BF16)
        B_actT = [Buf(f"actT{j}") for j in range(NFB)]
        h2b = sb(ph, "h2b", [128, KC, 514], BF16); B_h2b = Buf("h2b")
        wuR = Ring(ph, "wu", [128, KC, 2, 256], BF16, 2)
        wdR = Ring(ph, "wd", [128, 1024], BF16, 4)
        fnw = sb(ph, "fnw", [128, D]); B_fnw = Buf("fnw")
        g2_bc = sb(ph, "g2_bc", [128, D]); B_g2 = Buf("g2")
        xoR = Ring(ph, "xo", [128, D], F32, 4)
        acR = Ring(ph, "ac", [128, 256], F32, 3)
        gcR = Ring(ph, "gc", [128, 256], F32, 3)
        sgR = Ring(ph, "sg", [128, 256], F32, 3)
        bank = [psum(ph, f"bk{i}", [128, 512], F32) for i in range(8)]
        B_bank = [Buf(f"bk{i}", excl=True) for i in range(8)]
        ld(fnw[:], fnw_d[:, :], B_fnw)
        K.dma(sp, g2_bc[:], modrow_d[0:1, 10240:12288].to_broadcast([128, D]), B_modd, B_g2, B_g2)
        B_out = Buf("outd")
        unit = 0
        for b in range(4):
            ub = 1 + 512 * b
            K.wait_all(sp, B_h2d)
            K.dma(sp, h2b[:], h2_dv[:, :, ub:ub + 514], B_h2d[0], B_h2b, B_h2b)
            for jb in range(22):
                wu, Bwu = wuR.next()
                K.dma(sp, wu[:].rearrange("p c a n -> p (c a n)"), wup_b[128 * jb:128 * (jb + 1), :], B_wupb, Bwu, Bwu)
                for jj in range(2):
                    jf = 2 * jb + jj
                    bk = [bank[4 * (unit % 2) + i] for i in range(4)]
                    Bbk = [B_bank[4 * (unit % 2) + i] for i in range(4)]
                    unit += 1

                    def mm_u(wu=wu, jj=jj, bk=bk):
                        ins = None
                        for ag in range(2):
                            for s in range(2):
                                for c in range(KC):
                                    ins = T_.matmul(bk[2 * ag + s][:, 0:258], wu[:, c, ag, jj * 128:(jj + 1) * 128],
                                                    h2b[:, c, 256 * s: 256 * s + 258], start=(c == 0), stop=(c == KC - 1))
                        return ins
                    K.op(pe, mm_u, [B_h2b, Bwu], Bbk)
                    for s in range(2):
                        ac, Bac = acR.next(); gc, Bgc = gcR.next(); sg, Bsg = sgR.next()
                        pa, pg = bk[s], bk[2 + s]
                        mask_col = None
                        if b == 0 and s == 0:
                            mask_col = (0, 0)
                        if b == 3 and s == 1:
                            mask_col = (257, 1)
                        if mask_col is not None:
                            def mk(pa=pa, pg=pg, mc=mask_col):
                                V_.tensor_scalar(pa[:, mc[0]:mc[0] + 1], pa[:, mc[0]:mc[0] + 1], hmask[:, mc[1]:mc[1] + 1], None, op0=ALU.mult)
                                return V_.tensor_scalar(pg[:, mc[0]:mc[0] + 1], pg[:, mc[0]:mc[0] + 1], hmask[:, mc[1]:mc[1] + 1], None, op0=ALU.mult)
                            K.op(dve, mk, [B_hmask], [Bbk[s], Bbk[2 + s]])

                        def a_first(ac=ac, gc=gc, pa=pa, pg=pg, jf=jf):
                            S_.activation(out=ac[:], in_=pa[:, 0:256], func=AF.Identity, scale=fconvT[:, jf, 0:1])
                            return S_.activation(out=gc[:], in_=pg[:, 0:256], func=AF.Identity, scale=fconvT[:, NFB + jf, 0:1])
                        K.op(act, a_first, [Bbk[s], Bbk[2 + s], B_cw], [Bac, Bgc])

                        for kk in (1, 2):
                            def cv2(ac=ac, gc=gc, pa=pa, pg=pg, jf=jf, kk=kk):
                                V_.scalar_tensor_tensor(out=ac[:], in0=pa[:, kk:256 + kk], scalar=fconvT[:, jf, kk:kk + 1], in1=ac[:], op0=ALU.mult, op1=ALU.add)
                                return V_.scalar_tensor_tensor(out=gc[:], in0=pg[:, kk:256 + kk], scalar=fconvT[:, NFB + jf, kk:kk + 1], in1=gc[:],
                                                               op0=ALU.mult, op1=ALU.add)
                            K.op(dve, cv2, [Bbk[s], Bbk[2 + s], B_cw], [Bac, Bgc])
                        K.op(act, lambda sg=sg, gc=gc: S_.activation(out=sg[:], in_=gc[:], func=AF.Silu), [Bgc], [Bsg])
                        K.op(dve, lambda sg=sg, ac=ac, jf=jf, s=s: V_.tensor_tensor(out=actT[:, jf, 256 * s:256 * (s + 1)], in0=sg[:], in1=ac[:], op=ALU.mult),
                             [Bsg, Bac], [B_actT[jf]])
            xos = [xoR.next() for _ in range(4)]
            for tt in range(4):
                row0 = 512 * b + 128 * tt
                xo, Bxo = xos[tt]
                K.dma(sp, xo[:], x1_d[2 + row0: 2 + row0 + 128, :], B_x1d[0], Bxo, Bxo)
            for half in range(2):
                for jf in range(NFB):
                    wd, Bwd = wdR.next()
                    K.dma(sp, wd[:], wdn_b[128 * jf:128 * (jf + 1), half * 1024:(half + 1) * 1024], B_wdnb, Bwd, Bwd)

                    def mm_d(wd=wd, jf=jf):
                        ins = None
                        for tt in range(4):
                            for nbh in range(2):
                                ins = T_.matmul(bank[2 * tt + nbh][:, :], actT[:, jf, 128 * tt:128 * (tt + 1)], wd[:, nbh * 512:(nbh + 1) * 512],
                                                start=(jf == 0), stop=(jf == NFB - 1))
                        return ins
                    K.op(pe, mm_d, [B_actT[jf], Bwd], B_bank)
                for tt in range(4):
                    row0 = 512 * b + 128 * tt
                    xo, Bxo = xos[tt]

                    def ev_d1(tt=tt, half=half):
                        ins = None
                        for nbh in range(2):
                            c0 = half * 1024 + nbh * 512
                            ins = V_.tensor_tensor(out=bank[2 * tt + nbh][:, :], in0=bank[2 * tt + nbh][:, :], in1=g2_bc[:, c0:c0 + 512], op=ALU.mult)
                        return ins

                    def ev_d2(xo=xo, tt=tt, half=half):
                        ins = None
                        for nbh in range(2):
                            c0 = half * 1024 + nbh * 512
                            ins = V_.tensor_tensor(out=xo[:, c0:c0 + 512], in0=xo[:, c0:c0 + 512], in1=bank[2 * tt + nbh][:, :], op=ALU.add)
                        return ins
                    K.op(dve, ev_d1, [B_g2], [B_bank[2 * tt], B_bank[2 * tt + 1]])
                    K.op(dve, ev_d2, [B_bank[2 * tt], B_bank[2 * tt + 1]], [Bxo])
                    if half == 1:
                        st, Bst = stR.next()
                        K.op(dve, lambda st=st: V_.memset(st[:], 0.0), [], [Bst])
                        K.op(act, lambda xo=xo, st=st: S_.activation(out=junk[:], in_=xo[:], func=AF.Square, accum_out=st[:, 0:1]), [Bxo], [B_junk, Bst])
                        K.op(act, lambda st=st: S_.activation(out=st[:, 1:2], in_=st[:, 0:1], func=AF.Sqrt, scale=1.0 / D, bias=eps_t[:, 0:1]), [Bst, B_eps], [Bst])

                        K.op(dve, lambda st=st: V_.reciprocal(st[:, 2:3], st[:, 1:2]), [Bst], [Bst])
                        K.op(dve, lambda xo=xo, st=st: V_.scalar_tensor_tensor(out=xo[:], in0=xo[:], scalar=st[:, 2:3], in1=fnw[:], op0=ALU.mult, op1=ALU.mult),
                             [Bst, B_fnw], [Bxo])
                        K.dma(pool, out_d[row0:row0 + 128, :], xo[:], Bxo, B_out, Bxo)
    for s in K.sems:
        if s.cnt > 0:
            nc.sync.wait_ge(s.h, s.cnt)
    return nc, es


def prep_inputs(inp):
    f = np.float32
    x = np.ascontiguousarray(np.asarray(inp["x"], f)[0])
    ctx = np.ascontiguousarray(np.asarray(inp["ctx"], f)[0])
    c = np.asarray(inp["c"], f)[0]
    c_ctx = np.asarray(inp["c_ctx"], f)
    w_ada = np.ascontiguousarray(np.asarray(inp["w_ada"], f)[0])
    b_ada = np.asarray(inp["b_ada"], f)[0]

    def fmaj(v, nchunk):
        return np.ascontiguousarray(v.reshape(nchunk, 128).T)
    cT = np.stack([fmaj(c, KC), fmaj(c_ctx, KC)], axis=-1).reshape(128, KC * 2)
    qw = np.asarray(inp["q_norm_w"], f)[0]
    kw = np.asarray(inp["k_norm_w"], f)[0]
    conv_w = np.asarray(inp["conv_w"], f)[0]
    fconv = np.asarray(inp["ffn_conv_w"], f)[0]
    convT = np.ascontiguousarray(conv_w.reshape(3, 8, 128).transpose(2, 1, 0)).reshape(128, 24)
    fconvT = np.ascontiguousarray(fconv.reshape(3, 88, 128).transpose(2, 1, 0)).reshape(128, 264)
    p = np.arange(128)
    posr_all = (2 * np.arange(128)[None, :] + (p[:, None] // 64)).astype(f)
    posc_all = (p[:, None] % 64).astype(f)
    common = {
        "x_all": x, "ctx_a": ctx,
        "cT": np.ascontiguousarray(cT),
        "w_ada": w_ada,
        "b_ada2": np.ascontiguousarray(np.broadcast_to(b_ada, (2, 12288))),
        "n1T": fmaj(np.asarray(inp["norm1_w"], f)[0], KC),
        "n2T": fmaj(np.asarray(inp["norm2_w"], f)[0], KC),
        "w_in": np.ascontiguousarray(np.asarray(inp["w_in"], f)[0]),
        "w_o": np.ascontiguousarray(np.asarray(inp["w_o"], f)[0]),
        "w_up": np.ascontiguousarray(np.asarray(inp["w_ffn_up"], f)[0]),
        "w_dn": np.ascontiguousarray(np.asarray(inp["w_ffn_down"], f)[0]),
        "qw_bc": np.ascontiguousarray(np.broadcast_to(qw, (128, 128))),
        "kw_bc": np.ascontiguousarray(np.broadcast_to(kw, (128, 128))),
        "convT": convT, "fconvT": fconvT,
        "aon_bc": np.ascontiguousarray(np.broadcast_to(np.asarray(inp["attn_out_norm_w"], f)[0], (128, 1024))),
        "conT": fmaj(np.asarray(inp["conv_out_norm_w"], f)[0], 8),
        "fnw_bc": np.ascontiguousarray(np.broadcast_to(np.asarray(inp["final_norm_w"], f), (128, D))),
        "jfreq": np.ascontiguousarray(np.broadcast_to(
            (np.float32(10000.0) ** (-(np.arange(32, dtype=f) / np.float32(32.0)))).astype(f), (128, 32))),
        "ident": np.eye(128, dtype=f),
        "posr_all": np.ascontiguousarray(posr_all), "posc_all": np.ascontiguousarray(posc_all),
    }
    maps = []
    for r in range(NCORES):
        t0 = r * TOWN
        xe = np.zeros((E, D), f)
        lo, hi = t0 - 2, t0 + TOWN + 2
        slo, shi = max(lo, 0), min(hi, SEQ)
        xe[slo - lo: shi - lo] = x[slo:shi]
        tok = np.arange(lo, hi)
        tokc = np.clip(tok, 0, SEQ - 1)
        prow = np.zeros(NT * 128, f); pcol = np.zeros(NT * 128, f)
        prow[:E] = tokc // 64; pcol[:E] = tokc % 64
        m = dict(common)
        m.update({
            "x_ext": xe,
            "posr": np.ascontiguousarray(prow.reshape(NT, 128).T),
            "posc": np.ascontiguousarray(pcol.reshape(NT, 128).T),
            "hmask": np.ascontiguousarray(np.broadcast_to(
                np.array([0.0 if r == 0 else 1.0, 0.0 if r == NCORES - 1 else 1.0], f), (128, 2))),
        })
        maps.append(m)
    return maps


def kernel_debug(stop, **inputs):
    nc, es = build(stop)
    maps = prep_inputs(inputs)
    names = set()
    for alloc in nc.allocations:
        try:
            if alloc.kind == "ExternalInput":
                names.add(alloc.memorylocations[0].name)
        except Exception:
            pass
    maps = [{k: v for k, v in m.items() if k in names} for m in maps]
    res = run_bass_kernel_spmd(nc, maps, core_ids=list(range(NCORES)))
    es.close()
    return res.results


def kernel(**inputs):
    nc, es = build(None)
    maps = prep_inputs(inputs)
    res = run_bass_kernel_spmd(nc, maps, core_ids=list(range(NCORES)))
    es.close()
    out = np.concatenate([r["out"] for r in res.results], axis=0)
    return out.reshape(1, SEQ, D).astype(np.float32)
```

```python
import os
import numpy as np
from contextlib import ExitStack
import concourse.bass as bass
import concourse.mybir as mybir
from concourse.bass_utils import run_bass_kernel_spmd

F32 = mybir.dt.float32
BF16 = mybir.dt.bfloat16
AF = mybir.ActivationFunctionType
ALU = mybir.AluOpType
AX = mybir.AxisListType

NCORES = 8
D = 2048
KC = 16
SEQ = 16384
TOWN = 2048
E = 2052
NT = 17
CTXC = 32
KOWN = TOWN + CTXC
NKEYS = KOWN * NCORES
NKC = NKEYS // 128
DFF = 5632
NFB = DFF // 128
EPS = 1e-6
TWO_PI = 6.283185307179586
PI = 3.141592653589793


def tile_rows(t):
    return 128 if t < 16 else 4


class Sem:
    def __init__(self, h, name):
        self.h = h
        self.name = name
        self.cnt = 0


class Buf:
    def __init__(self, name, excl=False):
        self.name = name
        self.w = {}
        self.r = {}
        self.dsem = None
        self.excl = excl


class Eng:
    def __init__(self, e, sem, name, selfwait=True):
        self.e = e
        self.sem = sem
        self.name = name
        self.waited = {}
        self.selfwait = selfwait


class Ctx:
    def __init__(self, nc, es):
        self.nc = nc
        self.es = es
        self.nsem = 0
        self.pe = Eng(nc.tensor, self.new_sem("pe"), "pe", selfwait=False)
        self.act = Eng(nc.scalar, self.new_sem("act"), "act")
        self.dve = Eng(nc.vector, self.new_sem("dve"), "dve")
        self.pool = Eng(nc.gpsimd, self.new_sem("pool"), "pool")
        self.sp = Eng(nc.sync, None, "sp")
        self.engs = [self.pe, self.act, self.dve, self.pool, self.sp]

    def new_sem(self, name):
        self.nsem += 1
        h = self.es.enter_context(self.nc.semaphore(f"s{self.nsem}_{name}"))
        s = Sem(h, name)
        if not hasattr(self, "sems"):
            self.sems = []
        self.sems.append(s)
        return s

    def _wait(self, eng, need):
        for sem, val in need.items():
            if sem is eng.sem and not eng.selfwait:
                continue
            if eng.waited.get(sem, 0) < val:
                eng.e.wait_ge(sem.h, val)
                eng.waited[sem] = val

    @staticmethod
    def _merge(dst, src):
        for s, v in src.items():
            if dst.get(s, 0) < v:
                dst[s] = v

    def op(self, eng, fn, reads=(), writes=()):
        need = {}
        for b in reads:
            self._merge(need, b.w)
            if b.excl:
                for s_, v_ in b.r.items():
                    if s_ is not eng.sem and need.get(s_, 0) < v_:
                        need[s_] = v_
        for b in writes:
            self._merge(need, b.w)
            self._merge(need, b.r)
        self._wait(eng, need)
        ins = fn()
        eng.sem.cnt += 1
        ins.then_inc(eng.sem.h, 1)
        v = eng.sem.cnt
        for b in reads:
            if b.r.get(eng.sem, 0) < v:
                b.r[eng.sem] = v
        for b in writes:
            b.w = {eng.sem: v}
            b.r = {}
        return ins

    def dma(self, q, out, in_, src, dst, owner, n=1, fn=None):
        need = {}
        self._merge(need, src.w)
        self._merge(need, dst.w)
        self._merge(need, dst.r)
        self._wait(q, need)
        if owner.dsem is None:
            owner.dsem = self.new_sem("d_" + owner.name)
        sem = owner.dsem
        if fn is None:
            q.e.dma_start(out=out, in_=in_).then_inc(sem.h, 16)
            sem.cnt += 16
        else:
            for ins in fn():
                ins.then_inc(sem.h, 16)
                sem.cnt += 16
        v = sem.cnt
        if src.r.get(sem, 0) < v:
            src.r[sem] = v
        dst.w = {sem: v}
        dst.r = {}

    def wait_all(self, eng, bufs):
        need = {}
        for b in bufs:
            self._merge(need, b.w)
            self._merge(need, b.r)
        self._wait(eng, need)


def build(stop=None):
    lvl = 5 if stop is None else stop
    nc = bass.Bass("TRN2", target_bir_lowering=False)
    es = ExitStack()
    K = Ctx(nc, es)
    pe, act, dve, pool, sp = K.pe, K.act, K.dve, K.pool, K.sp
    V_, S_, T_, G_ = nc.vector, nc.scalar, nc.tensor, nc.gpsimd

    def din(name, shape, dt=F32, need=0):
        if lvl < need:
            return None
        return nc.dram_tensor(name, shape, dt, kind="ExternalInput").ap()

    def dint(name, shape, dt, dbg_lvl=None):
        kind = "ExternalOutput" if (stop is not None and dbg_lvl is not None and stop + 0.5 >= dbg_lvl) else "Internal"
        return nc.dram_tensor(name, shape, dt, kind=kind).ap()

    x_all = din("x_all", [SEQ, D], need=2)
    x_ext = din("x_ext", [E, D], need=0.5)
    ctx_a = din("ctx_a", [256, D], need=2)
    cT_d = din("cT", [128, KC * 2])
    wada = din("w_ada", [D, 12288])
    bada = din("b_ada2", [2, 12288])
    n1T_d = din("n1T", [128, KC])
    n2T_d = din("n2T", [128, KC])
    w_in = din("w_in", [D, 4608], need=0.5)
    w_o = din("w_o", [D, D], need=4)
    w_up = din("w_up", [D, 2 * DFF], need=5)
    w_dn = din("w_dn", [DFF, D], need=5)
    qw_d = din("qw_bc", [128, 128])
    kw_d = din("kw_bc", [128, 128])
    convT_d = din("convT", [128, 24])
    fconvT_d = din("fconvT", [128, 264])
    aon_d = din("aon_bc", [128, 1024])
    conT_d = din("conT", [128, 8])
    fnw_d = din("fnw_bc", [128, D], need=5)
    posr_d = din("posr", [128, NT])
    posc_d = din("posc", [128, NT])
    posra_d = din("posr_all", [128, 128])
    posca_d = din("posc_all", [128, 1])
    jf_d = din("jfreq", [128, 32])
    hmask_d = din("hmask", [128, 2])
    ident_d = din("ident", [128, 128])
    out_d = nc.dram_tensor("out", [TOWN, D], F32, kind="ExternalOutput").ap() if stop is None else None

    tab_d = dint("tab_d", [128, 128 * 96], F32, 0)
    q_d = dint("q_d", [128, 8 * E], BF16, 1)
    zT_d = dint("zT_d", [128, 8 * E], BF16, 1)
    x1_d = dint("x1_d", [E, D], F32, 4)
    ao_d = dint("ao_d", [128, 8 * E], BF16, 3)
    h2_d = dint("h2_d", [128, KC * E], BF16, 4)
    modrow_d = dint("modrow_d", [2, 12288], F32, 0)
    wup_b = dint("wup_b", [22 * 128, 16 * 512], BF16)
    wdn_b = dint("wdn_b", [DFF, D], BF16)
    q_dv = q_d.rearrange("p (h e) -> p h e", h=8)
    zT_dv = zT_d.rearrange("p (h e) -> p h e", h=8)
    ao_dv = ao_d.rearrange("p (h e) -> p h e", h=8)
    h2_dv = h2_d.rearrange("p (c e) -> p c e", c=KC)

    uid = [0]

    def sb(st, name, shape, dt=F32):
        uid[0] += 1
        return st.enter_context(nc.sbuf_tensor(f"sb{uid[0]}_{name}", shape, dt))

    def psum(st, name, shape, dt=F32):
        uid[0] += 1
        return st.enter_context(nc.psum_tensor(f"pp{uid[0]}_{name}", shape, dt))

    class Ring:
        def __init__(self, st, name, shape, dt, n, ps=False):
            mk = psum if ps else sb
            self.t = [mk(st, f"{name}{i}", shape, dt) for i in range(n)]
            self.b = [Buf(f"{name}{i}", excl=ps) for i in range(n)]
            if ps:
                for b_ in self.b:
                    b_.hi = Buf(b_.name + "hi", excl=True)
            self.i = 0

        def next(self):
            k = self.i % len(self.t)
            self.i += 1
            return self.t[k], self.b[k]

    XB = Buf("ext")
    DBG = os.environ.get("KDUMP") is not None and stop is not None

    def ddump(name, ap, buf, shape, dt=F32):
        if not DBG:
            return
        t_ = nc.dram_tensor("dd_" + name, shape, dt, kind="ExternalOutput").ap()
        K.dma(sp, t_, ap, buf, Buf("dd" + name), buf)

    ident_f = sb(es, "ident_f", [128, 128]); B_identf = Buf("identf")
    ident_b = sb(es, "ident_b", [128, 128], BF16); B_identb = Buf("identb")
    n1T = sb(es, "n1T", [128, KC]); n2T = sb(es, "n2T", [128, KC]); B_nT = Buf("nT")
    a1 = sb(es, "a1", [128, KC]); s1 = sb(es, "s1", [128, KC])
    a1c = sb(es, "a1c", [128, KC]); s1c = sb(es, "s1c", [128, KC])
    a2 = sb(es, "a2", [128, KC]); s2 = sb(es, "s2", [128, KC]); B_mods = Buf("mods")
    eps_t = sb(es, "eps_t", [128, 1]); B_eps = Buf("eps")
    ssa = sb(es, "ssa", [128, NT]); B_ssa = Buf("ssa")
    ssc = sb(es, "ssc", [128, NT]); B_ssc = Buf("ssc")
    hmask = sb(es, "hmask", [128, 2]); B_hmask = Buf("hmask")
    convT = sb(es, "convT", [128, 8, 3]); fconvT = sb(es, "fconvT", [128, 88, 3]); conT = sb(es, "conT", [128, 8])
    B_cw = Buf("cw")
    ones_c = sb(es, "ones_c", [128, 1]); B_onesc = Buf("onesc")
    junk = sb(es, "junk", [128, D], BF16); B_junk = Buf("junk")
    stR = Ring(es, "st", [128, 16], F32, 6)

    def ld(dst, src, B):
        K.dma(sp, dst, src, XB, B, B)
    ld(ident_f[:], ident_d[:, :], B_identf)
    K.op(dve, lambda: V_.tensor_copy(ident_b[:], ident_f[:]), [B_identf], [B_identb])
    K.dma(sp, None, None, XB, B_nT, B_nT,
          fn=lambda: [nc.sync.dma_start(out=n1T[:], in_=n1T_d[:, :]), nc.sync.dma_start(out=n2T[:], in_=n2T_d[:, :])])
    ld(hmask[:], hmask_d[:, :], B_hmask)
    K.dma(sp, None, None, XB, B_cw, B_cw, fn=lambda: [
        nc.sync.dma_start(out=convT[:], in_=convT_d.rearrange("p (j k) -> p j k", k=3)),
        nc.sync.dma_start(out=fconvT[:], in_=fconvT_d.rearrange("p (j k) -> p j k", k=3)),
        nc.sync.dma_start(out=conT[:], in_=conT_d[:, :])])
    K.op(dve, lambda: V_.memset(ones_c[:], 1.0), [], [B_onesc])
    K.op(dve, lambda: V_.memset(eps_t[:], EPS), [], [B_eps])

    B_wupb = Buf("wupb"); B_wdnb = Buf("wdnb")
    wup_bv = wup_b.rearrange("(j p) (c a n) -> j p c a n", p=128, c=16, a=2)
    w_up_v = w_up.rearrange("(c p) (a j n) -> j p c a n", p=128, a=2, j=22) if w_up is not None else None

    def cvt_up():
        return [G_.dma_start(out=wup_bv[j][:, :, a, :], in_=w_up_v[j][:, :, a, :]) for j in range(22) for a in range(2)]

    def cvt_dn():
        return [G_.dma_start(out=wdn_b[i * 704:(i + 1) * 704, :], in_=w_dn[i * 704:(i + 1) * 704, :]) for i in range(8)]

    def norm_T(xs, Bx, nq, av, sv, dst, Bdst2, xhR, tpR, defer=False):
        st, Bst = stR.next()
        K.op(dve, lambda: V_.memset(st[:], 0.0), [], [Bst])
        K.op(act, lambda: S_.activation(out=junk[:nq], in_=xs[:nq], func=AF.Square, accum_out=st[:nq, 0:1]),
             [Bx], [B_junk, Bst])
        K.op(act, lambda: S_.activation(out=st[:nq, 1:2], in_=st[:nq, 0:1], func=AF.Sqrt, scale=1.0 / D, bias=eps_t[:nq, 0:1]),
             [Bst, B_eps], [Bst])
        K.op(dve, lambda: V_.reciprocal(st[:nq, 2:3], st[:nq, 1:2]), [Bst], [Bst])
        xh, Bxh = xhR.next()
        K.op(dve, lambda: V_.tensor_scalar(xh[:nq], xs[:nq], st[:nq, 2:3], None, op0=ALU.mult), [Bx, Bst], [Bxh])
        tp, Btp = tpR.next()

        def trs():
            ins = None
            for c in range(KC):
                ins = T_.transpose(tp[:, c * 128: c * 128 + nq], xh[:nq, c * 128:(c + 1) * 128], ident_b[:nq, :nq])
            return ins
        K.op(pe, trs, [Bxh, B_identb], [Btp, Btp.hi])

        def ev_act():
            ins = None
            for c in range(0, KC // 2):
                ins = S_.activation(out=dst(c), in_=tp[:, c * 128: c * 128 + nq], func=AF.Identity,
                                    scale=av[:, c:c + 1], bias=sv[:, c:c + 1])
            return ins

        def ev_dve():
            ins = None
            for c in range(KC // 2, KC):
                ins = V_.tensor_scalar(dst(c), tp[:, c * 128: c * 128 + nq], av[:, c:c + 1], sv[:, c:c + 1],
                                       op0=ALU.mult, op1=ALU.add)
            return ins
        def evac():
            K.op(act, ev_act, [Btp, B_mods], [Bdst2[0]])
            K.op(dve, ev_dve, [Btp.hi, B_mods], [Bdst2[1]])
        if defer:
            return evac
        evac()

    def rope_add(tmpA, tmpB, nq):
        return V_.tensor_tensor(out=tmpA[:nq], in0=tmpA[:nq], in1=tmpB[:nq], op=ALU.add)

    def rope(xq, tmpA, tmpB, nq, nh, cr, sr, nsr, cc_, sc_, nsc_):
        ins = None
        xv = xq[:nq].rearrange("p h (a f j) -> p h a f j", a=2, f=2)
        av = tmpA[:nq].rearrange("p h (a f j) -> p h a f j", a=2, f=2)
        bv = tmpB[:nq].rearrange("p h (a f j) -> p h a f j", a=2, f=2)
        for ax, (c_, s_, ns_) in enumerate(((cr, sr, nsr), (cc_, sc_, nsc_))):
            cb = c_.unsqueeze(1).unsqueeze(1).to_broadcast([nq, nh, 2, 32])
            V_.tensor_tensor(out=av[:, :, ax, :, :], in0=xv[:, :, ax, :, :], in1=cb, op=ALU.mult)
            sb_ = s_.unsqueeze(1).to_broadcast([nq, nh, 32])
            nsb = ns_.unsqueeze(1).to_broadcast([nq, nh, 32])
            V_.tensor_tensor(out=bv[:, :, ax, 0, :], in0=xv[:, :, ax, 1, :], in1=nsb, op=ALU.mult)
            ins = V_.tensor_tensor(out=bv[:, :, ax, 1, :], in0=xv[:, :, ax, 0, :], in1=sb_, op=ALU.mult)
        return ins

    def make_tables(st, pos, n, tab, Btab, Bpos, name):
        fr = sb(st, name + "fr", [128, 32]); jf = sb(st, name + "jf", [128, 32])
        ang = sb(st, name + "ang", [128, n, 32]); tmp = sb(st, name + "tmp", [128, n, 32]); ang2 = sb(st, name + "ang2", [128, n, 32])
        negpi = sb(st, name + "npi", [128, 1])
        Bl = Buf(name + "tl")
        K.dma(sp, jf[:], jf_d[:, :], XB, Bl, Bl)
        K.op(dve, lambda: V_.tensor_copy(fr[:], jf[:]), [Bl], [Bl])

        D1 = lambda f_: K.op(dve, f_, [Bl, Bpos], [Bl])
        D1(lambda: V_.memset(negpi[:], -PI))
        D1(lambda: V_.tensor_tensor(out=ang[:], in0=pos.unsqueeze(2).to_broadcast([128, n, 32]),
                                    in1=fr[:].unsqueeze(1).to_broadcast([128, n, 32]), op=ALU.mult))
        D1(lambda: V_.tensor_scalar(ang[:], ang[:], PI, None, op0=ALU.add))
        for m in (32, 16, 8, 4, 2, 1):
            cst = float(m * TWO_PI)
            D1(lambda cst=cst: V_.tensor_scalar(tmp[:], ang[:], cst, cst, op0=ALU.is_ge, op1=ALU.mult))
            D1(lambda: V_.tensor_tensor(out=ang[:], in0=ang[:], in1=tmp[:], op=ALU.subtract))
        D1(lambda: V_.tensor_scalar(ang2[:], ang[:], PI / 2, None, op0=ALU.add))
        D1(lambda: V_.tensor_scalar(tmp[:], ang2[:], TWO_PI, TWO_PI, op0=ALU.is_ge, op1=ALU.mult))
        D1(lambda: V_.tensor_tensor(out=ang2[:], in0=ang2[:], in1=tmp[:], op=ALU.subtract))

        def sins():
            S_.activation(out=tab[:, :, 32:64], in_=ang[:], func=AF.Sin, bias=negpi[:, 0:1])
            return S_.activation(out=tab[:, :, 0:32], in_=ang2[:], func=AF.Sin, bias=negpi[:, 0:1])
        K.op(act, sins, [Bl], [Btab])
        K.op(dve, lambda: V_.tensor_scalar(tab[:, :, 64:96], tab[:, :, 32:64], -1.0, None, op0=ALU.mult), [Btab], [Btab])

    tabca = sb(es, "tabca", [128, 1, 96]); B_tabca = Buf("tabca")
    tab_st = ExitStack()
    tabo = sb(tab_st, "tabo", [128, NT, 96]); B_tabo = Buf("tabo")
    tabc = sb(tab_st, "tabc", [128, NT, 96]); B_tabc = Buf("tabc")
    B_tabd = Buf("tabd"); B_modd = Buf("modd")
    with ExitStack() as ph:
        cT = sb(ph, "cT", [128, KC, 2]); B_cT = Buf("cT")
        sT = sb(ph, "sT", [128, KC, 2]); B_sT = Buf("sT")
        wadR = Ring(ph, "wad", [128, KC, 512], F32, 2)
        bsR = Ring(ph, "bsb", [2, 512], F32, 2)
        modrow = sb(ph, "modrow", [2, 12288]); B_modrow = Buf("modrow")
        fm = sb(ph, "fm", [128, 64, 2]); B_fm = Buf("fm")
        ones_r = sb(ph, "ones_r", [1, 128]); B_ones = Buf("ones")
        psmR = Ring(ph, "ps_mod", [2, 512], F32, 2, ps=True)
        ps_fm = psum(ph, "ps_fm", [128, 64, 2]); B_psfm = Buf("psfm", excl=True)

        ld(cT[:], cT_d.rearrange("p (c i) -> p c i", i=2), B_cT)
        K.op(act, lambda: S_.activation(out=sT[:], in_=cT[:], func=AF.Silu), [B_cT], [B_sT])
        K.op(dve, lambda: V_.memset(ones_r[:], 1.0), [], [B_ones])
        wr = wada.rearrange("(c p) n -> p c n", p=128)
        for i in range(24):
            wad, Bw = wadR.next()
            ld(wad[:], wr[:, :, i * 512:(i + 1) * 512], Bw)
            bsb, B_bsb = bsR.next()
            ld(bsb[:], bada[:, i * 512:(i + 1) * 512], B_bsb)
            psm, Bpsm = psmR.next()

            def mm_mod(wad=wad, psm=psm):
                ins = None
                for c in range(KC):
                    ins = T_.matmul(psm[0:2, :], sT[:, c, :], wad[:, c, :], start=(c == 0), stop=(c == KC - 1))
                return ins
            K.op(pe, mm_mod, [B_sT, Bw], [Bpsm])
            K.op(dve, lambda psm=psm, i=i, bsb=bsb: V_.tensor_tensor(out=modrow[0:2, i * 512:(i + 1) * 512], in0=psm[0:2, :],
                                                             in1=bsb[0:2, :], op=ALU.add),
                 [Bpsm, B_bsb], [B_modrow])
        bases = [0, 2048, 6144, 8192]

        def mm_fm():
            ins = None
            for q in range(4):
                for c in range(KC):
                    ins = T_.matmul(ps_fm[:, q * KC + c, :], modrow[0:2, bases[q] + c * 128: bases[q] + (c + 1) * 128],
                                    ident_f[0:2, 0:2], start=True, stop=True)
            return ins
        K.op(pe, mm_fm, [B_modrow, B_identf], [B_psfm])
        K.op(dve, lambda: V_.tensor_copy(fm[:], ps_fm[:]), [B_psfm], [B_fm])

        for (av, sv, nT, qsc, qsh, i) in ((a1, s1, n1T, 1, 0, 0), (a1c, s1c, n1T, 1, 0, 1), (a2, s2, n2T, 3, 2, 0)):
            K.op(dve, lambda av=av, qsc=qsc, i=i: V_.tensor_scalar(av[:], fm[:, qsc * KC:(qsc + 1) * KC, i], 1.0, None, op0=ALU.add),
                 [B_fm, B_nT], [B_mods])
            K.op(dve, lambda av=av, nT=nT: V_.tensor_tensor(out=av[:], in0=av[:], in1=nT[:], op=ALU.mult), [B_fm, B_nT], [B_mods])
            K.op(dve, lambda sv=sv, qsh=qsh, i=i: V_.tensor_copy(sv[:], fm[:, qsh * KC:(qsh + 1) * KC, i]), [B_fm, B_nT], [B_mods])
        K.dma(sp, modrow_d[:, :], modrow[:], B_modrow, B_modd, B_modrow)
        allb = [B_cT, B_sT, B_modrow, B_fm, B_ones, B_psfm, B_modd, B_mods] + wadR.b + psmR.b + bsR.b
        for e in K.engs:
            K.wait_all(e, allb)
    with ExitStack() as ph:
        posr = sb(ph, "posr", [128, NT]); posc = sb(ph, "posc", [128, NT])
        posra = sb(ph, "posra", [128, 128]); posca = sb(ph, "posca", [128, 1]); B_pos = Buf("pos")
        K.dma(sp, None, None, XB, B_pos, B_pos, fn=lambda: [
            nc.sync.dma_start(out=posr[:], in_=posr_d[:, :]), nc.sync.dma_start(out=posc[:], in_=posc_d[:, :]),
            nc.sync.dma_start(out=posra[:], in_=posra_d[:, :]), nc.sync.dma_start(out=posca[:], in_=posca_d[:, :])])
        make_tables(ph, posr[:], NT, tabo, B_tabo, B_pos, "to")
        make_tables(ph, posc[:], NT, tabc, B_tabc, B_pos, "tc")
        make_tables(ph, posca[:], 1, tabca, B_tabca, B_pos, "tca")
        taba = sb(ph, "taba", [128, 128, 96]); B_taba = Buf("taba")
        make_tables(ph, posra[:], 128, taba, B_taba, B_pos, "ta")
        K.dma(sp, tab_d[:, :], taba[:].rearrange("p n k -> p (n k)"), B_taba, B_tabd, B_taba)
        allb = [B_pos, B_taba, B_tabd, B_tabo, B_tabc, B_tabca]
        for e in K.engs:
            K.wait_all(e, allb)


    if lvl == 0:
        tab_st.close()
        for s_ in K.sems:
            if s_.cnt > 0:
                nc.sync.wait_ge(s_.h, s_.cnt)
        return nc, es
    with ExitStack() as ph:
        hT = sb(ph, "hT", [128, KC, E], BF16)
        B_hT = [[Buf(f"hTe{t}"), Buf(f"hTo{t}")] for t in range(NT)]
        zacc = sb(ph, "zacc", [128, E]); B_zacc = Buf("zacc")
        w_in_v = w_in.rearrange("(c p) n -> p c n", p=128)
        K.op(dve, lambda: V_.memset(zacc[:], 0.0), [], [B_zacc])
        B_qd = [Buf(f"qd{t}") for t in range(NT)]
        with ExitStack() as ph2:
            wq = sb(ph2, "wq", [128, KC, 1024], BF16); B_wq = Buf("wq")
            qw_bc = sb(ph2, "qw_bc", [128, 128]); B_qw = Buf("qw")
            xsR = Ring(ph2, "xs", [128, D], F32, 2)
            xhR = Ring(ph2, "xh", [128, D], BF16, 2)
            ld(qw_bc[:], qw_d[:, :], B_qw)
            K.dma(pool, None, None, XB, B_wq, B_wq, fn=lambda: [
                G_.dma_start(out=wq[:, :, i * 512:(i + 1) * 512], in_=w_in_v[:, :, i * 512:(i + 1) * 512]) for i in range(2)])
            tpR = Ring(ph2, "tp", [128, D], BF16, 1, ps=True)
            psqR = Ring(ph2, "psq", [128, 1024], F32, 2, ps=True)
            pstR = Ring(ph2, "pst", [128, 1024], BF16, 1, ps=True)
            sqR = Ring(ph2, "sq", [128, 1024], F32, 1)
            qnR = Ring(ph2, "qn", [128, 8, 128], F32, 1)
            qaR = Ring(ph2, "qa", [128, 8, 128], F32, 1)
            qbR = Ring(ph2, "qb", [128, 8, 128], F32, 1)
            qhR = Ring(ph2, "qh", [128, 1024], BF16, 1)
            qTR = Ring(ph2, "qT", [128, 8, 128], BF16, 2)
            def p1A(t):
                nq = tile_rows(t)
                xs, Bxs = xsR.next()
                ld(xs[:nq], x_ext[128 * t: 128 * t + nq, :], Bxs)
                norm_T(xs, Bxs, nq, a1, s1, lambda c, t=t, nq=nq: hT[:, c, 128 * t: 128 * t + nq], B_hT[t], xhR, tpR)
                psq, Bpsq = psqR.next()

                def mm_q(t=t, nq=nq, psq=psq):
                    ins = None
                    for nb in range(2):
                        for c in range(KC):
                            ins = T_.matmul(psq[:nq, nb * 512:(nb + 1) * 512], hT[:, c, 128 * t: 128 * t + nq],
                                            wq[:, c, nb * 512:(nb + 1) * 512], start=(c == 0), stop=(c == KC - 1))
                    return ins
                K.op(pe, mm_q, B_hT[t] + [B_wq], [Bpsq])
                return (psq, Bpsq, nq)

            def p1B(t, st_):
                psq, Bpsq, nq = st_
                sq, Bsq = sqR.next(); st, Bst = stR.next()
                K.op(act, lambda sq=sq, psq=psq, nq=nq: S_.activation(out=sq[:nq], in_=psq[:nq], func=AF.Square), [Bpsq], [Bsq])
                K.op(dve, lambda sq=sq, st=st, nq=nq: V_.reduce_sum(out=st[:nq, 0:8], in_=sq[:nq].rearrange("p (h d) -> p h d", h=8), axis=AX.X),
                     [Bsq], [Bst])
                K.op(act, lambda st=st, nq=nq: S_.activation(out=st[:nq, 8:16], in_=st[:nq, 0:8], func=AF.Sqrt, scale=1.0 / 128, bias=eps_t[:nq, 0:1]),
                     [Bst], [Bst])
                qn, Bqn = qnR.next(); qa, Bqa = qaR.next(); qb, Bqb = qbR.next()

                K.op(dve, lambda st=st, nq=nq: V_.reciprocal(st[:nq, 0:8], st[:nq, 8:16]), [Bst], [Bst])
                K.op(dve, lambda st=st, nq=nq, psq=psq, qn=qn: V_.tensor_tensor(
                    out=qn[:nq], in0=psq[:nq].rearrange("p (h d) -> p h d", h=8),
                    in1=st[:nq, 0:8].unsqueeze(2).to_broadcast([nq, 8, 128]), op=ALU.mult), [Bst, Bpsq], [Bqn])
                K.op(dve, lambda nq=nq, qn=qn: V_.tensor_tensor(out=qn[:nq], in0=qn[:nq], in1=qw_bc[:nq].unsqueeze(1).to_broadcast([nq, 8, 128]),
                                                               op=ALU.mult), [B_qw], [Bqn])
                K.op(dve, lambda nq=nq, qn=qn, qa=qa, qb=qb, t=t: rope(
                    qn, qa, qb, nq, 8, tabo[:nq, t, 0:32], tabo[:nq, t, 32:64], tabo[:nq, t, 64:96],
                    tabc[:nq, t, 0:32], tabc[:nq, t, 32:64], tabc[:nq, t, 64:96]), [Bqn, B_tabo, B_tabc], [Bqa, Bqb])
                K.op(dve, lambda nq=nq, qa=qa, qb=qb: rope_add(qa, qb, nq), [Bqb], [Bqa])
                qh, Bqh = qhR.next()
                K.op(act, lambda qh=qh, qa=qa, nq=nq: S_.copy(qh[:nq], qa[:nq].rearrange("p h d -> p (h d)")), [Bqa], [Bqh])
                pst, Bpst = pstR.next()

                def trq(pst=pst, qh=qh, nq=nq):
                    ins = None
                    for h in range(8):
                        ins = T_.transpose(pst[:, h * 128: h * 128 + nq], qh[:nq, h * 128:(h + 1) * 128], ident_b[:nq, :nq])
                    return ins
                K.op(pe, trq, [Bqh, B_identb], [Bpst])
                qT, BqT = qTR.next()
                K.op(dve, lambda qT=qT, pst=pst, nq=nq: V_.tensor_copy(qT[:, :, :nq], pst[:].rearrange("p (h d) -> p h d", h=8)[:, :, :nq]),
                     [Bpst], [BqT])
                K.dma(sp, q_dv[:, :, 128 * t: 128 * t + nq], qT[:, :, :nq], BqT, B_qd[t], BqT)
            st1 = p1A(0)
            for t in range(NT):
                nx1 = p1A(t + 1) if t + 1 < NT else None
                p1B(t, st1)
                st1 = nx1
            for e in K.engs:
                K.wait_all(e, tpR.b + [b_.hi for b_ in tpR.b] + psqR.b + pstR.b + [B_wq, B_qw] + xsR.b + xhR.b + sqR.b + qnR.b + qaR.b + qbR.b + qhR.b + qTR.b + B_qd)
        if lvl >= 1:
            with ExitStack() as ph2:
                pcR = Ring(ph2, "pc", [128, 3, 512], F32, 2, ps=True)
                uR = Ring(ph2, "u", [128, 512], F32, 2)
                ucR = Ring(ph2, "uc", [128, 512], F32, 2)
                yR = Ring(ph2, "y", [128, 512], F32, 2)
                zR = Ring(ph2, "z", [128, 512], F32, 2)
                zsR = Ring(ph2, "zs", [128, 512], F32, 2)
                zbR = Ring(ph2, "zb", [128, 512], BF16, 2)
                wcR = Ring(ph2, "wc", [128, KC, 3, 128], BF16, 2)
                zz = sb(ph2, "zz", [128, 8, 1], BF16); B_zz = Buf("zz")
                B_zTd = Buf("zTd")
                K.op(dve, lambda: V_.memset(zz[:], 0.0), [], [B_zz])
                K.dma(sp, None, None, B_zz, B_zTd, B_zz, fn=lambda: [
                    nc.sync.dma_start(out=zT_dv[:, :, 0:1], in_=zz[:], allow_slow_non_contiguous=True),
                    nc.sync.dma_start(out=zT_dv[:, :, 2051:2052], in_=zz[:], allow_slow_non_contiguous=True)])
                ps_ss = psum(ph2, "ps_ss", [128, NT]); B_psss = Buf("psss", excl=True)
                allhT = [b for pr in B_hT for b in pr]
                for j in range(8):
                    wc, Bwc = wcR.next()
                    K.dma(pool, None, None, XB, Bwc, Bwc, fn=lambda wc=wc, j=j: [
                        G_.dma_start(out=wc[:, :, k, :], in_=w_in_v[:, :, 1536 + 1024 * k + 128 * j: 1536 + 1024 * k + 128 * (j + 1)])
                        for k in range(3)])
                    for s in range(5):
                        c0 = 510 * s
                        N = 512 if s < 4 else 12
                        pc, Bpc = pcR.next()

                        def mm_c(pc=pc, wc=wc, c0=c0, N=N):
                            ins = None
                            for k in range(3):
                                for c in range(KC):
                                    ins = T_.matmul(pc[:, k, :N], wc[:, c, k, :], hT[:, c, c0:c0 + N], start=(c == 0), stop=(c == KC - 1))
                            return ins
                        K.op(pe, mm_c, allhT + [Bwc], [Bpc])
                        uc, Buc = ucR.next(); u, Bu = uR.next(); y, By = yR.next(); z, Bz = zR.next(); zs, Bzs = zsR.next()
                        zb, Bzb = zbR.next()
                        K.op(act, lambda uc=uc, pc=pc, N=N: S_.copy(uc[:, :N], pc[:, 1, :N]), [Bpc], [Buc])

                        M = N - 2
                        K.op(dve, lambda u=u, uc=uc, pc=pc, N=N: V_.tensor_tensor(out=u[:, :N], in0=uc[:, :N], in1=pc[:, 2, :N], op=ALU.mult),
                             [Buc, Bpc], [Bu])
                        if c0 <= 1 < c0 + N:
                            K.op(dve, lambda u=u, c0=c0: V_.tensor_scalar(u[:, 1 - c0:2 - c0], u[:, 1 - c0:2 - c0], hmask[:, 0:1], None, op0=ALU.mult),
                                 [B_hmask], [Bu])
                        if c0 <= 2050 < c0 + N:
                            K.op(dve, lambda u=u, c0=c0: V_.tensor_scalar(u[:, 2050 - c0:2051 - c0], u[:, 2050 - c0:2051 - c0], hmask[:, 1:2], None, op0=ALU.mult),
                                 [B_hmask], [Bu])
                        K.op(dve, lambda u=u, y=y, M=M, j=j: V_.tensor_scalar(y[:, :M], u[:, 0:M], convT[:, j, 0:1], None, op0=ALU.mult), [Bu, B_cw], [By])
                        K.op(dve, lambda u=u, y=y, M=M, j=j: V_.scalar_tensor_tensor(out=y[:, :M], in0=u[:, 1:M + 1], scalar=convT[:, j, 1:2], in1=y[:, :M],
                                                                                     op0=ALU.mult, op1=ALU.add), [Bu, B_cw], [By])
                        K.op(dve, lambda u=u, y=y, M=M, j=j: V_.scalar_tensor_tensor(out=y[:, :M], in0=u[:, 2:M + 2], scalar=convT[:, j, 2:3], in1=y[:, :M],
                                                                                     op0=ALU.mult, op1=ALU.add), [Bu, B_cw], [By])
                        K.op(dve, lambda y=y, z=z, pc=pc, M=M: V_.tensor_tensor(out=z[:, :M], in0=y[:, :M], in1=pc[:, 0, 1:M + 1], op=ALU.mult), [By, Bpc], [Bz])
                        K.op(dve, lambda z=z, zs=zs, M=M: V_.tensor_tensor(out=zs[:, :M], in0=z[:, :M], in1=z[:, :M], op=ALU.mult), [Bz], [Bzs])
                        K.op(dve, lambda zs=zs, M=M, c0=c0: V_.tensor_tensor(out=zacc[:, c0 + 1:c0 + 1 + M], in0=zacc[:, c0 + 1:c0 + 1 + M], in1=zs[:, :M], op=ALU.add),
                             [Bzs], [B_zacc])
                        K.op(dve, lambda z=z, zb=zb, M=M, j=j: V_.tensor_scalar(zb[:, :M], z[:, :M], conT[:, j:j + 1], None, op0=ALU.mult), [Bz, B_cw], [Bzb])
                        K.dma(sp, zT_dv[:, j, c0 + 1:c0 + N - 1], zb[:, :N - 2], Bzb, B_zTd, Bzb)

                def mm_ss():
                    ins = None
                    for t in range(NT):
                        nq = tile_rows(t)
                        ins = T_.matmul(ps_ss[:nq, t:t + 1], zacc[:, 128 * t:128 * t + nq], ones_c[:, 0:1], start=True, stop=True)
                    return ins
                K.op(dve, lambda: V_.memset(ssc[:], 1.0), [], [B_ssc])
                K.op(pe, mm_ss, [B_zacc, B_onesc], [B_psss])

                def cp_ss():
                    V_.tensor_copy(ssc[:, 0:16], ps_ss[:, 0:16])
                    return V_.tensor_copy(ssc[0:4, 16:17], ps_ss[0:4, 16:17])
                K.op(dve, cp_ss, [B_psss], [B_ssc])
                allb = pcR.b + uR.b + ucR.b + yR.b + zR.b + zsR.b + zbR.b + [B_psss, B_zacc, B_zTd, B_zz] + allhT + wcR.b
                for e in K.engs:
                    K.wait_all(e, allb)

    tab_st.close()
    if lvl <= 1:
        ssc_dbg = nc.dram_tensor("ssc_dbg", [128, NT], F32, kind="ExternalOutput").ap()
        K.dma(sp, ssc_dbg[:, :], ssc[:], B_ssc, Buf("sscd"), B_ssc)
        for s_ in K.sems:
            if s_.cnt > 0:
                nc.sync.wait_ge(s_.h, s_.cnt)
        return nc, es
    B_AO = [Buf(f"AO{t}") for t in range(NT)]
    with ExitStack() as ph:
        KT = sb(ph, "KT", [128, 2, NKEYS], BF16); B_KT = Buf("KT")
        VV = sb(ph, "VV", [128, NKC, 2, 130], BF16); B_VV = Buf("VV")
        K.op(dve, lambda: V_.memset(VV[:, :, :, 128:130], 1.0), [], [B_VV])
        with ExitStack() as ph2:
            wkv = sb(ph2, "wkv", [128, KC, 512], BF16); B_wkv = Buf("wkv")
            kw_bc = sb(ph2, "kw_bc", [128, 128]); B_kw = Buf("kw")
            xsR = Ring(ph2, "xs", [128, D], F32, 2)
            xhR = Ring(ph2, "xh", [128, D], BF16, 2)
            hTR = Ring(ph2, "hTt", [128, KC, 128], BF16, 2)
            hTB = [[Buf("hTa0"), Buf("hTb0")], [Buf("hTa1"), Buf("hTb1")]]
            tbR = Ring(ph2, "tb", [128, 96], F32, 3)
            tpR = Ring(ph2, "tp", [128, D], BF16, 2, ps=True)
            pkvR = Ring(ph2, "pkv", [128, 512], F32, 3, ps=True)
            pktR = Ring(ph2, "pkt", [128, 256], BF16, 1, ps=True)
            sqR = Ring(ph2, "sq", [128, 256], F32, 1)
            knR = Ring(ph2, "kn", [128, 2, 128], F32, 1)
            kaR = Ring(ph2, "ka", [128, 2, 128], F32, 1)
            kbR = Ring(ph2, "kb", [128, 2, 128], F32, 1)
            khR = Ring(ph2, "kh", [128, 256], BF16, 1)
            ld(kw_bc[:], kw_d[:, :], B_kw)
            w_in_v = w_in.rearrange("(c p) n -> p c n", p=128)
            K.dma(pool, wkv[:], w_in_v[:, :, 1024:1536], XB, B_wkv, B_wkv, fn=lambda: [G_.dma_start(out=wkv[:], in_=w_in_v[:, :, 1024:1536])])
            tab_dv = tab_d.rearrange("p (n k) -> p n k", k=96)
            def partA1(kc):
                lat = kc < 128
                xs, Bxs = xsR.next()
                if lat:
                    ld(xs[:], x_all[128 * kc:128 * (kc + 1), :], Bxs)
                    tb, Btb = tbR.next()
                    K.dma(sp, tb[:], tab_dv[:, kc, :], B_tabd, Btb, Btb)
                else:
                    ld(xs[:], ctx_a[128 * (kc - 128):128 * (kc - 127), :], Bxs)
                hTt, BhTt = hTR.next()
                hb = hTB[kc % 2]
                ev_ = norm_T(xs, Bxs, 128, a1 if lat else a1c, s1 if lat else s1c, lambda c, hTt=hTt: hTt[:, c, :], hb, xhR, tpR, defer=True)
                return (hTt, hb, (tb if lat else None), (Btb if lat else None), lat), ev_

            def partA2(kc, a1_):
                hTt, hb, tb, Btb, lat = a1_
                pkv, Bpkv = pkvR.next()

                def mm_kv(pkv=pkv, hTt=hTt):
                    ins = None
                    for c in range(KC):
                        ins = T_.matmul(pkv[:, :], hTt[:, c, :], wkv[:, c, :], start=(c == 0), stop=(c == KC - 1))
                    return ins
                K.op(pe, mm_kv, hb + [B_wkv], [Bpkv])
                return (pkv, Bpkv, tb, Btb, lat)

            def partB(kc, stt_):
                pkv, Bpkv, tb, Btb, lat = stt_
                sq, Bsq = sqR.next(); st, Bst = stR.next()

                K.op(dve, lambda st=st: V_.memset(st[:], 0.0), [], [Bst])

                def a_post(pkv=pkv, sq=sq, kc=kc, st=st):
                    S_.copy(VV[:, kc, :, 0:128], pkv[:, 256:512].rearrange("p (g d) -> p g d", g=2))
                    S_.activation(out=sq[:, 0:128], in_=pkv[:, 0:128], func=AF.Square, accum_out=st[:, 0:1])
                    return S_.activation(out=sq[:, 128:256], in_=pkv[:, 128:256], func=AF.Square, accum_out=st[:, 1:2])
                K.op(act, a_post, [Bpkv], [Bsq, B_VV, Bst])
                K.op(act, lambda st=st: S_.activation(out=st[:, 2:4], in_=st[:, 0:2], func=AF.Sqrt, scale=1.0 / 128, bias=eps_t[:, 0:1]), [Bst], [Bst])
                kn, Bkn = knR.next(); ka, Bka = kaR.next(); kb, Bkb = kbR.next(); kh, Bkh = khR.next()

                K.op(dve, lambda st=st: V_.reciprocal(st[:, 0:2], st[:, 2:4]), [Bst], [Bst])
                def knorm(st=st, pkv=pkv, dst=(kn if lat else ka)):
                    ins = None
                    for h in range(2):
                        ins = V_.scalar_tensor_tensor(out=dst[:, h, :], in0=pkv[:, h * 128:(h + 1) * 128], scalar=st[:, h:h + 1], in1=kw_bc[:],
                                                      op0=ALU.mult, op1=ALU.mult)
                    return ins
                K.op(dve, knorm, [Bst, Bpkv, B_kw], [Bkn if lat else Bka])
                if lat:
                    K.op(dve, lambda kn=kn, ka=ka, kb=kb, tb=tb: rope(kn, ka, kb, 128, 2, tb[:, 0:32], tb[:, 32:64], tb[:, 64:96],
                                                                       tabca[:, 0, 0:32], tabca[:, 0, 32:64], tabca[:, 0, 64:96]),
                         [Bkn, Btb, B_tabca], [Bka, Bkb])
                    K.op(dve, lambda ka=ka, kb=kb: rope_add(ka, kb, 128), [Bkb], [Bka])
                K.op(dve, lambda kh=kh, ka=ka: V_.tensor_copy(kh[:], ka[:].rearrange("p h d -> p (h d)")), [Bka], [Bkh])
                pkt, Bpkt = pktR.next()

                def trk(pkt=pkt, kh=kh):
                    T_.transpose(pkt[:, 0:128], kh[:, 0:128], ident_b[:, :])
                    return T_.transpose(pkt[:, 128:256], kh[:, 128:256], ident_b[:, :])
                K.op(pe, trk, [Bkh, B_identb], [Bpkt])
                K.op(act, lambda pkt=pkt, kc=kc: S_.copy(KT[:, :, 128 * kc:128 * (kc + 1)], pkt[:].rearrange("p (g d) -> p g d", g=2)),
                     [Bpkt], [B_KT])
            a1s = {}
            for k0 in (0, 1):
                a1s[k0], ev0 = partA1(k0)
                ev0()
            sts = {0: partA2(0, a1s.pop(0))}
            for kc in range(NKC):
                ev_n = None
                if kc + 2 < NKC:
                    a1s[kc + 2], ev_n = partA1(kc + 2)
                if kc + 1 < NKC:
                    sts[kc + 1] = partA2(kc + 1, a1s.pop(kc + 1))
                partB(kc, sts.pop(kc))
                if ev_n is not None:
                    ev_n()
            allb = [B_wkv, B_kw] + xsR.b + xhR.b + hTB[0] + hTB[1] + tbR.b + tpR.b + [b_.hi for b_ in tpR.b] + pkvR.b + pktR.b + sqR.b + knR.b + kaR.b + kbR.b + khR.b
            for e in K.engs:
                K.wait_all(e, allb)

        if lvl == 2:
            kt_dbg = nc.dram_tensor("kt_dbg", [128, 2 * NKEYS], BF16, kind="ExternalOutput").ap()
            vv_dbg = nc.dram_tensor("vv_dbg", [128, NKC * 260], BF16, kind="ExternalOutput").ap()
            K.dma(sp, kt_dbg[:, :], KT[:].rearrange("p g n -> p (g n)"), B_KT, Buf("ktd"), B_KT)
            K.dma(sp, vv_dbg[:, :], VV[:].rearrange("p c g n -> p (c g n)"), B_VV, Buf("vvd"), B_VV)
            for s_ in K.sems:
                if s_.cnt > 0:
                    nc.sync.wait_ge(s_.h, s_.cnt)
            return nc, es
        with ExitStack() as ph2:
            if lvl >= 5:
                K.dma(pool, None, None, XB, B_wupb, B_wupb, fn=cvt_up)
                K.dma(pool, None, None, XB, B_wdnb, B_wdnb, fn=cvt_dn)
            aon_bc = sb(ph2, "aon_bc", [128, 8, 128]); B_aon = Buf("aon")
            aoR = Ring(ph2, "aot", [128, 8, 128], BF16, 2)
            ld(aon_bc[:], aon_d.rearrange("p (h d) -> p h d", h=8), B_aon)
            qTR = Ring(ph2, "qTa", [128, 8, 128], BF16, 2)
            stpR = Ring(ph2, "stp", [128, 1024], F32, 2, ps=True)
            ptR = Ring(ph2, "pt", [128, 1024], BF16, 3)
            o_ps = [psum(ph2, f"ops{i}", [128, 512], F32) for i in range(3)]; B_ops = Buf("ops", excl=True)
            patR = Ring(ph2, "pat", [128, 1024], BF16, 1, ps=True)
            atR = Ring(ph2, "at", [128, 8, 128], F32, 1)
            asqR = Ring(ph2, "asq", [128, 1024], F32, 1)
            ahR = Ring(ph2, "ah", [128, 1024], BF16, 2)
            K.op(dve, lambda: V_.memset(ssa[:], 1.0), [], [B_ssa])
            SC = float(128 ** -0.5)
            pending = None

            def attn_post_pe(pp):
                (t, nq, ah, Bah) = pp
                pat, Bpat = patR.next()

                def tra():
                    ins = None
                    for h in range(8):
                        ins = T_.transpose(pat[:, h * 128:h * 128 + nq], ah[:nq, h * 128:(h + 1) * 128], ident_b[:nq, :nq])
                    return ins
                K.op(pe, tra, [Bah, B_identb], [Bpat])
                aot, Baot = aoR.next()
                K.op(dve, lambda: V_.tensor_copy(aot[:, :, :nq], pat[:].rearrange("p (h d) -> p h d", h=8)[:, :, :nq]),
                     [Bpat], [Baot])
                K.dma(sp, ao_dv[:, :, 128 * t:128 * t + nq], aot[:, :, :nq], Baot, B_AO[t], Baot)

            for t in range(NT):
                nq = tile_rows(t)
                qT, BqT = qTR.next()
                K.dma(sp, qT[:, :, :nq], q_dv[:, :, 128 * t:128 * t + nq], B_qd[t], BqT, BqT)
                def issue_qk(kc, qT=qT, BqT=BqT, nq=nq):
                    stp, Bstp = stpR.next()

                    def mm_qk():
                        ins = None
                        for g in range(2):
                            ins = T_.matmul(stp[:, g * 512:g * 512 + 4 * nq].rearrange("p (h q) -> p h q", h=4),
                                            KT[:, g, 128 * kc:128 * (kc + 1)], qT[:, 4 * g:4 * g + 4, :nq], start=True, stop=True)
                        return ins
                    K.op(pe, mm_qk, [BqT, B_KT], [Bstp])
                    pt, Bpt = ptR.next()
                    K.op(act, lambda: S_.activation(
                        out=pt[:].rearrange("p (g x) -> p g x", g=2)[:, :, 0:4 * nq],
                        in_=stp[:].rearrange("p (g x) -> p g x", g=2)[:, :, 0:4 * nq], func=AF.Exp, scale=SC), [Bstp], [Bpt])
                    return pt, Bpt
                nxt = issue_qk(0)
                for kc in range(NKC):
                    pt, Bpt = nxt
                    if kc + 1 < NKC:
                        nxt = issue_qk(kc + 1)

                    def mm_pv(pt=pt, kc=kc, nq=nq):
                        ins = None
                        for h in range(8):
                            g, hh = divmod(h, 4)
                            ins = T_.matmul(o_ps[h // 3][:nq, (h % 3) * 129:(h % 3) * 129 + 129],
                                            pt[:, g * 512 + hh * nq: g * 512 + (hh + 1) * nq], VV[:, kc, g, 0:129],
                                            start=(kc == 0), stop=(kc == NKC - 1))
                        return ins
                    K.op(pe, mm_pv, [Bpt, B_VV], [B_ops])
                    if kc == 6 and pending is not None:
                        attn_post_pe(pending)
                        pending = None
                at, Bat = atR.next(); st, Bst = stR.next(); asq, Basq = asqR.next(); ah, Bah = ahR.next()

                def ap1(st=st, nq=nq):
                    ins = None
                    for h in range(8):
                        ins = V_.reciprocal(st[:nq, h:h + 1], o_ps[h // 3][:nq, (h % 3) * 129 + 128:(h % 3) * 129 + 129])
                    return ins

                def ap2(at=at, st=st, nq=nq):
                    ins = None
                    for h in range(8):
                        ins = V_.tensor_scalar(at[:nq, h, :], o_ps[h // 3][:nq, (h % 3) * 129:(h % 3) * 129 + 128], st[:nq, h:h + 1], None, op0=ALU.mult)
                    return ins
                K.op(dve, ap1, [B_ops], [Bst])
                K.op(dve, ap2, [B_ops, Bst], [Bat])
                K.op(dve, lambda at=at, asq=asq, nq=nq: V_.tensor_tensor(out=asq[:nq], in0=at[:nq].rearrange("p h d -> p (h d)"),
                                                                         in1=at[:nq].rearrange("p h d -> p (h d)"), op=ALU.mult), [Bat], [Basq])
                K.op(dve, lambda asq=asq, nq=nq, t=t: V_.reduce_sum(out=ssa[:nq, t:t + 1], in_=asq[:nq], axis=AX.X), [Basq], [B_ssa])
                K.op(dve, lambda at=at, ah=ah, nq=nq: V_.tensor_tensor(out=ah[:nq].rearrange("p (h d) -> p h d", h=8), in0=at[:nq], in1=aon_bc[:nq], op=ALU.mult),
                     [Bat, B_aon], [Bah])
                pending = (t, nq, ah, Bah)
            attn_post_pe(pending)
            allb = [B_KT, B_VV, B_aon, B_ops] + qTR.b + stpR.b + [b_.hi for b_ in tpR.b] + ptR.b + patR.b + atR.b + asqR.b + ahR.b + B_AO + aoR.b
            for e in K.engs:
                K.wait_all(e, allb)

    if lvl == 3:
        for s_ in K.sems:
            if s_.cnt > 0:
                nc.sync.wait_ge(s_.h, s_.cnt)
        return nc, es
    B_h2d = [Buf(f"h2d{t}") for t in range(NT)]
    B_x1d = [Buf(f"x1d{t}") for t in range(NT)]
    with ExitStack() as ph:
        wo = sb(ph, "wo", [128, KC, D], BF16); B_wo = Buf("wo")
        g1_bc = sb(ph, "g1_bc", [128, D]); B_g1 = Buf("g1")
        rsa = sb(ph, "rsa", [128, NT]); rsc = sb(ph, "rsc", [128, NT]); B_rs = Buf("rs")
        xsR = Ring(ph, "xs", [128, D], F32, 2)
        x1R = Ring(ph, "x1", [128, D], F32, 2)
        t1R = Ring(ph, "t1", [128, 512], F32, 2)
        xhR = Ring(ph, "xh", [128, D], BF16, 2)
        aoR = Ring(ph, "aot4", [128, 8, 128], BF16, 2)
        zTR = Ring(ph, "zT4", [128, 8, 128], BF16, 2)
        h2R = Ring(ph, "h2t", [128, KC, 128], BF16, 2)
        h2B = [[Buf("h2a0"), Buf("h2b0")], [Buf("h2a1"), Buf("h2b1")]]
        tpR = Ring(ph, "tp", [128, D], BF16, 2, ps=True)
        pacR = Ring(ph, "pac", [128, 2, 512], F32, 2, ps=True)
        w_o_v = w_o.rearrange("(c p) n -> p c n", p=128)
        K.dma(pool, None, None, XB, B_wo, B_wo, fn=lambda: [
            G_.dma_start(out=wo[:, :, i * 512:(i + 1) * 512], in_=w_o_v[:, :, i * 512:(i + 1) * 512]) for i in range(4)])
        K.dma(sp, g1_bc[:], modrow_d[0:1, 4096:6144].to_broadcast([128, D]), B_modd, B_g1, B_g1)

        def rstd2():
            S_.activation(out=rsa[:], in_=ssa[:], func=AF.Sqrt, scale=1.0 / 1024, bias=eps_t[:, 0:1])
            return S_.activation(out=rsc[:], in_=ssc[:], func=AF.Sqrt, scale=1.0 / 1024, bias=eps_t[:, 0:1])
        K.op(act, rstd2, [B_ssa, B_ssc, B_eps], [B_rs])

        def rstd3():
            V_.reciprocal(rsa[:], rsa[:])
            return V_.reciprocal(rsc[:], rsc[:])
        K.op(dve, rstd3, [B_rs], [B_rs])
        def p4A(t):
            nq = tile_rows(t)
            xs, Bxs = xsR.next()
            ld(xs[:nq], x_ext[128 * t:128 * t + nq, :], Bxs)
            aot, Baot = aoR.next()
            K.dma(sp, aot[:, :, :nq], ao_dv[:, :, 128 * t:128 * t + nq], B_AO[t], Baot, Baot)
            zTt, BzTt = zTR.next()
            K.dma(sp, zTt[:, :, :nq], zT_dv[:, :, 128 * t:128 * t + nq], B_zTd, BzTt, BzTt)
            x1, Bx1 = x1R.next()
            for nb in range(4):
                pac, Bpac = pacR.next()

                def mm_o(pac=pac, nq=nq, nb=nb, aot=aot, zTt=zTt):
                    ins = None
                    for h in range(8):
                        ins = T_.matmul(pac[:nq, 0, :], aot[:, h, :nq], wo[:, h, nb * 512:(nb + 1) * 512],
                                        start=(h == 0), stop=(h == 7))
                    for j in range(8):
                        ins = T_.matmul(pac[:nq, 1, :], zTt[:, j, :nq], wo[:, 8 + j, nb * 512:(nb + 1) * 512],
                                        start=(j == 0), stop=(j == 7))
                    return ins
                K.op(pe, mm_o, [Baot, BzTt, B_wo], [Bpac])
                t1, Bt1 = t1R.next()
                K.op(act, lambda t1=t1, pac=pac, nq=nq, t=t: S_.activation(out=t1[:nq, :], in_=pac[:nq, 0, :], func=AF.Identity,
                                                                             scale=rsa[:nq, t:t + 1]), [Bpac, B_rs], [Bt1])

                K.op(dve, lambda t1=t1, pac=pac, nq=nq, t=t: V_.scalar_tensor_tensor(
                    out=t1[:nq, :], in0=pac[:nq, 1, :], scalar=rsc[:nq, t:t + 1], in1=t1[:nq, :], op0=ALU.mult, op1=ALU.add), [Bpac, B_rs], [Bt1])
                K.op(dve, lambda t1=t1, nq=nq, nb=nb: V_.tensor_tensor(out=t1[:nq, :], in0=t1[:nq, :], in1=g1_bc[:nq, nb * 512:(nb + 1) * 512], op=ALU.mult),
                     [B_g1], [Bt1])
                K.op(dve, lambda t1=t1, nq=nq, nb=nb, x1=x1, xs=xs: V_.tensor_tensor(
                    out=x1[:nq, nb * 512:(nb + 1) * 512], in0=t1[:nq, :], in1=xs[:nq, nb * 512:(nb + 1) * 512], op=ALU.add), [Bt1, Bxs], [Bx1])
            return (x1, Bx1, nq)

        def p4B(t, st_):
            x1, Bx1, nq = st_
            K.dma(pool, x1_d[128 * t:128 * t + nq, :], x1[:nq], Bx1, B_x1d[t], Bx1)
            h2t, Bh2t = h2R.next()
            hb = h2B[t % 2]
            norm_T(x1, Bx1, nq, a2, s2, lambda c, h2t=h2t, nq=nq: h2t[:, c, :nq], hb, xhR, tpR)
            K.wait_all(pool, hb)
            K.dma(pool, h2_dv[:, :, 128 * t:128 * t + nq], h2t[:, :, :nq], hb[0], B_h2d[t], hb[0])
            hb[1].r[hb[0].dsem] = hb[0].dsem.cnt
        st4 = p4A(0)
        for t in range(NT):
            nx4 = p4A(t + 1) if t + 1 < NT else None
            p4B(t, st4)
            st4 = nx4
        allb = [B_wo, B_g1, B_rs] + xsR.b + x1R.b + t1R.b + xhR.b + tpR.b + [b_.hi for b_ in tpR.b] + pacR.b + B_x1d + B_AO + B_h2d + aoR.b + zTR.b + h2B[0] + h2B[1]
        for e in K.engs:
            K.wait_all(e, allb)

    if lvl == 4:
        for s_ in K.sems:
            if s_.cnt > 0:
                nc.sync.wait_ge(s_.h, s_.cnt)
        return nc, es
    with ExitStack() as ph:
        actT = sb(ph, "actT", [128, NFB, 512], BF16)
        B_actT = [Buf(f"actT{j}") for j in range(NFB)]
        h2b = sb(ph, "h2b", [128, KC, 514], BF16); B_h2b = Buf("h2b")
        wuR = Ring(ph, "wu", [128, KC, 2, 256], BF16, 2)
        wdR = Ring(ph, "wd", [128, 1024], BF16, 4)
        fnw = sb(ph, "fnw", [128, D]); B_fnw = Buf("fnw")
        g2_bc = sb(ph, "g2_bc", [128, D]); B_g2 = Buf("g2")
        xoR = Ring(ph, "xo", [128, D], F32, 4)
        acR = Ring(ph, "ac", [128, 256], F32, 3)
        gcR = Ring(ph, "gc", [128, 256], F32, 3)
        sgR = Ring(ph, "sg", [128, 256], F32, 3)
        bank = [psum(ph, f"bk{i}", [128, 512], F32) for i in range(8)]
        B_bank = [Buf(f"bk{i}", excl=True) for i in range(8)]
        ld(fnw[:], fnw_d[:, :], B_fnw)
        K.dma(sp, g2_bc[:], modrow_d[0:1, 10240:12288].to_broadcast([128, D]), B_modd, B_g2, B_g2)
        B_out = Buf("outd")
        unit = 0
        for b in range(4):
            ub = 1 + 512 * b
            K.wait_all(sp, B_h2d)
            K.dma(sp, h2b[:], h2_dv[:, :, ub:ub + 514], B_h2d[0], B_h2b, B_h2b)
            for jb in range(22):
                wu, Bwu = wuR.next()
                K.dma(sp, wu[:].rearrange("p c a n -> p (c a n)"), wup_b[128 * jb:128 * (jb + 1), :], B_wupb, Bwu, Bwu)
                for jj in range(2):
                    jf = 2 * jb + jj
                    bk = [bank[4 * (unit % 2) + i] for i in range(4)]
                    Bbk = [B_bank[4 * (unit % 2) + i] for i in range(4)]
                    unit += 1

                    def mm_u(wu=wu, jj=jj, bk=bk):
                        ins = None
                        for ag in range(2):
                            for s in range(2):
                                for c in range(KC):
                                    ins = T_.matmul(bk[2 * ag + s][:, 0:258], wu[:, c, ag, jj * 128:(jj + 1) * 128],
                                                    h2b[:, c, 256 * s: 256 * s + 258], start=(c == 0), stop=(c == KC - 1))
                        return ins
                    K.op(pe, mm_u, [B_h2b, Bwu], Bbk)
                    for s in range(2):
                        ac, Bac = acR.next(); gc, Bgc = gcR.next(); sg, Bsg = sgR.next()
                        pa, pg = bk[s], bk[2 + s]
                        mask_col = None
                        if b == 0 and s == 0:
                            mask_col = (0, 0)
                        if b == 3 and s == 1:
                            mask_col = (257, 1)
                        if mask_col is not None:
                            def mk(pa=pa, pg=pg, mc=mask_col):
                                V_.tensor_scalar(pa[:, mc[0]:mc[0] + 1], pa[:, mc[0]:mc[0] + 1], hmask[:, mc[1]:mc[1] + 1], None, op0=ALU.mult)
                                return V_.tensor_scalar(pg[:, mc[0]:mc[0] + 1], pg[:, mc[0]:mc[0] + 1], hmask[:, mc[1]:mc[1] + 1], None, op0=ALU.mult)
                            K.op(dve, mk, [B_hmask], [Bbk[s], Bbk[2 + s]])

                        def a_first(ac=ac, gc=gc, pa=pa, pg=pg, jf=jf):
                            S_.activation(out=ac[:], in_=pa[:, 0:256], func=AF.Identity, scale=fconvT[:, jf, 0:1])
                            return S_.activation(out=gc[:], in_=pg[:, 0:256], func=AF.Identity, scale=fconvT[:, NFB + jf, 0:1])
                        K.op(act, a_first, [Bbk[s], Bbk[2 + s], B_cw], [Bac, Bgc])

                        for kk in (1, 2):
                            def cv2(ac=ac, gc=gc, pa=pa, pg=pg, jf=jf, kk=kk):
                                V_.scalar_tensor_tensor(out=ac[:], in0=pa[:, kk:256 + kk], scalar=fconvT[:, jf, kk:kk + 1], in1=ac[:], op0=ALU.mult, op1=ALU.add)
                                return V_.scalar_tensor_tensor(out=gc[:], in0=pg[:, kk:256 + kk], scalar=fconvT[:, NFB + jf, kk:kk + 1], in1=gc[:],
                                                               op0=ALU.mult, op1=ALU.add)
                            K.op(dve, cv2, [Bbk[s], Bbk[2 + s], B_cw], [Bac, Bgc])
                        K.op(act, lambda sg=sg, gc=gc: S_.activation(out=sg[:], in_=gc[:], func=AF.Silu), [Bgc], [Bsg])
                        K.op(dve, lambda sg=sg, ac=ac, jf=jf, s=s: V_.tensor_tensor(out=actT[:, jf, 256 * s:256 * (s + 1)], in0=sg[:], in1=ac[:], op=ALU.mult),
                             [Bsg, Bac], [B_actT[jf]])
            xos = [xoR.next() for _ in range(4)]
            for tt in range(4):
                row0 = 512 * b + 128 * tt
                xo, Bxo = xos[tt]
                K.dma(sp, xo[:], x1_d[2 + row0: 2 + row0 + 128, :], B_x1d[0], Bxo, Bxo)
            for half in range(2):
                for jf in range(NFB):
                    wd, Bwd = wdR.next()
                    K.dma(sp, wd[:], wdn_b[128 * jf:128 * (jf + 1), half * 1024:(half + 1) * 1024], B_wdnb, Bwd, Bwd)

                    def mm_d(wd=wd, jf=jf):
                        ins = None
                        for tt in range(4):
                            for nbh in range(2):
                                ins = T_.matmul(bank[2 * tt + nbh][:, :], actT[:, jf, 128 * tt:128 * (tt + 1)], wd[:, nbh * 512:(nbh + 1) * 512],
                                                start=(jf == 0), stop=(jf == NFB - 1))
                        return ins
                    K.op(pe, mm_d, [B_actT[jf], Bwd], B_bank)
                for tt in range(4):
                    row0 = 512 * b + 128 * tt
                    xo, Bxo = xos[tt]

                    def ev_d1(tt=tt, half=half):
                        ins = None
                        for nbh in range(2):
                            c0 = half * 1024 + nbh * 512
                            ins = V_.tensor_tensor(out=bank[2 * tt + nbh][:, :], in0=bank[2 * tt + nbh][:, :], in1=g2_bc[:, c0:c0 + 512], op=ALU.mult)
                        return ins

                    def ev_d2(xo=xo, tt=tt, half=half):
                        ins = None
                        for nbh in range(2):
                            c0 = half * 1024 + nbh * 512
                            ins = V_.tensor_tensor(out=xo[:, c0:c0 + 512], in0=xo[:, c0:c0 + 512], in1=bank[2 * tt + nbh][:, :], op=ALU.add)
                        return ins
                    K.op(dve, ev_d1, [B_g2], [B_bank[2 * tt], B_bank[2 * tt + 1]])
                    K.op(dve, ev_d2, [B_bank[2 * tt], B_bank[2 * tt + 1]], [Bxo])
                    if half == 1:
                        st, Bst = stR.next()
                        K.op(dve, lambda st=st: V_.memset(st[:], 0.0), [], [Bst])
                        K.op(act, lambda xo=xo, st=st: S_.activation(out=junk[:], in_=xo[:], func=AF.Square, accum_out=st[:, 0:1]), [Bxo], [B_junk, Bst])
                        K.op(act, lambda st=st: S_.activation(out=st[:, 1:2], in_=st[:, 0:1], func=AF.Sqrt, scale=1.0 / D, bias=eps_t[:, 0:1]), [Bst, B_eps], [Bst])

                        K.op(dve, lambda st=st: V_.reciprocal(st[:, 2:3], st[:, 1:2]), [Bst], [Bst])
                        K.op(dve, lambda xo=xo, st=st: V_.scalar_tensor_tensor(out=xo[:], in0=xo[:], scalar=st[:, 2:3], in1=fnw[:], op0=ALU.mult, op1=ALU.mult),
                             [Bst, B_fnw], [Bxo])
                        K.dma(pool, out_d[row0:row0 + 128, :], xo[:], Bxo, B_out, Bxo)
    for s in K.sems:
        if s.cnt > 0:
            nc.sync.wait_ge(s.h, s.cnt)
    return nc, es


def prep_inputs(inp):
    f = np.float32
    x = np.ascontiguousarray(np.asarray(inp["x"], f)[0])
    ctx = np.ascontiguousarray(np.asarray(inp["ctx"], f)[0])
    c = np.asarray(inp["c"], f)[0]
    c_ctx = np.asarray(inp["c_ctx"], f)
    w_ada = np.ascontiguousarray(np.asarray(inp["w_ada"], f)[0])
    b_ada = np.asarray(inp["b_ada"], f)[0]

    def fmaj(v, nchunk):
        return np.ascontiguousarray(v.reshape(nchunk, 128).T)
    cT = np.stack([fmaj(c, KC), fmaj(c_ctx, KC)], axis=-1).reshape(128, KC * 2)
    qw = np.asarray(inp["q_norm_w"], f)[0]
    kw = np.asarray(inp["k_norm_w"], f)[0]
    conv_w = np.asarray(inp["conv_w"], f)[0]
    fconv = np.asarray(inp["ffn_conv_w"], f)[0]
    convT = np.ascontiguousarray(conv_w.reshape(3, 8, 128).transpose(2, 1, 0)).reshape(128, 24)
    fconvT = np.ascontiguousarray(fconv.reshape(3, 88, 128).transpose(2, 1, 0)).reshape(128, 264)
    p = np.arange(128)
    posr_all = (2 * np.arange(128)[None, :] + (p[:, None] // 64)).astype(f)
    posc_all = (p[:, None] % 64).astype(f)
    common = {
        "x_all": x, "ctx_a": ctx,
        "cT": np.ascontiguousarray(cT),
        "w_ada": w_ada,
        "b_ada2": np.ascontiguousarray(np.broadcast_to(b_ada, (2, 12288))),
        "n1T": fmaj(np.asarray(inp["norm1_w"], f)[0], KC),
        "n2T": fmaj(np.asarray(inp["norm2_w"], f)[0], KC),
        "w_in": np.ascontiguousarray(np.asarray(inp["w_in"], f)[0]),
        "w_o": np.ascontiguousarray(np.asarray(inp["w_o"], f)[0]),
        "w_up": np.ascontiguousarray(np.asarray(inp["w_ffn_up"], f)[0]),
        "w_dn": np.ascontiguousarray(np.asarray(inp["w_ffn_down"], f)[0]),
        "qw_bc": np.ascontiguousarray(np.broadcast_to(qw, (128, 128))),
        "kw_bc": np.ascontiguousarray(np.broadcast_to(kw, (128, 128))),
        "convT": convT, "fconvT": fconvT,
        "aon_bc": np.ascontiguousarray(np.broadcast_to(np.asarray(inp["attn_out_norm_w"], f)[0], (128, 1024))),
        "conT": fmaj(np.asarray(inp["conv_out_norm_w"], f)[0], 8),
        "fnw_bc": np.ascontiguousarray(np.broadcast_to(np.asarray(inp["final_norm_w"], f), (128, D))),
        "jfreq": np.ascontiguousarray(np.broadcast_to(
            (np.float32(10000.0) ** (-(np.arange(32, dtype=f) / np.float32(32.0)))).astype(f), (128, 32))),
        "ident": np.eye(128, dtype=f),
        "posr_all": np.ascontiguousarray(posr_all), "posc_all": np.ascontiguousarray(posc_all),
    }
    maps = []
    for r in range(NCORES):
        t0 = r * TOWN
        xe = np.zeros((E, D), f)
        lo, hi = t0 - 2, t0 + TOWN + 2
        slo, shi = max(lo, 0), min(hi, SEQ)
        xe[slo - lo: shi - lo] = x[slo:shi]
        tok = np.arange(lo, hi)
        tokc = np.clip(tok, 0, SEQ - 1)
        prow = np.zeros(NT * 128, f); pcol = np.zeros(NT * 128, f)
        prow[:E] = tokc // 64; pcol[:E] = tokc % 64
        m = dict(common)
        m.update({
            "x_ext": xe,
            "posr": np.ascontiguousarray(prow.reshape(NT, 128).T),
            "posc": np.ascontiguousarray(pcol.reshape(NT, 128).T),
            "hmask": np.ascontiguousarray(np.broadcast_to(
                np.array([0.0 if r == 0 else 1.0, 0.0 if r == NCORES - 1 else 1.0], f), (128, 2))),
        })
        maps.append(m)
    return maps


def kernel_debug(stop, **inputs):
    nc, es = build(stop)
    maps = prep_inputs(inputs)
    names = set()
    for alloc in nc.allocations:
        try:
            if alloc.kind == "ExternalInput":
                names.add(alloc.memorylocations[0].name)
        except Exception:
            pass
    maps = [{k: v for k, v in m.items() if k in names} for m in maps]
    res = run_bass_kernel_spmd(nc, maps, core_ids=list(range(NCORES)))
    es.close()
    return res.results


def kernel(**inputs):
    nc, es = build(None)
    maps = prep_inputs(inputs)
    res = run_bass_kernel_spmd(nc, maps, core_ids=list(range(NCORES)))
    es.close()
    out = np.concatenate([r["out"] for r in res.results], axis=0)
    return out.reshape(1, SEQ, D).astype(np.float32)
```

```python
import os
import numpy as np
from contextlib import ExitStack
import concourse.bass as bass
import concourse.mybir as mybir
from concourse.bass_utils import run_bass_kernel_spmd

F32 = mybir.dt.float32
BF16 = mybir.dt.bfloat16
AF = mybir.ActivationFunctionType
ALU = mybir.AluOpType
AX = mybir.AxisListType

NCORES = 8
D = 2048
KC = 16
SEQ = 16384
TOWN = 2048
E = 2052
NT = 17
CTXC = 32
KOWN = TOWN + CTXC
NKEYS = KOWN * NCORES
NKC = NKEYS // 128
DFF = 5632
NFB = DFF // 128
EPS = 1e-6
TWO_PI = 6.283185307179586
PI = 3.141592653589793


def tile_rows(t):
    return 128 if t < 16 else 4


class Sem:
    def __init__(self, h, name):
        self.h = h
        self.name = name
        self.cnt = 0


class Buf:
    def __init__(self, name, excl=False):
        self.name = name
        self.w = {}
        self.r = {}
        self.dsem = None
        self.excl = excl


class Eng:
    def __init__(self, e, sem, name, selfwait=True):
        self.e = e
        self.sem = sem
        self.name = name
        self.waited = {}
        self.selfwait = selfwait


class Ctx:
    def __init__(self, nc, es):
        self.nc = nc
        self.es = es
        self.nsem = 0
        self.pe = Eng(nc.tensor, self.new_sem("pe"), "pe", selfwait=False)
        self.act = Eng(nc.scalar, self.new_sem("act"), "act")
        self.dve = Eng(nc.vector, self.new_sem("dve"), "dve")
        self.pool = Eng(nc.gpsimd, self.new_sem("pool"), "pool")
        self.sp = Eng(nc.sync, None, "sp")
        self.engs = [self.pe, self.act, self.dve, self.pool, self.sp]

    def new_sem(self, name):
        self.nsem += 1
        h = self.es.enter_context(self.nc.semaphore(f"s{self.nsem}_{name}"))
        s = Sem(h, name)
        if not hasattr(self, "sems"):
            self.sems = []
        self.sems.append(s)
        return s

    def _wait(self, eng, need):
        for sem, val in need.items():
            if sem is eng.sem and not eng.selfwait:
                continue
            if eng.waited.get(sem, 0) < val:
                eng.e.wait_ge(sem.h, val)
                eng.waited[sem] = val

    @staticmethod
    def _merge(dst, src):
        for s, v in src.items():
            if dst.get(s, 0) < v:
                dst[s] = v

    def op(self, eng, fn, reads=(), writes=()):
        need = {}
        for b in reads:
            self._merge(need, b.w)
            if b.excl:
                for s_, v_ in b.r.items():
                    if s_ is not eng.sem and need.get(s_, 0) < v_:
                        need[s_] = v_
        for b in writes:
            self._merge(need, b.w)
            self._merge(need, b.r)
        self._wait(eng, need)
        ins = fn()
        eng.sem.cnt += 1
        ins.then_inc(eng.sem.h, 1)
        v = eng.sem.cnt
        for b in reads:
            if b.r.get(eng.sem, 0) < v:
                b.r[eng.sem] = v
        for b in writes:
            b.w = {eng.sem: v}
            b.r = {}
        return ins

    def dma(self, q, out, in_, src, dst, owner, n=1, fn=None):
        need = {}
        self._merge(need, src.w)
        self._merge(need, dst.w)
        self._merge(need, dst.r)
        self._wait(q, need)
        if owner.dsem is None:
            owner.dsem = self.new_sem("d_" + owner.name)
        sem = owner.dsem
        if fn is None:
            q.e.dma_start(out=out, in_=in_).then_inc(sem.h, 16)
            sem.cnt += 16
        else:
            for ins in fn():
                ins.then_inc(sem.h, 16)
                sem.cnt += 16
        v = sem.cnt
        if src.r.get(sem, 0) < v:
            src.r[sem] = v
        dst.w = {sem: v}
        dst.r = {}

    def wait_all(self, eng, bufs):
        need = {}
        for b in bufs:
            self._merge(need, b.w)
            self._merge(need, b.r)
        self._wait(eng, need)


def build(stop=None):
    lvl = 5 if stop is None else stop
    nc = bass.Bass("TRN2", target_bir_lowering=False)
    es = ExitStack()
    K = Ctx(nc, es)
    pe, act, dve, pool, sp = K.pe, K.act, K.dve, K.pool, K.sp
    V_, S_, T_, G_ = nc.vector, nc.scalar, nc.tensor, nc.gpsimd

    def din(name, shape, dt=F32, need=0):
        if lvl < need:
            return None
        return nc.dram_tensor(name, shape, dt, kind="ExternalInput").ap()

    def dint(name, shape, dt, dbg_lvl=None):
        kind = "ExternalOutput" if (stop is not None and dbg_lvl is not None and stop + 0.5 >= dbg_lvl) else "Internal"
        return nc.dram_tensor(name, shape, dt, kind=kind).ap()

    x_all = din("x_all", [SEQ, D], need=2)
    x_ext = din("x_ext", [E, D], need=0.5)
    ctx_a = din("ctx_a", [256, D], need=2)
    cT_d = din("cT", [128, KC * 2])
    wada = din("w_ada", [D, 12288])
    bada = din("b_ada2", [2, 12288])
    n1T_d = din("n1T", [128, KC])
    n2T_d = din("n2T", [128, KC])
    w_in = din("w_in", [D, 4608], need=0.5)
    w_o = din("w_o", [D, D], need=4)
    w_up = din("w_up", [D, 2 * DFF], need=5)
    w_dn = din("w_dn", [DFF, D], need=5)
    qw_d = din("qw_bc", [128, 128])
    kw_d = din("kw_bc", [128, 128])
    convT_d = din("convT", [128, 24])
    fconvT_d = din("fconvT", [128, 264])
    aon_d = din("aon_bc", [128, 1024])
    conT_d = din("conT", [128, 8])
    fnw_d = din("fnw_bc", [128, D], need=5)
    posr_d = din("posr", [128, NT])
    posc_d = din("posc", [128, NT])
    posra_d = din("posr_all", [128, 128])
    posca_d = din("posc_all", [128, 1])
    jf_d = din("jfreq", [128, 32])
    hmask_d = din("hmask", [128, 2])
    ident_d = din("ident", [128, 128])
    out_d = nc.dram_tensor("out", [TOWN, D], F32, kind="ExternalOutput").ap() if stop is None else None

    tab_d = dint("tab_d", [128, 128 * 96], F32, 0)
    q_d = dint("q_d", [128, 8 * E], BF16, 1)
    zT_d = dint("zT_d", [128, 8 * E], BF16, 1)
    x1_d = dint("x1_d", [E, D], F32, 4)
    ao_d = dint("ao_d", [128, 8 * E], BF16, 3)
    h2_d = dint("h2_d", [128, KC * E], BF16, 4)
    modrow_d = dint("modrow_d", [2, 12288], F32, 0)
    wup_b = dint("wup_b", [22 * 128, 16 * 512], BF16)
    wdn_b = dint("wdn_b", [DFF, D], BF16)
    q_dv = q_d.rearrange("p (h e) -> p h e", h=8)
    zT_dv = zT_d.rearrange("p (h e) -> p h e", h=8)
    ao_dv = ao_d.rearrange("p (h e) -> p h e", h=8)
    h2_dv = h2_d.rearrange("p (c e) -> p c e", c=KC)

    uid = [0]

    def sb(st, name, shape, dt=F32):
        uid[0] += 1
        return st.enter_context(nc.sbuf_tensor(f"sb{uid[0]}_{name}", shape, dt))

    def psum(st, name, shape, dt=F32):
        uid[0] += 1
        return st.enter_context(nc.psum_tensor(f"pp{uid[0]}_{name}", shape, dt))

    class Ring:
        def __init__(self, st, name, shape, dt, n, ps=False):
            mk = psum if ps else sb
            self.t = [mk(st, f"{name}{i}", shape, dt) for i in range(n)]
            self.b = [Buf(f"{name}{i}", excl=ps) for i in range(n)]
            if ps:
                for b_ in self.b:
                    b_.hi = Buf(b_.name + "hi", excl=True)
            self.i = 0

        def next(self):
            k = self.i % len(self.t)
            self.i += 1
            return self.t[k], self.b[k]

    XB = Buf("ext")
    DBG = os.environ.get("KDUMP") is not None and stop is not None

    def ddump(name, ap, buf, shape, dt=F32):
        if not DBG:
            return
        t_ = nc.dram_tensor("dd_" + name, shape, dt, kind="ExternalOutput").ap()
        K.dma(sp, t_, ap, buf, Buf("dd" + name), buf)

    ident_f = sb(es, "ident_f", [128, 128]); B_identf = Buf("identf")
    ident_b = sb(es, "ident_b", [128, 128], BF16); B_identb = Buf("identb")
    n1T = sb(es, "n1T", [128, KC]); n2T = sb(es, "n2T", [128, KC]); B_nT = Buf("nT")
    a1 = sb(es, "a1", [128, KC]); s1 = sb(es, "s1", [128, KC])
    a1c = sb(es, "a1c", [128, KC]); s1c = sb(es, "s1c", [128, KC])
    a2 = sb(es, "a2", [128, KC]); s2 = sb(es, "s2", [128, KC]); B_mods = Buf("mods")
    eps_t = sb(es, "eps_t", [128, 1]); B_eps = Buf("eps")
    ssa = sb(es, "ssa", [128, NT]); B_ssa = Buf("ssa")
    ssc = sb(es, "ssc", [128, NT]); B_ssc = Buf("ssc")
    hmask = sb(es, "hmask", [128, 2]); B_hmask = Buf("hmask")
    convT = sb(es, "convT", [128, 8, 3]); fconvT = sb(es, "fconvT", [128, 88, 3]); conT = sb(es, "conT", [128, 8])
    B_cw = Buf("cw")
    ones_c = sb(es, "ones_c", [128, 1]); B_onesc = Buf("onesc")
    junk = sb(es, "junk", [128, D], BF16); B_junk = Buf("junk")
    stR = Ring(es, "st", [128, 16], F32, 6)

    def ld(dst, src, B):
        K.dma(sp, dst, src, XB, B, B)
    ld(ident_f[:], ident_d[:, :], B_identf)
    K.op(dve, lambda: V_.tensor_copy(ident_b[:], ident_f[:]), [B_identf], [B_identb])
    K.dma(sp, None, None, XB, B_nT, B_nT,
          fn=lambda: [nc.sync.dma_start(out=n1T[:], in_=n1T_d[:, :]), nc.sync.dma_start(out=n2T[:], in_=n2T_d[:, :])])
    ld(hmask[:], hmask_d[:, :], B_hmask)
    K.dma(sp, None, None, XB, B_cw, B_cw, fn=lambda: [
        nc.sync.dma_start(out=convT[:], in_=convT_d.rearrange("p (j k) -> p j k", k=3)),
        nc.sync.dma_start(out=fconvT[:], in_=fconvT_d.rearrange("p (j k) -> p j k", k=3)),
        nc.sync.dma_start(out=conT[:], in_=conT_d[:, :])])
    K.op(dve, lambda: V_.memset(ones_c[:], 1.0), [], [B_onesc])
    K.op(dve, lambda: V_.memset(eps_t[:], EPS), [], [B_eps])

    B_wupb = Buf("wupb"); B_wdnb = Buf("wdnb")
    wup_bv = wup_b.rearrange("(j p) (c a n) -> j p c a n", p=128, c=16, a=2)
    w_up_v = w_up.rearrange("(c p) (a j n) -> j p c a n", p=128, a=2, j=22) if w_up is not None else None

    def cvt_up():
        return [G_.dma_start(out=wup_bv[j][:, :, a, :], in_=w_up_v[j][:, :, a, :]) for j in range(22) for a in range(2)]

    def cvt_dn():
        return [G_.dma_start(out=wdn_b[i * 704:(i + 1) * 704, :], in_=w_dn[i * 704:(i + 1) * 704, :]) for i in range(8)]

    def norm_T(xs, Bx, nq, av, sv, dst, Bdst2, xhR, tpR, defer=False):
        st, Bst = stR.next()
        K.op(dve, lambda: V_.memset(st[:], 0.0), [], [Bst])
        K.op(act, lambda: S_.activation(out=junk[:nq], in_=xs[:nq], func=AF.Square, accum_out=st[:nq, 0:1]),
             [Bx], [B_junk, Bst])
        K.op(act, lambda: S_.activation(out=st[:nq, 1:2], in_=st[:nq, 0:1], func=AF.Sqrt, scale=1.0 / D, bias=eps_t[:nq, 0:1]),
             [Bst, B_eps], [Bst])
        K.op(dve, lambda: V_.reciprocal(st[:nq, 2:3], st[:nq, 1:2]), [Bst], [Bst])
        xh, Bxh = xhR.next()
        K.op(dve, lambda: V_.tensor_scalar(xh[:nq], xs[:nq], st[:nq, 2:3], None, op0=ALU.mult), [Bx, Bst], [Bxh])
        tp, Btp = tpR.next()

        def trs():
            ins = None
            for c in range(KC):
                ins = T_.transpose(tp[:, c * 128: c * 128 + nq], xh[:nq, c * 128:(c + 1) * 128], ident_b[:nq, :nq])
            return ins
        K.op(pe, trs, [Bxh, B_identb], [Btp, Btp.hi])

        def ev_act():
            ins = None
            for c in range(0, KC // 2):
                ins = S_.activation(out=dst(c), in_=tp[:, c * 128: c * 128 + nq], func=AF.Identity,
                                    scale=av[:, c:c + 1], bias=sv[:, c:c + 1])
            return ins

        def ev_dve():
            ins = None
            for c in range(KC // 2, KC):
                ins = V_.tensor_scalar(dst(c), tp[:, c * 128: c * 128 + nq], av[:, c:c + 1], sv[:, c:c + 1],
                                       op0=ALU.mult, op1=ALU.add)
            return ins
        def evac():
            K.op(act, ev_act, [Btp, B_mods], [Bdst2[0]])
            K.op(dve, ev_dve, [Btp.hi, B_mods], [Bdst2[1]])
        if defer:
            return evac
        evac()

    def rope_add(tmpA, tmpB, nq):
        return V_.tensor_tensor(out=tmpA[:nq], in0=tmpA[:nq], in1=tmpB[:nq], op=ALU.add)

    def rope(xq, tmpA, tmpB, nq, nh, cr, sr, nsr, cc_, sc_, nsc_):
        ins = None
        xv = xq[:nq].rearrange("p h (a f j) -> p h a f j", a=2, f=2)
        av = tmpA[:nq].rearrange("p h (a f j) -> p h a f j", a=2, f=2)
        bv = tmpB[:nq].rearrange("p h (a f j) -> p h a f j", a=2, f=2)
        for ax, (c_, s_, ns_) in enumerate(((cr, sr, nsr), (cc_, sc_, nsc_))):
            cb = c_.unsqueeze(1).unsqueeze(1).to_broadcast([nq, nh, 2, 32])
            V_.tensor_tensor(out=av[:, :, ax, :, :], in0=xv[:, :, ax, :, :], in1=cb, op=ALU.mult)
            sb_ = s_.unsqueeze(1).to_broadcast([nq, nh, 32])
            nsb = ns_.unsqueeze(1).to_broadcast([nq, nh, 32])
            V_.tensor_tensor(out=bv[:, :, ax, 0, :], in0=xv[:, :, ax, 1, :], in1=nsb, op=ALU.mult)
            ins = V_.tensor_tensor(out=bv[:, :, ax, 1, :], in0=xv[:, :, ax, 0, :], in1=sb_, op=ALU.mult)
        return ins

    def make_tables(st, pos, n, tab, Btab, Bpos, name):
        fr = sb(st, name + "fr", [128, 32]); jf = sb(st, name + "jf", [128, 32])
        ang = sb(st, name + "ang", [128, n, 32]); tmp = sb(st, name + "tmp", [128, n, 32]); ang2 = sb(st, name + "ang2", [128, n, 32])
        negpi = sb(st, name + "npi", [128, 1])
        Bl = Buf(name + "tl")
        K.dma(sp, jf[:], jf_d[:, :], XB, Bl, Bl)
        K.op(dve, lambda: V_.tensor_copy(fr[:], jf[:]), [Bl], [Bl])

        D1 = lambda f_: K.op(dve, f_, [Bl, Bpos], [Bl])
        D1(lambda: V_.memset(negpi[:], -PI))
        D1(lambda: V_.tensor_tensor(out=ang[:], in0=pos.unsqueeze(2).to_broadcast([128, n, 32]),
                                    in1=fr[:].unsqueeze(1).to_broadcast([128, n, 32]), op=ALU.mult))
        D1(lambda: V_.tensor_scalar(ang[:], ang[:], PI, None, op0=ALU.add))
        for m in (32, 16, 8, 4, 2, 1):
            cst = float(m * TWO_PI)
            D1(lambda cst=cst: V_.tensor_scalar(tmp[:], ang[:], cst, cst, op0=ALU.is_ge, op1=ALU.mult))
            D1(lambda: V_.tensor_tensor(out=ang[:], in0=ang[:], in1=tmp[:], op=ALU.subtract))
        D1(lambda: V_.tensor_scalar(ang2[:], ang[:], PI / 2, None, op0=ALU.add))
        D1(lambda: V_.tensor_scalar(tmp[:], ang2[:], TWO_PI, TWO_PI, op0=ALU.is_ge, op1=ALU.mult))
        D1(lambda: V_.tensor_tensor(out=ang2[:], in0=ang2[:], in1=tmp[:], op=ALU.subtract))

        def sins():
            S_.activation(out=tab[:, :, 32:64], in_=ang[:], func=AF.Sin, bias=negpi[:, 0:1])
            return S_.activation(out=tab[:, :, 0:32], in_=ang2[:], func=AF.Sin, bias=negpi[:, 0:1])
        K.op(act, sins, [Bl], [Btab])
        K.op(dve, lambda: V_.tensor_scalar(tab[:, :, 64:96], tab[:, :, 32:64], -1.0, None, op0=ALU.mult), [Btab], [Btab])

    tabca = sb(es, "tabca", [128, 1, 96]); B_tabca = Buf("tabca")
    tab_st = ExitStack()
    tabo = sb(tab_st, "tabo", [128, NT, 96]); B_tabo = Buf("tabo")
    tabc = sb(tab_st, "tabc", [128, NT, 96]); B_tabc = Buf("tabc")
    B_tabd = Buf("tabd"); B_modd = Buf("modd")
    with ExitStack() as ph:
        cT = sb(ph, "cT", [128, KC, 2]); B_cT = Buf("cT")
        sT = sb(ph, "sT", [128, KC, 2]); B_sT = Buf("sT")
        wadR = Ring(ph, "wad", [128, KC, 512], F32, 2)
        bsR = Ring(ph, "bsb", [2, 512], F32, 2)
        modrow = sb(ph, "modrow", [2, 12288]); B_modrow = Buf("modrow")
        fm = sb(ph, "fm", [128, 64, 2]); B_fm = Buf("fm")
        ones_r = sb(ph, "ones_r", [1, 128]); B_ones = Buf("ones")
        psmR = Ring(ph, "ps_mod", [2, 512], F32, 2, ps=True)
        ps_fm = psum(ph, "ps_fm", [128, 64, 2]); B_psfm = Buf("psfm", excl=True)

        ld(cT[:], cT_d.rearrange("p (c i) -> p c i", i=2), B_cT)
        K.op(act, lambda: S_.activation(out=sT[:], in_=cT[:], func=AF.Silu), [B_cT], [B_sT])
        K.op(dve, lambda: V_.memset(ones_r[:], 1.0), [], [B_ones])
        wr = wada.rearrange("(c p) n -> p c n", p=128)
        for i in range(24):
            wad, Bw = wadR.next()
            ld(wad[:], wr[:, :, i * 512:(i + 1) * 512], Bw)
            bsb, B_bsb = bsR.next()
            ld(bsb[:], bada[:, i * 512:(i + 1) * 512], B_bsb)
            psm, Bpsm = psmR.next()

            def mm_mod(wad=wad, psm=psm):
                ins = None
                for c in range(KC):
                    ins = T_.matmul(psm[0:2, :], sT[:, c, :], wad[:, c, :], start=(c == 0), stop=(c == KC - 1))
                return ins
            K.op(pe, mm_mod, [B_sT, Bw], [Bpsm])
            K.op(dve, lambda psm=psm, i=i, bsb=bsb: V_.tensor_tensor(out=modrow[0:2, i * 512:(i + 1) * 512], in0=psm[0:2, :],
                                                             in1=bsb[0:2, :], op=ALU.add),
                 [Bpsm, B_bsb], [B_modrow])
        bases = [0, 2048, 6144, 8192]

        def mm_fm():
            ins = None
            for q in range(4):
                for c in range(KC):
                    ins = T_.matmul(ps_fm[:, q * KC + c, :], modrow[0:2, bases[q] + c * 128: bases[q] + (c + 1) * 128],
                                    ident_f[0:2, 0:2], start=True, stop=True)
            return ins
        K.op(pe, mm_fm, [B_modrow, B_identf], [B_psfm])
        K.op(dve, lambda: V_.tensor_copy(fm[:], ps_fm[:]), [B_psfm], [B_fm])

        for (av, sv, nT, qsc, qsh, i) in ((a1, s1, n1T, 1, 0, 0), (a1c, s1c, n1T, 1, 0, 1), (a2, s2, n2T, 3, 2, 0)):
            K.op(dve, lambda av=av, qsc=qsc, i=i: V_.tensor_scalar(av[:], fm[:, qsc * KC:(qsc + 1) * KC, i], 1.0, None, op0=ALU.add),
                 [B_fm, B_nT], [B_mods])
            K.op(dve, lambda av=av, nT=nT: V_.tensor_tensor(out=av[:], in0=av[:], in1=nT[:], op=ALU.mult), [B_fm, B_nT], [B_mods])
            K.op(dve, lambda sv=sv, qsh=qsh, i=i: V_.tensor_copy(sv[:], fm[:, qsh * KC:(qsh + 1) * KC, i]), [B_fm, B_nT], [B_mods])
        K.dma(sp, modrow_d[:, :], modrow[:], B_modrow, B_modd, B_modrow)
        allb = [B_cT, B_sT, B_modrow, B_fm, B_ones, B_psfm, B_modd, B_mods] + wadR.b + psmR.b + bsR.b
        for e in K.engs:
            K.wait_all(e, allb)
    with ExitStack() as ph:
        posr = sb(ph, "posr", [128, NT]); posc = sb(ph, "posc", [128, NT])
        posra = sb(ph, "posra", [128, 128]); posca = sb(ph, "posca", [128, 1]); B_pos = Buf("pos")
        K.dma(sp, None, None, XB, B_pos, B_pos, fn=lambda: [
            nc.sync.dma_start(out=posr[:], in_=posr_d[:, :]), nc.sync.dma_start(out=posc[:], in_=posc_d[:, :]),
            nc.sync.dma_start(out=posra[:], in_=posra_d[:, :]), nc.sync.dma_start(out=posca[:], in_=posca_d[:, :])])
        make_tables(ph, posr[:], NT, tabo, B_tabo, B_pos, "to")
        make_tables(ph, posc[:], NT, tabc, B_tabc, B_pos, "tc")
        make_tables(ph, posca[:], 1, tabca, B_tabca, B_pos, "tca")
        taba = sb(ph, "taba", [128, 128, 96]); B_taba = Buf("taba")
        make_tables(ph, posra[:], 128, taba, B_taba, B_pos, "ta")
        K.dma(sp, tab_d[:, :], taba[:].rearrange("p n k -> p (n k)"), B_taba, B_tabd, B_taba)
        allb = [B_pos, B_taba, B_tabd, B_tabo, B_tabc, B_tabca]
        for e in K.engs:
            K.wait_all(e, allb)


    if lvl == 0:
        tab_st.close()
        for s_ in K.sems:
            if s_.cnt > 0:
                nc.sync.wait_ge(s_.h, s_.cnt)
        return nc, es
    with ExitStack() as ph:
        hT = sb(ph, "hT", [128, KC, E], BF16)
        B_hT = [[Buf(f"hTe{t}"), Buf(f"hTo{t}")] for t in range(NT)]
        zacc = sb(ph, "zacc", [128, E]); B_zacc = Buf("zacc")
        w_in_v = w_in.rearrange("(c p) n -> p c n", p=128)
        K.op(dve, lambda: V_.memset(zacc[:], 0.0), [], [B_zacc])
        B_qd = [Buf(f"qd{t}") for t in range(NT)]
        with ExitStack() as ph2:
            wq = sb(ph2, "wq", [128, KC, 1024], BF16); B_wq = Buf("wq")
            qw_bc = sb(ph2, "qw_bc", [128, 128]); B_qw = Buf("qw")
            xsR = Ring(ph2, "xs", [128, D], F32, 2)
            xhR = Ring(ph2, "xh", [128, D], BF16, 2)
            ld(qw_bc[:], qw_d[:, :], B_qw)
            K.dma(pool, None, None, XB, B_wq, B_wq, fn=lambda: [
                G_.dma_start(out=wq[:, :, i * 512:(i + 1) * 512], in_=w_in_v[:, :, i * 512:(i + 1) * 512]) for i in range(2)])
            tpR = Ring(ph2, "tp", [128, D], BF16, 1, ps=True)
            psqR = Ring(ph2, "psq", [128, 1024], F32, 2, ps=True)
            pstR = Ring(ph2, "pst", [128, 1024], BF16, 1, ps=True)
            sqR = Ring(ph2, "sq", [128, 1024], F32, 1)
            qnR = Ring(ph2, "qn", [128, 8, 128], F32, 1)
            qaR = Ring(ph2, "qa", [128, 8, 128], F32, 1)
            qbR = Ring(ph2, "qb", [128, 8, 128], F32, 1)
            qhR = Ring(ph2, "qh", [128, 1024], BF16, 1)
            qTR = Ring(ph2, "qT", [128, 8, 128], BF16, 2)
            def p1A(t):
                nq = tile_rows(t)
                xs, Bxs = xsR.next()
                ld(xs[:nq], x_ext[128 * t: 128 * t + nq, :], Bxs)
                norm_T(xs, Bxs, nq, a1, s1, lambda c, t=t, nq=nq: hT[:, c, 128 * t: 128 * t + nq], B_hT[t], xhR, tpR)
                psq, Bpsq = psqR.next()

                def mm_q(t=t, nq=nq, psq=psq):
                    ins = None
                    for nb in range(2):
                        for c in range(KC):
                            ins = T_.matmul(psq[:nq, nb * 512:(nb + 1) * 512], hT[:, c, 128 * t: 128 * t + nq],
                                            wq[:, c, nb * 512:(nb + 1) * 512], start=(c == 0), stop=(c == KC - 1))
                    return ins
                K.op(pe, mm_q, B_hT[t] + [B_wq], [Bpsq])
                return (psq, Bpsq, nq)

            def p1B(t, st_):
                psq, Bpsq, nq = st_
                sq, Bsq = sqR.next(); st, Bst = stR.next()
                K.op(act, lambda sq=sq, psq=psq, nq=nq: S_.activation(out=sq[:nq], in_=psq[:nq], func=AF.Square), [Bpsq], [Bsq])
                K.op(dve, lambda sq=sq, st=st, nq=nq: V_.reduce_sum(out=st[:nq, 0:8], in_=sq[:nq].rearrange("p (h d) -> p h d", h=8), axis=AX.X),
                     [Bsq], [Bst])
                K.op(act, lambda st=st, nq=nq: S_.activation(out=st[:nq, 8:16], in_=st[:nq, 0:8], func=AF.Sqrt, scale=1.0 / 128, bias=eps_t[:nq, 0:1]),
                     [Bst], [Bst])
                qn, Bqn = qnR.next(); qa, Bqa = qaR.next(); qb, Bqb = qbR.next()

                K.op(dve, lambda st=st, nq=nq: V_.reciprocal(st[:nq, 0:8], st[:nq, 8:16]), [Bst], [Bst])
                K.op(dve, lambda st=st, nq=nq, psq=psq, qn=qn: V_.tensor_tensor(
                    out=qn[:nq], in0=psq[:nq].rearrange("p (h d) -> p h d", h=8),
                    in1=st[:nq, 0:8].unsqueeze(2).to_broadcast([nq, 8, 128]), op=ALU.mult), [Bst, Bpsq], [Bqn])
                K.op(dve, lambda nq=nq, qn=qn: V_.tensor_tensor(out=qn[:nq], in0=qn[:nq], in1=qw_bc[:nq].unsqueeze(1).to_broadcast([nq, 8, 128]),
                                                               op=ALU.mult), [B_qw], [Bqn])
                K.op(dve, lambda nq=nq, qn=qn, qa=qa, qb=qb, t=t: rope(
                    qn, qa, qb, nq, 8, tabo[:nq, t, 0:32], tabo[:nq, t, 32:64], tabo[:nq, t, 64:96],
                    tabc[:nq, t, 0:32], tabc[:nq, t, 32:64], tabc[:nq, t, 64:96]), [Bqn, B_tabo, B_tabc], [Bqa, Bqb])
                K.op(dve, lambda nq=nq, qa=qa, qb=qb: rope_add(qa, qb, nq), [Bqb], [Bqa])
                qh, Bqh = qhR.next()
                K.op(act, lambda qh=qh, qa=qa, nq=nq: S_.copy(qh[:nq], qa[:nq].rearrange("p h d -> p (h d)")), [Bqa], [Bqh])
                pst, Bpst = pstR.next()

                def trq(pst=pst, qh=qh, nq=nq):
                    ins = None
                    for h in range(8):
                        ins = T_.transpose(pst[:, h * 128: h * 128 + nq], qh[:nq, h * 128:(h + 1) * 128], ident_b[:nq, :nq])
                    return ins
                K.op(pe, trq, [Bqh, B_identb], [Bpst])
                qT, BqT = qTR.next()
                K.op(dve, lambda qT=qT, pst=pst, nq=nq: V_.tensor_copy(qT[:, :, :nq], pst[:].rearrange("p (h d) -> p h d", h=8)[:, :, :nq]),
                     [Bpst], [BqT])
                K.dma(sp, q_dv[:, :, 128 * t: 128 * t + nq], qT[:, :, :nq], BqT, B_qd[t], BqT)
            st1 = p1A(0)
            for t in range(NT):
                nx1 = p1A(t + 1) if t + 1 < NT else None
                p1B(t, st1)
                st1 = nx1
            for e in K.engs:
                K.wait_all(e, tpR.b + [b_.hi for b_ in tpR.b] + psqR.b + pstR.b + [B_wq, B_qw] + xsR.b + xhR.b + sqR.b + qnR.b + qaR.b + qbR.b + qhR.b + qTR.b + B_qd)
        if lvl >= 1:
            with ExitStack() as ph2:
                pcR = Ring(ph2, "pc", [128, 3, 512], F32, 2, ps=True)
                uR = Ring(ph2, "u", [128, 512], F32, 2)
                ucR = Ring(ph2, "uc", [128, 512], F32, 2)
                yR = Ring(ph2, "y", [128, 512], F32, 2)
                zR = Ring(ph2, "z", [128, 512], F32, 2)
                zsR = Ring(ph2, "zs", [128, 512], F32, 2)
                zbR = Ring(ph2, "zb", [128, 512], BF16, 2)
                wcR = Ring(ph2, "wc", [128, KC, 3, 128], BF16, 2)
                zz = sb(ph2, "zz", [128, 8, 1], BF16); B_zz = Buf("zz")
                B_zTd = Buf("zTd")
                K.op(dve, lambda: V_.memset(zz[:], 0.0), [], [B_zz])
                K.dma(sp, None, None, B_zz, B_zTd, B_zz, fn=lambda: [
                    nc.sync.dma_start(out=zT_dv[:, :, 0:1], in_=zz[:], allow_slow_non_contiguous=True),
                    nc.sync.dma_start(out=zT_dv[:, :, 2051:2052], in_=zz[:], allow_slow_non_contiguous=True)])
                ps_ss = psum(ph2, "ps_ss", [128, NT]); B_psss = Buf("psss", excl=True)
                allhT = [b for pr in B_hT for b in pr]
                for j in range(8):
                    wc, Bwc = wcR.next()
                    K.dma(pool, None, None, XB, Bwc, Bwc, fn=lambda wc=wc, j=j: [
                        G_.dma_start(out=wc[:, :, k, :], in_=w_in_v[:, :, 1536 + 1024 * k + 128 * j: 1536 + 1024 * k + 128 * (j + 1)])
                        for k in range(3)])
                    for s in range(5):
                        c0 = 510 * s
                        N = 512 if s < 4 else 12
                        pc, Bpc = pcR.next()

                        def mm_c(pc=pc, wc=wc, c0=c0, N=N):
                            ins = None
                            for k in range(3):
                                for c in range(KC):
                                    ins = T_.matmul(pc[:, k, :N], wc[:, c, k, :], hT[:, c, c0:c0 + N], start=(c == 0), stop=(c == KC - 1))
                            return ins
                        K.op(pe, mm_c, allhT + [Bwc], [Bpc])
                        uc, Buc = ucR.next(); u, Bu = uR.next(); y, By = yR.next(); z, Bz = zR.next(); zs, Bzs = zsR.next()
                        zb, Bzb = zbR.next()
                        K.op(act, lambda uc=uc, pc=pc, N=N: S_.copy(uc[:, :N], pc[:, 1, :N]), [Bpc], [Buc])

                        M = N - 2
                        K.op(dve, lambda u=u, uc=uc, pc=pc, N=N: V_.tensor_tensor(out=u[:, :N], in0=uc[:, :N], in1=pc[:, 2, :N], op=ALU.mult),
                             [Buc, Bpc], [Bu])
                        if c0 <= 1 < c0 + N:
                            K.op(dve, lambda u=u, c0=c0: V_.tensor_scalar(u[:, 1 - c0:2 - c0], u[:, 1 - c0:2 - c0], hmask[:, 0:1], None, op0=ALU.mult),
                                 [B_hmask], [Bu])
                        if c0 <= 2050 < c0 + N:
                            K.op(dve, lambda u=u, c0=c0: V_.tensor_scalar(u[:, 2050 - c0:2051 - c0], u[:, 2050 - c0:2051 - c0], hmask[:, 1:2], None, op0=ALU.mult),
                                 [B_hmask], [Bu])
                        K.op(dve, lambda u=u, y=y, M=M, j=j: V_.tensor_scalar(y[:, :M], u[:, 0:M], convT[:, j, 0:1], None, op0=ALU.mult), [Bu, B_cw], [By])
                        K.op(dve, lambda u=u, y=y, M=M, j=j: V_.scalar_tensor_tensor(out=y[:, :M], in0=u[:, 1:M + 1], scalar=convT[:, j, 1:2], in1=y[:, :M],
                                                                                     op0=ALU.mult, op1=ALU.add), [Bu, B_cw], [By])
                        K.op(dve, lambda u=u, y=y, M=M, j=j: V_.scalar_tensor_tensor(out=y[:, :M], in0=u[:, 2:M + 2], scalar=convT[:, j, 2:3], in1=y[:, :M],
                                                                                     op0=ALU.mult, op1=ALU.add), [Bu, B_cw], [By])
                        K.op(dve, lambda y=y, z=z, pc=pc, M=M: V_.tensor_tensor(out=z[:, :M], in0=y[:, :M], in1=pc[:, 0, 1:M + 1], op=ALU.mult), [By, Bpc], [Bz])
                        K.op(dve, lambda z=z, zs=zs, M=M: V_.tensor_tensor(out=zs[:, :M], in0=z[:, :M], in1=z[:, :M], op=ALU.mult), [Bz], [Bzs])
                        K.op(dve, lambda zs=zs, M=M, c0=c0: V_.tensor_tensor(out=zacc[:, c0 + 1:c0 + 1 + M], in0=zacc[:, c0 + 1:c0 + 1 + M], in1=zs[:, :M], op=ALU.add),
                             [Bzs], [B_zacc])
                        K.op(dve, lambda z=z, zb=zb, M=M, j=j: V_.tensor_scalar(zb[:, :M], z[:, :M], conT[:, j:j + 1], None, op0=ALU.mult), [Bz, B_cw], [Bzb])
                        K.dma(sp, zT_dv[:, j, c0 + 1:c0 + N - 1], zb[:, :N - 2], Bzb, B_zTd, Bzb)

                def mm_ss():
                    ins = None
                    for t in range(NT):
                        nq = tile_rows(t)
                        ins = T_.matmul(ps_ss[:nq, t:t + 1], zacc[:, 128 * t:128 * t + nq], ones_c[:, 0:1], start=True, stop=True)
                    return ins
                K.op(dve, lambda: V_.memset(ssc[:], 1.0), [], [B_ssc])
                K.op(pe, mm_ss, [B_zacc, B_onesc], [B_psss])

                def cp_ss():
                    V_.tensor_copy(ssc[:, 0:16], ps_ss[:, 0:16])
                    return V_.tensor_copy(ssc[0:4, 16:17], ps_ss[0:4, 16:17])
                K.op(dve, cp_ss, [B_psss], [B_ssc])
                allb = pcR.b + uR.b + ucR.b + yR.b + zR.b + zsR.b + zbR.b + [B_psss, B_zacc, B_zTd, B_zz] + allhT + wcR.b
                for e in K.engs:
                    K.wait_all(e, allb)

    tab_st.close()
    if lvl <= 1:
        ssc_dbg = nc.dram_tensor("ssc_dbg", [128, NT], F32, kind="ExternalOutput").ap()
        K.dma(sp, ssc_dbg[:, :], ssc[:], B_ssc, Buf("sscd"), B_ssc)
        for s_ in K.sems:
            if s_.cnt > 0:
                nc.sync.wait_ge(s_.h, s_.cnt)
        return nc, es
    B_AO = [Buf(f"AO{t}") for t in range(NT)]
    with ExitStack() as ph:
        KT = sb(ph, "KT", [128, 2, NKEYS], BF16); B_KT = Buf("KT")
        VV = sb(ph, "VV", [128, NKC, 2, 130], BF16); B_VV = Buf("VV")
        K.op(dve, lambda: V_.memset(VV[:, :, :, 128:130], 1.0), [], [B_VV])
        with ExitStack() as ph2:
            wkv = sb(ph2, "wkv", [128, KC, 512], BF16); B_wkv = Buf("wkv")
            kw_bc = sb(ph2, "kw_bc", [128, 128]); B_kw = Buf("kw")
            xsR = Ring(ph2, "xs", [128, D], F32, 2)
            xhR = Ring(ph2, "xh", [128, D], BF16, 2)
            hTR = Ring(ph2, "hTt", [128, KC, 128], BF16, 2)
            hTB = [[Buf("hTa0"), Buf("hTb0")], [Buf("hTa1"), Buf("hTb1")]]
            tbR = Ring(ph2, "tb", [128, 96], F32, 3)
            tpR = Ring(ph2, "tp", [128, D], BF16, 2, ps=True)
            pkvR = Ring(ph2, "pkv", [128, 512], F32, 3, ps=True)
            pktR = Ring(ph2, "pkt", [128, 256], BF16, 1, ps=True)
            sqR = Ring(ph2, "sq", [128, 256], F32, 1)
            knR = Ring(ph2, "kn", [128, 2, 128], F32, 1)
            kaR = Ring(ph2, "ka", [128, 2, 128], F32, 1)
            kbR = Ring(ph2, "kb", [128, 2, 128], F32, 1)
            khR = Ring(ph2, "kh", [128, 256], BF16, 1)
            ld(kw_bc[:], kw_d[:, :], B_kw)
            w_in_v = w_in.rearrange("(c p) n -> p c n", p=128)
            K.dma(pool, wkv[:], w_in_v[:, :, 1024:1536], XB, B_wkv, B_wkv, fn=lambda: [G_.dma_start(out=wkv[:], in_=w_in_v[:, :, 1024:1536])])
            tab_dv = tab_d.rearrange("p (n k) -> p n k", k=96)
            def partA1(kc):
                lat = kc < 128
                xs, Bxs = xsR.next()
                if lat:
                    ld(xs[:], x_all[128 * kc:128 * (kc + 1), :], Bxs)
                    tb, Btb = tbR.next()
                    K.dma(sp, tb[:], tab_dv[:, kc, :], B_tabd, Btb, Btb)
                else:
                    ld(xs[:], ctx_a[128 * (kc - 128):128 * (kc - 127), :], Bxs)
                hTt, BhTt = hTR.next()
                hb = hTB[kc % 2]
                ev_ = norm_T(xs, Bxs, 128, a1 if lat else a1c, s1 if lat else s1c, lambda c, hTt=hTt: hTt[:, c, :], hb, xhR, tpR, defer=True)
                return (hTt, hb, (tb if lat else None), (Btb if lat else None), lat), ev_

            def partA2(kc, a1_):
                hTt, hb, tb, Btb, lat = a1_
                pkv, Bpkv = pkvR.next()

                def mm_kv(pkv=pkv, hTt=hTt):
                    ins = None
                    for c in range(KC):
                        ins = T_.matmul(pkv[:, :], hTt[:, c, :], wkv[:, c, :], start=(c == 0), stop=(c == KC - 1))
                    return ins
                K.op(pe, mm_kv, hb + [B_wkv], [Bpkv])
                return (pkv, Bpkv, tb, Btb, lat)

            def partB(kc, stt_):
                pkv, Bpkv, tb, Btb, lat = stt_
                sq, Bsq = sqR.next(); st, Bst = stR.next()

                K.op(dve, lambda st=st: V_.memset(st[:], 0.0), [], [Bst])

                def a_post(pkv=pkv, sq=sq, kc=kc, st=st):
                    S_.copy(VV[:, kc, :, 0:128], pkv[:, 256:512].rearrange("p (g d) -> p g d", g=2))
                    S_.activation(out=sq[:, 0:128], in_=pkv[:, 0:128], func=AF.Square, accum_out=st[:, 0:1])
                    return S_.activation(out=sq[:, 128:256], in_=pkv[:, 128:256], func=AF.Square, accum_out=st[:, 1:2])
                K.op(act, a_post, [Bpkv], [Bsq, B_VV, Bst])
                K.op(act, lambda st=st: S_.activation(out=st[:, 2:4], in_=st[:, 0:2], func=AF.Sqrt, scale=1.0 / 128, bias=eps_t[:, 0:1]), [Bst], [Bst])
                kn, Bkn = knR.next(); ka, Bka = kaR.next(); kb, Bkb = kbR.next(); kh, Bkh = khR.next()

                K.op(dve, lambda st=st: V_.reciprocal(st[:, 0:2], st[:, 2:4]), [Bst], [Bst])
                def knorm(st=st, pkv=pkv, dst=(kn if lat else ka)):
                    ins = None
                    for h in range(2):
                        ins = V_.scalar_tensor_tensor(out=dst[:, h, :], in0=pkv[:, h * 128:(h + 1) * 128], scalar=st[:, h:h + 1], in1=kw_bc[:],
                                                      op0=ALU.mult, op1=ALU.mult)
                    return ins
                K.op(dve, knorm, [Bst, Bpkv, B_kw], [Bkn if lat else Bka])
                if lat:
                    K.op(dve, lambda kn=kn, ka=ka, kb=kb, tb=tb: rope(kn, ka, kb, 128, 2, tb[:, 0:32], tb[:, 32:64], tb[:, 64:96],
                                                                       tabca[:, 0, 0:32], tabca[:, 0, 32:64], tabca[:, 0, 64:96]),
                         [Bkn, Btb, B_tabca], [Bka, Bkb])
                    K.op(dve, lambda ka=ka, kb=kb: rope_add(ka, kb, 128), [Bkb], [Bka])
                K.op(dve, lambda kh=kh, ka=ka: V_.tensor_copy(kh[:], ka[:].rearrange("p h d -> p (h d)")), [Bka], [Bkh])
                pkt, Bpkt = pktR.next()

                def trk(pkt=pkt, kh=kh):
                    T_.transpose(pkt[:, 0:128], kh[:, 0:128], ident_b[:, :])
                    return T_.transpose(pkt[:, 128:256], kh[:, 128:256], ident_b[:, :])
                K.op(pe, trk, [Bkh, B_identb], [Bpkt])
                K.op(act, lambda pkt=pkt, kc=kc: S_.copy(KT[:, :, 128 * kc:128 * (kc + 1)], pkt[:].rearrange("p (g d) -> p g d", g=2)),
                     [Bpkt], [B_KT])
            a1s = {}
            for k0 in (0, 1):
                a1s[k0], ev0 = partA1(k0)
                ev0()
            sts = {0: partA2(0, a1s.pop(0))}
            for kc in range(NKC):
                ev_n = None
                if kc + 2 < NKC:
                    a1s[kc + 2], ev_n = partA1(kc + 2)
                if kc + 1 < NKC:
                    sts[kc + 1] = partA2(kc + 1, a1s.pop(kc + 1))
                partB(kc, sts.pop(kc))
                if ev_n is not None:
                    ev_n()
            allb = [B_wkv, B_kw] + xsR.b + xhR.b + hTB[0] + hTB[1] + tbR.b + tpR.b + [b_.hi for b_ in tpR.b] + pkvR.b + pktR.b + sqR.b + knR.b + kaR.b + kbR.b + khR.b
            for e in K.engs:
                K.wait_all(e, allb)

        if lvl == 2:
            kt_dbg = nc.dram_tensor("kt_dbg", [128, 2 * NKEYS], BF16, kind="ExternalOutput").ap()
            vv_dbg = nc.dram_tensor("vv_dbg", [128, NKC * 260], BF16, kind="ExternalOutput").ap()
            K.dma(sp, kt_dbg[:, :], KT[:].rearrange("p g n -> p (g n)"), B_KT, Buf("ktd"), B_KT)
            K.dma(sp, vv_dbg[:, :], VV[:].rearrange("p c g n -> p (c g n)"), B_VV, Buf("vvd"), B_VV)
            for s_ in K.sems:
                if s_.cnt > 0:
                    nc.sync.wait_ge(s_.h, s_.cnt)
            return nc, es
        with ExitStack() as ph2:
            if lvl >= 5:
                K.dma(pool, None, None, XB, B_wupb, B_wupb, fn=cvt_up)
                K.dma(pool, None, None, XB, B_wdnb, B_wdnb, fn=cvt_dn)
            aon_bc = sb(ph2, "aon_bc", [128, 8, 128]); B_aon = Buf("aon")
            aoR = Ring(ph2, "aot", [128, 8, 128], BF16, 2)
            ld(aon_bc[:], aon_d.rearrange("p (h d) -> p h d", h=8), B_aon)
            qTR = Ring(ph2, "qTa", [128, 8, 128], BF16, 2)
            stpR = Ring(ph2, "stp", [128, 1024], F32, 2, ps=True)
            ptR = Ring(ph2, "pt", [128, 1024], BF16, 3)
            o_ps = [psum(ph2, f"ops{i}", [128, 512], F32) for i in range(3)]; B_ops = Buf("ops", excl=True)
            patR = Ring(ph2, "pat", [128, 1024], BF16, 1, ps=True)
            atR = Ring(ph2, "at", [128, 8, 128], F32, 1)
            asqR = Ring(ph2, "asq", [128, 1024], F32, 1)
            ahR = Ring(ph2, "ah", [128, 1024], BF16, 2)
            K.op(dve, lambda: V_.memset(ssa[:], 1.0), [], [B_ssa])
            SC = float(128 ** -0.5)
            pending = None

            def attn_post_pe(pp):
                (t, nq, ah, Bah) = pp
                pat, Bpat = patR.next()

                def tra():
                    ins = None
                    for h in range(8):
                        ins = T_.transpose(pat[:, h * 128:h * 128 + nq], ah[:nq, h * 128:(h + 1) * 128], ident_b[:nq, :nq])
                    return ins
                K.op(pe, tra, [Bah, B_identb], [Bpat])
                aot, Baot = aoR.next()
                K.op(dve, lambda: V_.tensor_copy(aot[:, :, :nq], pat[:].rearrange("p (h d) -> p h d", h=8)[:, :, :nq]),
                     [Bpat], [Baot])
                K.dma(sp, ao_dv[:, :, 128 * t:128 * t + nq], aot[:, :, :nq], Baot, B_AO[t], Baot)

            for t in range(NT):
                nq = tile_rows(t)
                qT, BqT = qTR.next()
                K.dma(sp, qT[:, :, :nq], q_dv[:, :, 128 * t:128 * t + nq], B_qd[t], BqT, BqT)
                def issue_qk(kc, qT=qT, BqT=BqT, nq=nq):
                    stp, Bstp = stpR.next()

                    def mm_qk():
                        ins = None
                        for g in range(2):
                            ins = T_.matmul(stp[:, g * 512:g * 512 + 4 * nq].rearrange("p (h q) -> p h q", h=4),
                                            KT[:, g, 128 * kc:128 * (kc + 1)], qT[:, 4 * g:4 * g + 4, :nq], start=True, stop=True)
                        return ins
                    K.op(pe, mm_qk, [BqT, B_KT], [Bstp])
                    pt, Bpt = ptR.next()
                    K.op(act, lambda: S_.activation(
                        out=pt[:].rearrange("p (g x) -> p g x", g=2)[:, :, 0:4 * nq],
                        in_=stp[:].rearrange("p (g x) -> p g x", g=2)[:, :, 0:4 * nq], func=AF.Exp, scale=SC), [Bstp], [Bpt])
                    return pt, Bpt
                nxt = issue_qk(0)
                for kc in range(NKC):
                    pt, Bpt = nxt
                    if kc + 1 < NKC:
                        nxt = issue_qk(kc + 1)

                    def mm_pv(pt=pt, kc=kc, nq=nq):
                        ins = None
                        for h in range(8):
                            g, hh = divmod(h, 4)
                            ins = T_.matmul(o_ps[h // 3][:nq, (h % 3) * 129:(h % 3) * 129 + 129],
                                            pt[:, g * 512 + hh * nq: g * 512 + (hh + 1) * nq], VV[:, kc, g, 0:129],
                                            start=(kc == 0), stop=(kc == NKC - 1))
                        return ins
                    K.op(pe, mm_pv, [Bpt, B_VV], [B_ops])
                    if kc == 6 and pending is not None:
                        attn_post_pe(pending)
                        pending = None
                at, Bat = atR.next(); st, Bst = stR.next(); asq, Basq = asqR.next(); ah, Bah = ahR.next()

                def ap1(st=st, nq=nq):
                    ins = None
                    for h in range(8):
                        ins = V_.reciprocal(st[:nq, h:h + 1], o_ps[h // 3][:nq, (h % 3) * 129 + 128:(h % 3) * 129 + 129])
                    return ins

                def ap2(at=at, st=st, nq=nq):
                    ins = None
                    for h in range(8):
                        ins = V_.tensor_scalar(at[:nq, h, :], o_ps[h // 3][:nq, (h % 3) * 129:(h % 3) * 129 + 128], st[:nq, h:h + 1], None, op0=ALU.mult)
                    return ins
                K.op(dve, ap1, [B_ops], [Bst])
                K.op(dve, ap2, [B_ops, Bst], [Bat])
                K.op(dve, lambda at=at, asq=asq, nq=nq: V_.tensor_tensor(out=asq[:nq], in0=at[:nq].rearrange("p h d -> p (h d)"),
                                                                         in1=at[:nq].rearrange("p h d -> p (h d)"), op=ALU.mult), [Bat], [Basq])
                K.op(dve, lambda asq=asq, nq=nq, t=t: V_.reduce_sum(out=ssa[:nq, t:t + 1], in_=asq[:nq], axis=AX.X), [Basq], [B_ssa])
                K.op(dve, lambda at=at, ah=ah, nq=nq: V_.tensor_tensor(out=ah[:nq].rearrange("p (h d) -> p h d", h=8), in0=at[:nq], in1=aon_bc[:nq], op=ALU.mult),
                     [Bat, B_aon], [Bah])
                pending = (t, nq, ah, Bah)
            attn_post_pe(pending)
            allb = [B_KT, B_VV, B_aon, B_ops] + qTR.b + stpR.b + [b_.hi for b_ in tpR.b] + ptR.b + patR.b + atR.b + asqR.b + ahR.b + B_AO + aoR.b
            for e in K.engs:
                K.wait_all(e, allb)

    if lvl == 3:
        for s_ in K.sems:
            if s_.cnt > 0:
                nc.sync.wait_ge(s_.h, s_.cnt)
        return nc, es
    B_h2d = [Buf(f"h2d{t}") for t in range(NT)]
    B_x1d = [Buf(f"x1d{t}") for t in range(NT)]
    with ExitStack() as ph:
        wo = sb(ph, "wo", [128, KC, D], BF16); B_wo4 = [Buf(f"wo{i}") for i in range(4)]
        g1_bc = sb(ph, "g1_bc", [128, D]); B_g1 = Buf("g1")
        rsa = sb(ph, "rsa", [128, NT]); rsc = sb(ph, "rsc", [128, NT]); B_rs = Buf("rs")
        xsR = Ring(ph, "xs", [128, D], F32, 2)
        x1R = Ring(ph, "x1", [128, D], F32, 2)
        t1R = Ring(ph, "t1", [128, 512], F32, 2)
        xhR = Ring(ph, "xh", [128, D], BF16, 2)
        aoR = Ring(ph, "aot4", [128, 8, 128], BF16, 2)
        zTR = Ring(ph, "zT4", [128, 8, 128], BF16, 2)
        h2R = Ring(ph, "h2t", [128, KC, 128], BF16, 2)
        h2B = [[Buf("h2a0"), Buf("h2b0")], [Buf("h2a1"), Buf("h2b1")]]
        tpR = Ring(ph, "tp", [128, D], BF16, 2, ps=True)
        pacR = Ring(ph, "pac", [128, 2, 512], F32, 2, ps=True)
        w_o_v = w_o.rearrange("(c p) n -> p c n", p=128)
        for i in range(4):
            K.dma(pool, wo[:, :, i * 512:(i + 1) * 512], w_o_v[:, :, i * 512:(i + 1) * 512], XB, B_wo4[i], B_wo4[i])
        K.dma(sp, g1_bc[:], modrow_d[0:1, 4096:6144].to_broadcast([128, D]), B_modd, B_g1, B_g1)

        def rstd2():
            S_.activation(out=rsa[:], in_=ssa[:], func=AF.Sqrt, scale=1.0 / 1024, bias=eps_t[:, 0:1])
            return S_.activation(out=rsc[:], in_=ssc[:], func=AF.Sqrt, scale=1.0 / 1024, bias=eps_t[:, 0:1])
        K.op(act, rstd2, [B_ssa, B_ssc, B_eps], [B_rs])

        def rstd3():
            V_.reciprocal(rsa[:], rsa[:])
            return V_.reciprocal(rsc[:], rsc[:])
        K.op(dve, rstd3, [B_rs], [B_rs])
        def p4A(t):
            nq = tile_rows(t)
            xs, Bxs = xsR.next()
            ld(xs[:nq], x_ext[128 * t:128 * t + nq, :], Bxs)
            aot, Baot = aoR.next()
            K.dma(sp, aot[:, :, :nq], ao_dv[:, :, 128 * t:128 * t + nq], B_AO[t], Baot, Baot)
            zTt, BzTt = zTR.next()
            K.dma(sp, zTt[:, :, :nq], zT_dv[:, :, 128 * t:128 * t + nq], B_zTd, BzTt, BzTt)
            x1, Bx1 = x1R.next()
            for nb in range(4):
                pac, Bpac = pacR.next()

                def mm_o(pac=pac, nq=nq, nb=nb, aot=aot, zTt=zTt):
                    ins = None
                    for h in range(8):
                        ins = T_.matmul(pac[:nq, 0, :], aot[:, h, :nq], wo[:, h, nb * 512:(nb + 1) * 512],
                                        start=(h == 0), stop=(h == 7))
                    for j in range(8):
                        ins = T_.matmul(pac[:nq, 1, :], zTt[:, j, :nq], wo[:, 8 + j, nb * 512:(nb + 1) * 512],
                                        start=(j == 0), stop=(j == 7))
                    return ins
                K.op(pe, mm_o, [Baot, BzTt, B_wo4[nb]], [Bpac])
                t1, Bt1 = t1R.next()
                K.op(act, lambda t1=t1, pac=pac, nq=nq, t=t: S_.activation(out=t1[:nq, :], in_=pac[:nq, 0, :], func=AF.Identity,
                                                                             scale=rsa[:nq, t:t + 1]), [Bpac, B_rs], [Bt1])

                K.op(dve, lambda t1=t1, pac=pac, nq=nq, t=t: V_.scalar_tensor_tensor(
                    out=t1[:nq, :], in0=pac[:nq, 1, :], scalar=rsc[:nq, t:t + 1], in1=t1[:nq, :], op0=ALU.mult, op1=ALU.add), [Bpac, B_rs], [Bt1])
                K.op(dve, lambda t1=t1, nq=nq, nb=nb: V_.tensor_tensor(out=t1[:nq, :], in0=t1[:nq, :], in1=g1_bc[:nq, nb * 512:(nb + 1) * 512], op=ALU.mult),
                     [B_g1], [Bt1])
                K.op(dve, lambda t1=t1, nq=nq, nb=nb, x1=x1, xs=xs: V_.tensor_tensor(
                    out=x1[:nq, nb * 512:(nb + 1) * 512], in0=t1[:nq, :], in1=xs[:nq, nb * 512:(nb + 1) * 512], op=ALU.add), [Bt1, Bxs], [Bx1])
            return (x1, Bx1, nq)

        def p4B(t, st_):
            x1, Bx1, nq = st_
            K.dma(pool, x1_d[128 * t:128 * t + nq, :], x1[:nq], Bx1, B_x1d[t], Bx1)
            h2t, Bh2t = h2R.next()
            hb = h2B[t % 2]
            norm_T(x1, Bx1, nq, a2, s2, lambda c, h2t=h2t, nq=nq: h2t[:, c, :nq], hb, xhR, tpR)
            K.wait_all(pool, hb)
            K.dma(pool, h2_dv[:, :, 128 * t:128 * t + nq], h2t[:, :, :nq], hb[0], B_h2d[t], hb[0])
            hb[1].r[hb[0].dsem] = hb[0].dsem.cnt
        st4 = p4A(0)
        for t in range(NT):
            nx4 = p4A(t + 1) if t + 1 < NT else None
            p4B(t, st4)
            st4 = nx4
        allb = B_wo4 + [B_g1, B_rs] + xsR.b + x1R.b + t1R.b + xhR.b + tpR.b + [b_.hi for b_ in tpR.b] + pacR.b + B_x1d + B_AO + B_h2d + aoR.b + zTR.b + h2B[0] + h2B[1]
        for e in K.engs:
            K.wait_all(e, allb)

    if lvl == 4:
        for s_ in K.sems:
            if s_.cnt > 0:
                nc.sync.wait_ge(s_.h, s_.cnt)
        return nc, es
    with ExitStack() as ph:
        actT = sb(ph, "actT", [128, NFB, 512], BF16)
        B_actT = [Buf(f"actT{j}") for j in range(NFB)]
        h2b = sb(ph, "h2b", [128, KC, 514], BF16); B_h2b = Buf("h2b")
        wuR = Ring(ph, "wu", [128, KC, 2, 256], BF16, 2)
        wdR = Ring(ph, "wd", [128, 1024], BF16, 4)
        fnw = sb(ph, "fnw", [128, D]); B_fnw = Buf("fnw")
        g2_bc = sb(ph, "g2_bc", [128, D]); B_g2 = Buf("g2")
        xoR = Ring(ph, "xo", [128, D], F32, 4)
        acR = Ring(ph, "ac", [128, 256], F32, 3)
        gcR = Ring(ph, "gc", [128, 256], F32, 3)
        sgR = Ring(ph, "sg", [128, 256], F32, 3)
        bank = [psum(ph, f"bk{i}", [128, 512], F32) for i in range(8)]
        B_bank = [Buf(f"bk{i}", excl=True) for i in range(8)]
        ld(fnw[:], fnw_d[:, :], B_fnw)
        K.dma(sp, g2_bc[:], modrow_d[0:1, 10240:12288].to_broadcast([128, D]), B_modd, B_g2, B_g2)
        B_out = Buf("outd")
        unit = 0
        for b in range(4):
            ub = 1 + 512 * b
            K.wait_all(sp, B_h2d)
            K.dma(sp, h2b[:], h2_dv[:, :, ub:ub + 514], B_h2d[0], B_h2b, B_h2b)
            for jb in range(22):
                wu, Bwu = wuR.next()
                K.dma(sp, wu[:].rearrange("p c a n -> p (c a n)"), wup_b[128 * jb:128 * (jb + 1), :], B_wupb, Bwu, Bwu)
                for jj in range(2):
                    jf = 2 * jb + jj
                    bk = [bank[4 * (unit % 2) + i] for i in range(4)]
                    Bbk = [B_bank[4 * (unit % 2) + i] for i in range(4)]
                    unit += 1

                    def mm_u(wu=wu, jj=jj, bk=bk):
                        ins = None
                        for ag in range(2):
                            for s in range(2):
                                for c in range(KC):
                                    ins = T_.matmul(bk[2 * ag + s][:, 0:258], wu[:, c, ag, jj * 128:(jj + 1) * 128],
                                                    h2b[:, c, 256 * s: 256 * s + 258], start=(c == 0), stop=(c == KC - 1))
                        return ins
                    K.op(pe, mm_u, [B_h2b, Bwu], Bbk)
                    for s in range(2):
                        ac, Bac = acR.next(); gc, Bgc = gcR.next(); sg, Bsg = sgR.next()
                        pa, pg = bk[s], bk[2 + s]
                        mask_col = None
                        if b == 0 and s == 0:
                            mask_col = (0, 0)
                        if b == 3 and s == 1:
                            mask_col = (257, 1)
                        if mask_col is not None:
                            def mk(pa=pa, pg=pg, mc=mask_col):
                                V_.tensor_scalar(pa[:, mc[0]:mc[0] + 1], pa[:, mc[0]:mc[0] + 1], hmask[:, mc[1]:mc[1] + 1], None, op0=ALU.mult)
                                return V_.tensor_scalar(pg[:, mc[0]:mc[0] + 1], pg[:, mc[0]:mc[0] + 1], hmask[:, mc[1]:mc[1] + 1], None, op0=ALU.mult)
                            K.op(dve, mk, [B_hmask], [Bbk[s], Bbk[2 + s]])

                        def a_first(ac=ac, gc=gc, pa=pa, pg=pg, jf=jf):
                            S_.activation(out=ac[:], in_=pa[:, 0:256], func=AF.Identity, scale=fconvT[:, jf, 0:1])
                            return S_.activation(out=gc[:], in_=pg[:, 0:256], func=AF.Identity, scale=fconvT[:, NFB + jf, 0:1])
                        K.op(act, a_first, [Bbk[s], Bbk[2 + s], B_cw], [Bac, Bgc])

                        for kk in (1, 2):
                            def cv2(ac=ac, gc=gc, pa=pa, pg=pg, jf=jf, kk=kk):
                                V_.scalar_tensor_tensor(out=ac[:], in0=pa[:, kk:256 + kk], scalar=fconvT[:, jf, kk:kk + 1], in1=ac[:], op0=ALU.mult, op1=ALU.add)
                                return V_.scalar_tensor_tensor(out=gc[:], in0=pg[:, kk:256 + kk], scalar=fconvT[:, NFB + jf, kk:kk + 1], in1=gc[:],
                                                               op0=ALU.mult, op1=ALU.add)
                            K.op(dve, cv2, [Bbk[s], Bbk[2 + s], B_cw], [Bac, Bgc])
                        K.op(act, lambda sg=sg, gc=gc: S_.activation(out=sg[:], in_=gc[:], func=AF.Silu), [Bgc], [Bsg])
                        K.op(dve, lambda sg=sg, ac=ac, jf=jf, s=s: V_.tensor_tensor(out=actT[:, jf, 256 * s:256 * (s + 1)], in0=sg[:], in1=ac[:], op=ALU.mult),
                             [Bsg, Bac], [B_actT[jf]])
            xos = [xoR.next() for _ in range(4)]
            for tt in range(4):
                row0 = 512 * b + 128 * tt
                xo, Bxo = xos[tt]
                K.dma(sp, xo[:], x1_d[2 + row0: 2 + row0 + 128, :], B_x1d[0], Bxo, Bxo)
            for half in range(2):
                for jf in range(NFB):
                    wd, Bwd = wdR.next()
                    K.dma(sp, wd[:], wdn_b[128 * jf:128 * (jf + 1), half * 1024:(half + 1) * 1024], B_wdnb, Bwd, Bwd)

                    def mm_d(wd=wd, jf=jf):
                        ins = None
                        for tt in range(4):
                            for nbh in range(2):
                                ins = T_.matmul(bank[2 * tt + nbh][:, :], actT[:, jf, 128 * tt:128 * (tt + 1)], wd[:, nbh * 512:(nbh + 1) * 512],
                                                start=(jf == 0), stop=(jf == NFB - 1))
                        return ins
                    K.op(pe, mm_d, [B_actT[jf], Bwd], B_bank)
                for tt in range(4):
                    row0 = 512 * b + 128 * tt
                    xo, Bxo = xos[tt]

                    def ev_d1(tt=tt, half=half):
                        ins = None
                        for nbh in range(2):
                            c0 = half * 1024 + nbh * 512
                            ins = V_.tensor_tensor(out=bank[2 * tt + nbh][:, :], in0=bank[2 * tt + nbh][:, :], in1=g2_bc[:, c0:c0 + 512], op=ALU.mult)
                        return ins

                    def ev_d2(xo=xo, tt=tt, half=half):
                        ins = None
                        for nbh in range(2):
                            c0 = half * 1024 + nbh * 512
                            ins = V_.tensor_tensor(out=xo[:, c0:c0 + 512], in0=xo[:, c0:c0 + 512], in1=bank[2 * tt + nbh][:, :], op=ALU.add)
                        return ins
                    K.op(dve, ev_d1, [B_g2], [B_bank[2 * tt], B_bank[2 * tt + 1]])
                    K.op(dve, ev_d2, [B_bank[2 * tt], B_bank[2 * tt + 1]], [Bxo])
                    if half == 1:
                        st, Bst = stR.next()
                        K.op(dve, lambda st=st: V_.memset(st[:], 0.0), [], [Bst])
                        K.op(act, lambda xo=xo, st=st: S_.activation(out=junk[:], in_=xo[:], func=AF.Square, accum_out=st[:, 0:1]), [Bxo], [B_junk, Bst])
                        K.op(act, lambda st=st: S_.activation(out=st[:, 1:2], in_=st[:, 0:1], func=AF.Sqrt, scale=1.0 / D, bias=eps_t[:, 0:1]), [Bst, B_eps], [Bst])

                        K.op(dve, lambda st=st: V_.reciprocal(st[:, 2:3], st[:, 1:2]), [Bst], [Bst])
                        K.op(dve, lambda xo=xo, st=st: V_.scalar_tensor_tensor(out=xo[:], in0=xo[:], scalar=st[:, 2:3], in1=fnw[:], op0=ALU.mult, op1=ALU.mult),
                             [Bst, B_fnw], [Bxo])
                        K.dma(pool, out_d[row0:row0 + 128, :], xo[:], Bxo, B_out, Bxo)
    for s in K.sems:
        if s.cnt > 0:
            nc.sync.wait_ge(s.h, s.cnt)
    return nc, es


def prep_inputs(inp):
    f = np.float32
    x = np.ascontiguousarray(np.asarray(inp["x"], f)[0])
    ctx = np.ascontiguousarray(np.asarray(inp["ctx"], f)[0])
    c = np.asarray(inp["c"], f)[0]
    c_ctx = np.asarray(inp["c_ctx"], f)
    w_ada = np.ascontiguousarray(np.asarray(inp["w_ada"], f)[0])
    b_ada = np.asarray(inp["b_ada"], f)[0]

    def fmaj(v, nchunk):
        return np.ascontiguousarray(v.reshape(nchunk, 128).T)
    cT = np.stack([fmaj(c, KC), fmaj(c_ctx, KC)], axis=-1).reshape(128, KC * 2)
    qw = np.asarray(inp["q_norm_w"], f)[0]
    kw = np.asarray(inp["k_norm_w"], f)[0]
    conv_w = np.asarray(inp["conv_w"], f)[0]
    fconv = np.asarray(inp["ffn_conv_w"], f)[0]
    convT = np.ascontiguousarray(conv_w.reshape(3, 8, 128).transpose(2, 1, 0)).reshape(128, 24)
    fconvT = np.ascontiguousarray(fconv.reshape(3, 88, 128).transpose(2, 1, 0)).reshape(128, 264)
    p = np.arange(128)
    posr_all = (2 * np.arange(128)[None, :] + (p[:, None] // 64)).astype(f)
    posc_all = (p[:, None] % 64).astype(f)
    common = {
        "x_all": x, "ctx_a": ctx,
        "cT": np.ascontiguousarray(cT),
        "w_ada": w_ada,
        "b_ada2": np.ascontiguousarray(np.broadcast_to(b_ada, (2, 12288))),
        "n1T": fmaj(np.asarray(inp["norm1_w"], f)[0], KC),
        "n2T": fmaj(np.asarray(inp["norm2_w"], f)[0], KC),
        "w_in": np.ascontiguousarray(np.asarray(inp["w_in"], f)[0]),
        "w_o": np.ascontiguousarray(np.asarray(inp["w_o"], f)[0]),
        "w_up": np.ascontiguousarray(np.asarray(inp["w_ffn_up"], f)[0]),
        "w_dn": np.ascontiguousarray(np.asarray(inp["w_ffn_down"], f)[0]),
        "qw_bc": np.ascontiguousarray(np.broadcast_to(qw, (128, 128))),
        "kw_bc": np.ascontiguousarray(np.broadcast_to(kw, (128, 128))),
        "convT": convT, "fconvT": fconvT,
        "aon_bc": np.ascontiguousarray(np.broadcast_to(np.asarray(inp["attn_out_norm_w"], f)[0], (128, 1024))),
        "conT": fmaj(np.asarray(inp["conv_out_norm_w"], f)[0], 8),
        "fnw_bc": np.ascontiguousarray(np.broadcast_to(np.asarray(inp["final_norm_w"], f), (128, D))),
        "jfreq": np.ascontiguousarray(np.broadcast_to(
            (np.float32(10000.0) ** (-(np.arange(32, dtype=f) / np.float32(32.0)))).astype(f), (128, 32))),
        "ident": np.eye(128, dtype=f),
        "posr_all": np.ascontiguousarray(posr_all), "posc_all": np.ascontiguousarray(posc_all),
    }
    maps = []
    for r in range(NCORES):
        t0 = r * TOWN
        xe = np.zeros((E, D), f)
        lo, hi = t0 - 2, t0 + TOWN + 2
        slo, shi = max(lo, 0), min(hi, SEQ)
        xe[slo - lo: shi - lo] = x[slo:shi]
        tok = np.arange(lo, hi)
        tokc = np.clip(tok, 0, SEQ - 1)
        prow = np.zeros(NT * 128, f); pcol = np.zeros(NT * 128, f)
        prow[:E] = tokc // 64; pcol[:E] = tokc % 64
        m = dict(common)
        m.update({
            "x_ext": xe,
            "posr": np.ascontiguousarray(prow.reshape(NT, 128).T),
            "posc": np.ascontiguousarray(pcol.reshape(NT, 128).T),
            "hmask": np.ascontiguousarray(np.broadcast_to(
                np.array([0.0 if r == 0 else 1.0, 0.0 if r == NCORES - 1 else 1.0], f), (128, 2))),
        })
        maps.append(m)
    return maps


def kernel_debug(stop, **inputs):
    nc, es = build(stop)
    maps = prep_inputs(inputs)
    names = set()
    for alloc in nc.allocations:
        try:
            if alloc.kind == "ExternalInput":
                names.add(alloc.memorylocations[0].name)
        except Exception:
            pass
    maps = [{k: v for k, v in m.items() if k in names} for m in maps]
    res = run_bass_kernel_spmd(nc, maps, core_ids=list(range(NCORES)))
    es.close()
    return res.results


def kernel(**inputs):
    nc, es = build(None)
    maps = prep_inputs(inputs)
    res = run_bass_kernel_spmd(nc, maps, core_ids=list(range(NCORES)))
    es.close()
    out = np.concatenate([r["out"] for r in res.results], axis=0)
    return out.reshape(1, SEQ, D).astype(np.float32)
```
